# Optimizing a Trainium2 kernel written in Bass

```python
import math
import jax, jax.numpy as jnp
from jax import lax
import numpy as np

D_MODEL = 1024
BATCH = 8
SEQ = 4096
DEPTH = 1

GRID_W = 64
DILATED_GROUPS = ((128, 1), (512, 4), (2048, 16))
N_GROUPS_A = len(DILATED_GROUPS)
HEADS_PER_GROUP_A = 8
HEAD_DIM_A = 64
N_HEADS_A = N_GROUPS_A * HEADS_PER_GROUP_A
GROUP_WIDTH_A = HEADS_PER_GROUP_A * HEAD_DIM_A
A_QKV_WIDTH = 3 * N_HEADS_A * HEAD_DIM_A
BAND_BLK = 64
N_HEADS_B = 8
N_KV_B = 2
GQA_GROUP_B = N_HEADS_B // N_KV_B
HEAD_DIM_B = 128
B_Q_WIDTH = N_HEADS_B * HEAD_DIM_B
B_KV_WIDTH = N_KV_B * HEAD_DIM_B
ROPE_THETA = 10000.0
Q_BLOCK = 128
N_BRANCHES = 2
GATE_WIDTH = N_BRANCHES * D_MODEL
IN_WIDTH = A_QKV_WIDTH + B_Q_WIDTH + 2 * B_KV_WIDTH + GATE_WIDTH
N_BUCKETS = 32
MAX_DISTANCE = 1024
D_FF = 2816
EPS = 1e-6
NEG_INF = -1e30

kernel_name = 'hybrid_dilated_axial_gqa_macaron'


def rmsnorm(x, g):
    xf = x.astype(jnp.float32)
    y = xf * lax.rsqrt(jnp.mean(xf * xf, axis=-1, keepdims=True) + EPS)
    return (y * g.astype(jnp.float32)).astype(x.dtype)


def swiglu(x, w1, w3, w2):
    return (jax.nn.silu(x @ w1) * (x @ w3)) @ w2


def t5_bucket(rel):
    n = N_BUCKETS // 2
    max_exact = n // 2
    ret = jnp.where(rel > 0, n, 0)
    a = jnp.abs(rel)
    af = jnp.maximum(a, 1).astype(jnp.float32)
    large = max_exact + (jnp.log(af / max_exact) / math.log(MAX_DISTANCE / max_exact)
                         * (n - max_exact)).astype(jnp.int32)
    large = jnp.minimum(large, n - 1)
    return ret + jnp.where(a < max_exact, a, large)


def dilated_group_attention(q, k, v, bias_tab, dilation, half):
    B, S, H, hd = q.shape
    L = S // dilation
    nblk = -(-L // BAND_BLK)
    Lp = nblk * BAND_BLK

    def to_sub(a):
        a = a.reshape(B, L, dilation, H, hd).transpose(0, 2, 1, 3, 4)
        return jnp.pad(a, ((0, 0), (0, 0), (0, Lp - L), (0, 0), (0, 0)))

    def band(a):
        a = jnp.pad(a, ((0, 0), (0, 0), (BAND_BLK, BAND_BLK), (0, 0), (0, 0)))
        blocks = a.reshape(B, dilation, nblk + 2, BAND_BLK, H, hd)
        return jnp.concatenate([blocks[:, :, :-2], blocks[:, :, 1:-1], blocks[:, :, 2:]], axis=3)

    qs = to_sub(q).reshape(B, dilation, nblk, BAND_BLK, H, hd)
    kb = band(to_sub(k))
    vb = band(to_sub(v))

    scores = jnp.einsum('brnqhd,brnkhd->brnhqk', qs, kb,
                        preferred_element_type=jnp.float32) * (hd ** -0.5)
    qi = jnp.arange(BAND_BLK, dtype=jnp.int32)
    ki = jnp.arange(3 * BAND_BLK, dtype=jnp.int32) - BAND_BLK
    rel_steps = ki[None, :] - qi[:, None]
    key_m = jnp.arange(nblk, dtype=jnp.int32)[:, None] * BAND_BLK + ki[None, :]
    valid = ((jnp.abs(rel_steps) <= half)[None]
             & ((key_m >= 0) & (key_m < L))[:, None, :])
    bias = bias_tab[t5_bucket(rel_steps * dilation)].transpose(2, 0, 1)
    scores = scores + bias.astype(jnp.float32)
    scores = jnp.where(valid[None, None, :, None], scores, NEG_INF)
    lse = jax.nn.logsumexp(scores, axis=-1)
    p = jnp.exp(scores - lse[..., None])
    out = jnp.einsum('brnhqk,brnkhd->brnqhd', p.astype(v.dtype), vb)
    out = out.reshape(B, dilation, Lp, H, hd)[:, :, :L]
    out = out.transpose(0, 2, 1, 3, 4).reshape(B, S, H, hd)
    lse = lse.transpose(0, 1, 2, 4, 3).reshape(B, dilation, Lp, H)[:, :, :L]
    lse = lse.transpose(0, 2, 1, 3).reshape(B, S, H)
    return out, lse


def axial_rope_tables(rows):
    row = jnp.repeat(jnp.arange(rows, dtype=jnp.float32), GRID_W)
    col = jnp.tile(jnp.arange(GRID_W, dtype=jnp.float32), rows)
    n_freq = HEAD_DIM_B // 4
    freq = ROPE_THETA ** (-jnp.arange(n_freq, dtype=jnp.float32) / n_freq)
    ang = jnp.concatenate([row[:, None] * freq, col[:, None] * freq], axis=-1)
    return jnp.cos(ang), jnp.sin(ang)


def apply_rope(x, cos, sin):
    xf = x.astype(jnp.float32).reshape(*x.shape[:-1], x.shape[-1] // 2, 2)
    x0, x1 = xf[..., 0], xf[..., 1]
    c = cos[None, :, None, :]
    s = sin[None, :, None, :]
    out = jnp.stack([x0 * c - x1 * s, x0 * s + x1 * c], axis=-1)
    return out.reshape(x.shape).astype(x.dtype)


def gqa_axial_attention(q, k, v, q_norm, k_norm, cos, sin):
    B, S = q.shape[0], q.shape[1]
    q = apply_rope(rmsnorm(q, q_norm), cos, sin)
    k = apply_rope(rmsnorm(k, k_norm), cos, sin)
    scale = HEAD_DIM_B ** -0.5
    qblocks = q.reshape(B, S // Q_BLOCK, Q_BLOCK, N_KV_B, GQA_GROUP_B, HEAD_DIM_B).swapaxes(0, 1)

    def attn_block(qb):
        s = jnp.einsum('bqkgd,bskd->bkgqs', qb, k, preferred_element_type=jnp.float32) * scale
        p = jax.nn.softmax(s, axis=-1)
        return jnp.einsum('bkgqs,bskd->bqkgd', p.astype(v.dtype), v)

    ob = lax.map(attn_block, qblocks)
    return ob.swapaxes(0, 1).reshape(B, S, B_Q_WIDTH)


def hybrid_mixer(h, w_in, b_gate, q_norm, k_norm, rel_bias, w_branch_a, w_branch_b, w_out, cos, sin):
    B, S, D = h.shape
    proj = h @ w_in
    o1 = A_QKV_WIDTH
    o2 = o1 + B_Q_WIDTH
    o3 = o2 + B_KV_WIDTH
    o4 = o3 + B_KV_WIDTH
    pa, pq, pk, pv, pg = proj[..., :o1], proj[..., o1:o2], proj[..., o2:o3], proj[..., o3:o4], proj[..., o4:]

    a = pa.reshape(B, S, 3, N_GROUPS_A, HEADS_PER_GROUP_A, HEAD_DIM_A)
    bias_groups = rel_bias.reshape(N_BUCKETS, N_GROUPS_A, HEADS_PER_GROUP_A)
    outs, lses = [], []
    for g, (window, dil) in enumerate(DILATED_GROUPS):
        o, lse = dilated_group_attention(a[:, :, 0, g], a[:, :, 1, g], a[:, :, 2, g],
                                         bias_groups[:, g], dil, window // (2 * dil))
        outs.append(o)
        lses.append(lse)
    wgt = jax.nn.softmax(jnp.stack(lses, axis=0), axis=0)
    o_a = jnp.sum(wgt[..., None] * jnp.stack(outs, axis=0).astype(jnp.float32), axis=0)
    o_a = o_a.astype(h.dtype).reshape(B, S, GROUP_WIDTH_A)

    o_b = gqa_axial_attention(pq.reshape(B, S, N_HEADS_B, HEAD_DIM_B),
                              pk.reshape(B, S, N_KV_B, HEAD_DIM_B),
                              pv.reshape(B, S, N_KV_B, HEAD_DIM_B),
                              q_norm, k_norm, cos, sin)

    gates = jax.nn.sigmoid((pg + b_gate).reshape(B, S, N_BRANCHES, D))
    merged = gates[:, :, 0] * (o_a @ w_branch_a) + gates[:, :, 1] * (o_b @ w_branch_b)
    return merged @ w_out


def setup_inputs(seed: int = 0) -> dict:
    key = jax.random.key(seed)
    ks = jax.random.split(key, 24)
    f32 = jnp.float32

    def w(k, shape, fan_in):
        return jax.random.normal(k, shape, f32) * (fan_in ** -0.5)

    def gain(k, shape):
        return 1.0 + 0.05 * jax.random.normal(k, shape, f32)

    L, D = DEPTH, D_MODEL
    return {
        'x': jax.random.normal(ks[0], (BATCH, SEQ, D), f32),
        'ffn1_norm': gain(ks[1], (L, D)),
        'ffn1_w1': w(ks[2], (L, D, D_FF), D),
        'ffn1_w3': w(ks[3], (L, D, D_FF), D),
        'ffn1_w2': w(ks[4], (L, D_FF, D), D_FF),
        'mix_norm': gain(ks[5], (L, D)),
        'w_in': w(ks[6], (L, D, IN_WIDTH), D),
        'b_gate': 0.02 * jax.random.normal(ks[7], (L, GATE_WIDTH), f32),
        'q_norm': gain(ks[8], (L, HEAD_DIM_B)),
        'k_norm': gain(ks[9], (L, HEAD_DIM_B)),
        'rel_bias': 0.5 * jax.random.normal(ks[10], (N_BUCKETS, N_HEADS_A), f32),
        'w_branch_a': w(ks[11], (L, GROUP_WIDTH_A, D), GROUP_WIDTH_A),
        'w_branch_b': w(ks[12], (L, B_Q_WIDTH, D), B_Q_WIDTH),
        'w_out': w(ks[13], (L, D, D), D),
        'ffn2_norm': gain(ks[14], (L, D)),
        'ffn2_w1': w(ks[15], (L, D, D_FF), D),
        'ffn2_w3': w(ks[16], (L, D, D_FF), D),
        'ffn2_w2': w(ks[17], (L, D_FF, D), D_FF),
        'final_norm': gain(ks[18], (D,)),
    }


def reference(x, ffn1_norm, ffn1_w1, ffn1_w3, ffn1_w2, mix_norm, w_in, b_gate, q_norm, k_norm,
              rel_bias, w_branch_a, w_branch_b, w_out, ffn2_norm, ffn2_w1, ffn2_w3, ffn2_w2,
              final_norm):
    S = x.shape[1]
    rows = S // GRID_W
    cos, sin = axial_rope_tables(rows)
    for l in range(DEPTH):
        x = x + 0.5 * swiglu(rmsnorm(x, ffn1_norm[l]), ffn1_w1[l], ffn1_w3[l], ffn1_w2[l])
        h = rmsnorm(x, mix_norm[l])
        x = x + hybrid_mixer(h, w_in[l], b_gate[l], q_norm[l], k_norm[l], rel_bias,
                             w_branch_a[l], w_branch_b[l], w_out[l], cos, sin)
        x = x + 0.5 * swiglu(rmsnorm(x, ffn2_norm[l]), ffn2_w1[l], ffn2_w3[l], ffn2_w2[l])
    return rmsnorm(x, final_norm)
```

```python
import numpy as np
from contextlib import ExitStack
import concourse.bass as bass
import concourse.mybir as mybir
from concourse.bass_utils import run_bass_kernel_spmd

F32 = mybir.dt.float32
BF16 = mybir.dt.bfloat16
AF = mybir.ActivationFunctionType
ALU = mybir.AluOpType

S = 4096
D = 1024
DFF = 2816
NCH = DFF // 128
KC = D // 128
NT = S // 512
EPS = 1e-6
GROUPS = ((128, 1), (512, 4), (2048, 16))
WBLK = ((0, 6), (6, 12), (12, 17), (17, 22))


class EngQ:
    def __init__(self, kb, eng, name):
        self.eng = eng
        self.sem = kb.sem("q_" + name)
        self.n = 0
        self.waited = {}

    def wait(self, *toks):
        for t in toks:
            if t is None:
                continue
            if isinstance(t, (list, tuple)) and not (len(t) == 2 and isinstance(t[1], int)):
                self.wait(*t)
                continue
            sem, val = t
            key = id(sem)
            if self.waited.get(key, 0) >= val:
                continue
            self.eng.wait_ge(sem, val)
            self.waited[key] = val

    def mark(self, inst):
        self.n += 1
        inst.then_inc(self.sem, 1)
        return (self.sem, self.n)


class DSem:
    def __init__(self, sem):
        self.sem = sem
        self.val = 0

    def tok(self):
        return (self.sem, self.val)


class KB:
    def __init__(self):
        self.nc = bass.Bass("TRN2", target_bir_lowering=False)
        self.es = ExitStack()
        nc = self.nc
        self.pe = EngQ(self, nc.tensor, "pe")
        self.act = EngQ(self, nc.scalar, "act")
        self.dve = EngQ(self, nc.vector, "dve")
        self.pool = EngQ(self, nc.gpsimd, "pool")
        self.sp = EngQ(self, nc.sync, "sp")
        self.engs = [self.pe, self.act, self.dve, self.pool, self.sp]
        self._ds = {}

    def sem(self, name):
        return self.es.enter_context(self.nc.semaphore(name))

    def dsem(self, name):
        if name not in self._ds:
            self._ds[name] = DSem(self.sem("d_" + name))
        return self._ds[name]

    def dma(self, q, out, in_, ds, waits=()):
        q.wait(*waits)
        inst = q.eng.dma_start(out=out, in_=in_)
        inst.then_inc(ds.sem, 16)
        ds.val += 16
        return (ds.sem, ds.val)

    def barrier(self, toks):
        for e in self.engs:
            e.wait(*toks)


def perm_view(ap3, d):
    return ap3.rearrange("p (m r) -> p r m", r=d)


def ffn_phase(kb, tag, x_src, gbc_d, w1d, w3d, w2d, ident_d, mode, x_dst=None, g2bc_d=None,
              hT_dst=None, fin_d=None):
    nc = kb.nc
    pe, act, dve, pool, sp = kb.pe, kb.act, kb.dve, kb.pool, kb.sp
    with ExitStack() as ph:
        def sb(name, shape, dt):
            return ph.enter_context(nc.sbuf_tensor(tag + name, shape, dt))

        def pst(name, shape, dt):
            return ph.enter_context(nc.psum_tensor(tag + name, shape, dt))

        w1s = sb("w1s", [128, KC, DFF], BF16)
        w3s = sb("w3s", [128, KC, DFF], BF16)
        w2s = sb("w2s", [128, NCH, D], BF16)
        gbc = sb("gbc", [128, KC, 128], F32)
        ident = sb("ident", [128, 128], BF16)
        xin = [sb(f"xin{i}", [128, D], F32) for i in range(2)]
        xr = [sb(f"xr{i}", [128, D], F32) for i in range(2)]
        xn = [sb(f"xn{i}", [128, D], BF16) for i in range(4)]
        xnT = sb("xnT", [128, KC, 512], BF16)
        gT = sb("gT", [128, NCH, 512], BF16)
        sl = [sb(f"sl{i}", [128, 512], F32) for i in range(2)]
        ssx = sb("ssx", [128, 8], F32)
        epsc = sb("epsc", [128, 1], F32)
        t_eps = dve.mark(nc.vector.memset(epsc[:], EPS))
        if mode == "ffn1":
            g2bc = sb("g2bc", [128, KC, 128], F32)
            xn2 = [sb(f"xn2{i}", [128, D], BF16) for i in range(2)]
            hst = sb("hst", [128, KC, 256], BF16)
            ss2 = sb("ss2", [128, 8], F32)
        else:
            finbc = sb("finbc", [128, D], F32)
            ss2 = sb("ss2", [128, 8], F32)
            junkb = [sb(f"junkb{i}", [128, D], BF16) for i in range(2)]
        pa = [pst(f"pa{i}", [128, 512], F32) for i in range(2)]
        pb = [pst(f"pb{i}", [128, 512], F32) for i in range(2)]
        py = [pst(f"py{i}", [128, 512], F32) for i in range(2)]
        ptp = [pst(f"ptp{i}", [128, KC, 128], BF16) for i in range(2)]

        dc = kb.dsem(tag + "const")
        kb.dma(sp, gbc[:], gbc_d, dc)
        dcp = kb.dsem(tag + "constp")
        kb.dma(pool, ident[:], ident_d, dcp)
        if mode == "ffn1":
            kb.dma(sp, g2bc[:], g2bc_d, dc)
        else:
            kb.dma(sp, finbc[:], fin_d, dc)
        tok_const = (dc.tok(), dcp.tok())
        w1v = w1d.rearrange("(kc p) n -> p kc n", p=128)
        w3v = w3d.rearrange("(kc p) n -> p kc n", p=128)
        w2v = w2d.rearrange("(c p) n -> p c n", p=128)
        tok_w13 = []
        tok_w2 = []
        for bi, (c0, c1) in enumerate(WBLK):
            ds = kb.dsem(tag + f"w13_{bi}")
            kb.dma(pool, w1s[:, :, c0 * 128:c1 * 128], w1v[:, :, c0 * 128:c1 * 128], ds)
            kb.dma(pool, w3s[:, :, c0 * 128:c1 * 128], w3v[:, :, c0 * 128:c1 * 128], ds)
            tok_w13.append(ds.tok())
        for bi, (c0, c1) in enumerate(WBLK):
            ds = kb.dsem(tag + f"w2_{bi}")
            kb.dma(pool, w2s[:, c0:c1, :], w2v[:, c0:c1, :], ds)
            tok_w2.append(ds.tok())

        def wblk_of(c):
            for bi, (c0, c1) in enumerate(WBLK):
                if c0 <= c < c1:
                    return bi

        def rstd_chain(src, junk, ssap, waits):
            act.wait(waits, t_eps)
            t = act.mark(nc.scalar.activation(out=junk, in_=src, func=AF.Square, accum_out=ssap))
            act.wait(t)
            t = act.mark(nc.scalar.activation(out=ssap, in_=ssap, func=AF.Ln, scale=1.0 / D, bias=epsc[:, 0:1]))
            act.wait(t)
            t = act.mark(nc.scalar.activation(out=ssap, in_=ssap, func=AF.Exp, scale=-0.5))
            return t

        xin_ld = [kb.dsem(tag + f"xin_ld{i}") for i in range(2)]
        xin_free = [None, None]
        xn_free = [None] * 4
        xr_ld = [kb.dsem(tag + f"xr_ld{i}") for i in range(2)]
        xr_st = [kb.dsem(tag + f"xr_st{i}") for i in range(2)]
        xr_free = [None, None]
        ptp_free = [None, None]
        pa_free = [None, None]
        pb_free = [None, None]
        sl_free = [None, None]
        py_free = [None, None]
        st = {"xnT_ready": None, "xnT_parts": [], "hst_free": None, "ctr_xin": 0, "ctr_tp": 0,
              "ctr_xr": 0, "ctr_y": 0, "ctr_xn2": 0}
        hst_ds = kb.dsem(tag + "hst_st") if mode == "ffn1" else None
        xn2_free = [None, None]
        pending_h = []
        final_toks = []

        def norm_chain(i, s):
            k = st["ctr_xin"]; st["ctr_xin"] += 1
            b = k % 2
            r0 = i * 512 + s * 128
            tl = kb.dma(sp, xin[b][:], x_src[r0:r0 + 128, :], xin_ld[b], waits=(xin_free[b],))
            t_r = rstd_chain(xin[b][:], xn[s][:], ssx[:, s:s + 1], (tl, xn_free[s]))
            dve.wait(t_r)
            t_xn = dve.mark(nc.vector.tensor_scalar(out=xn[s][:], in0=xin[b][:], scalar1=ssx[:, s:s + 1],
                                                    scalar2=None, op0=ALU.mult))
            xin_free[b] = t_xn
            st["t_xn", s] = t_xn

        def norm_tp(i, s):
            t_xn = st["t_xn", s]
            j = st["ctr_tp"]; st["ctr_tp"] += 1
            pbk = j % 2
            pe.wait(t_xn, ptp_free[pbk], tok_const)
            for kc in range(KC):
                ins = nc.tensor.transpose(ptp[pbk][:, kc, :], xn[s][:, kc * 128:(kc + 1) * 128], ident[:])
            t_tp = pe.mark(ins)
            xn_free[s] = t_tp
            dve.wait(t_tp, tok_const)
            t_ev = dve.mark(nc.vector.tensor_tensor(out=xnT[:, :, s * 128:(s + 1) * 128], in0=ptp[pbk][:],
                                                    in1=gbc[:], op=ALU.mult))
            ptp_free[pbk] = t_ev
            return t_ev

        def h_transposes():
            while pending_h:
                (bi2, t_rdy, ti, s) = pending_h.pop(0)
                j = st["ctr_tp"]; st["ctr_tp"] += 1
                pbk = j % 2
                pe.wait(t_rdy, ptp_free[pbk])
                for kc in range(KC):
                    ins = nc.tensor.transpose(ptp[pbk][:, kc, :], xn2[bi2][:, kc * 128:(kc + 1) * 128], ident[:])
                t_tp = pe.mark(ins)
                xn2_free[bi2] = t_tp
                waits = [t_tp]
                if s % 2 == 0:
                    waits.append(st["hst_free"])
                dve.wait(*waits)
                s2 = s % 2
                t_ev = dve.mark(nc.vector.tensor_tensor(out=hst[:, :, s2 * 128:(s2 + 1) * 128], in0=ptp[pbk][:],
                                                        in1=g2bc[:], op=ALU.mult))
                ptp_free[pbk] = t_ev
                if s2 == 1:
                    c0 = ti * 512 + (s // 2) * 256
                    tk = kb.dma(sp, hT_dst[:, :, c0:c0 + 256].rearrange("kc p t -> p kc t"), hst[:],
                                hst_ds, waits=(t_ev,))
                    st["hst_free"] = tk
                    final_toks.append(tk)

        def h_stage(i, t_xnT):
            toks = []
            for c in range(NCH):
                if i + 1 < NT and c in (3, 8, 13, 18):
                    norm_chain(i + 1, (3, 8, 13, 18).index(c))
                b = c % 2
                pe.wait(t_xnT, tok_w13[wblk_of(c)], pa_free[b], pb_free[b])
                for kc in range(KC):
                    ins = nc.tensor.matmul(pa[b][:], lhsT=w1s[:, kc, c * 128:(c + 1) * 128], rhs=xnT[:, kc, :],
                                           start=(kc == 0), stop=(kc == KC - 1))
                t_a = pe.mark(ins)
                for kc in range(KC):
                    ins = nc.tensor.matmul(pb[b][:], lhsT=w3s[:, kc, c * 128:(c + 1) * 128], rhs=xnT[:, kc, :],
                                           start=(kc == 0), stop=(kc == KC - 1))
                t_b = pe.mark(ins)
                act.wait(t_a, sl_free[b])
                t_s = act.mark(nc.scalar.activation(out=sl[b][:], in_=pa[b][:], func=AF.Silu))
                pa_free[b] = t_s
                dve.wait(t_s, t_b)
                t_g = dve.mark(nc.vector.tensor_tensor(out=gT[:, c, :], in0=sl[b][:], in1=pb[b][:], op=ALU.mult))
                pb_free[b] = t_g
                sl_free[b] = t_g
                toks.append(t_g)
            return toks[-1]

        def y_stage(i, t_g):
            for s in range(4):
                k = st["ctr_xr"]; st["ctr_xr"] += 1
                rb = k % 2
                r0 = i * 512 + s * 128
                tl = kb.dma(sp, xr[rb][:], x_src[r0:r0 + 128, :], xr_ld[rb], waits=(xr_free[rb],))
                t_res = None
                for hf in range(2):
                    j = st["ctr_y"]; st["ctr_y"] += 1
                    yb = j % 2
                    pe.wait(t_g, py_free[yb], *tok_w2)
                    for c in range(NCH):
                        ins = nc.tensor.matmul(py[yb][:], lhsT=gT[:, c, s * 128:(s + 1) * 128],
                                               rhs=w2s[:, c, hf * 512:(hf + 1) * 512],
                                               start=(c == 0), stop=(c == NCH - 1))
                    t_y = pe.mark(ins)
                    dve.wait(t_y, tl)
                    t_res = dve.mark(nc.vector.scalar_tensor_tensor(
                        out=xr[rb][:, hf * 512:(hf + 1) * 512], in0=py[yb][:], scalar=0.5,
                        in1=xr[rb][:, hf * 512:(hf + 1) * 512], op0=ALU.mult, op1=ALU.add))
                    py_free[yb] = t_res
                if mode == "ffn1":
                    tst = kb.dma(sp, x_dst[r0:r0 + 128, :], xr[rb][:], xr_st[rb], waits=(t_res,))
                    final_toks.append(tst)
                    k2 = st["ctr_xn2"]; st["ctr_xn2"] += 1
                    b2 = k2 % 2
                    t_r = rstd_chain(xr[rb][:], xn2[b2][:], ss2[:, b2:b2 + 1], (t_res, xn2_free[b2]))
                    dve.wait(t_r)
                    t_x2 = dve.mark(nc.vector.tensor_scalar(out=xn2[b2][:], in0=xr[rb][:], scalar1=ss2[:, b2:b2 + 1],
                                                            scalar2=None, op0=ALU.mult))
                    xr_free[rb] = (t_x2, tst)
                    pending_h.append((b2, t_x2, i, s))
                    if len(pending_h) > 1:
                        keep = pending_h.pop()
                        h_transposes()
                        pending_h.append(keep)
                else:
                    jb = k % 2
                    t_r = rstd_chain(xr[rb][:], junkb[jb][:], ss2[:, rb:rb + 1], (t_res,))
                    dve.wait(t_r, tok_const)
                    t_o = dve.mark(nc.vector.scalar_tensor_tensor(
                        out=xr[rb][:], in0=xr[rb][:], scalar=ss2[:, rb:rb + 1], in1=finbc[:],
                        op0=ALU.mult, op1=ALU.mult))
                    tst = kb.dma(sp, x_dst[r0:r0 + 128, :], xr[rb][:], xr_st[rb], waits=(t_o,))
                    final_toks.append(tst)
                    xr_free[rb] = (tst,)
            return

        for s_ in range(4):
            norm_chain(0, s_)
        for s_ in range(4):
            t_xnT = norm_tp(0, s_)
        for i in range(NT):
            t_g = h_stage(i, t_xnT)
            if i + 1 < NT:
                for s_ in range(4):
                    t_xnT = norm_tp(i + 1, s_)
            if mode == "ffn1":
                h_transposes()
            y_stage(i, t_g)
        if mode == "ffn1":
            h_transposes()
        last = {}
        for t in final_toks:
            last[id(t[0])] = t if (id(t[0]) not in last or last[id(t[0])][1] < t[1]) else last[id(t[0])]
        kb.barrier(list(last.values()))


def _t5_bucket_np(rel):
    n = 16
    max_exact = 8
    ret = np.where(rel > 0, n, 0)
    a = np.abs(rel)
    af = np.maximum(a, 1).astype(np.float32)
    large = max_exact + (np.log(af / np.float32(max_exact)) / np.float32(np.log(1024 / max_exact))
                         * np.float32(n - max_exact)).astype(np.int32)
    large = np.minimum(large, n - 1)
    return ret + np.where(a < max_exact, a, large)


def host_tables(rel_bias):
    NEG = np.float32(-30000.0)
    row = np.arange(128)[:, None]
    col = np.arange(256)[None, :]
    rel = np.where(col < 128, 64 + row - col, row - 64 - (col - 128))
    valid_int = np.abs(rel) <= 64
    bnd_ok = np.where(col < 128, row < 64, row >= 64)
    tab = np.empty((24, 2, 128, 256), np.float32)
    for g, (_, d) in enumerate(GROUPS):
        bucket = _t5_bucket_np((rel * d).astype(np.int32))
        for h in range(8):
            bias = rel_bias[bucket, g * 8 + h].astype(np.float32)
            tab[g * 8 + h, 0] = np.where(valid_int, bias, NEG)
            tab[g * 8 + h, 1] = np.where(valid_int & bnd_ok, bias, NEG)
    return tab


def rope_tables():
    t = np.arange(S)
    rowi = (t // 64).astype(np.float32)
    coli = (t % 64).astype(np.float32)
    nf = 32
    freq = (np.float32(10000.0) ** (-np.arange(nf, dtype=np.float32) / np.float32(nf))).astype(np.float32)
    ang = np.concatenate([rowi[:, None] * freq, coli[:, None] * freq], axis=-1).astype(np.float32)
    c = np.cos(ang).astype(np.float32)
    s = np.sin(ang).astype(np.float32)
    C = np.repeat(c.T, 2, axis=0)
    Sn = np.repeat(s.T, 2, axis=0)
    return np.ascontiguousarray(C), np.ascontiguousarray(Sn)


def rot_lhsT():
    m = np.zeros((128, 128), np.float32)
    for i in range(64):
        m[2 * i + 1, 2 * i] = -1.0
        m[2 * i, 2 * i + 1] = 1.0
    return m


P2PARTS = ("bqk", "bv", "gate", "aqk", "av")
AVG = (0, 1, 2)
AVKT = range(33)
AVSIMPLE = 0
AVNOSTORE = 0
AVALIGN = 0
A_Q0, A_K0, A_V0 = 0, 1536, 3072
B_Q0, B_K0, B_V0, G0 = 4608, 5632, 5888, 6144


def window_pieces(g, kt):
    _, d = GROUPS[g]
    L = S // d
    halves = []
    for j in (2 * kt - 1, 2 * kt):
        if j < 0 or j >= S // 64:
            halves.append(None)
        else:
            r, m0 = divmod(64 * j, L)
            halves.append((m0 * d + r, r, m0))
    h0, h1 = halves
    if h0 is not None and h1 is not None and h0[1] == h1[1]:
        return [(0, 128, h0[0], d)]
    out = []
    for i, h in enumerate(halves):
        out.append((64 * i, 64, None if h is None else h[0], d))
    return out


def proj_phase(kb, dr):
    nc = kb.nc
    pe, act, dve, pool, sp = kb.pe, kb.act, kb.dve, kb.pool, kb.sp
    with ExitStack() as ph:
        def sb(name, shape, dt):
            return ph.enter_context(nc.sbuf_tensor("p2" + name, shape, dt))

        def pst(name, shape, dt):
            return ph.enter_context(nc.psum_tensor("p2" + name, shape, dt))

        hTs = sb("hTs", [128, KC, S + 64], BF16)
        ropeC = sb("ropeC", [128, S], F32)
        ropeS = sb("ropeS", [128, S], F32)
        ones_bf = sb("ones", [128, 128], BF16)
        rotT = sb("rotT", [128, 128], BF16)
        bg = sb("bg", [128, 16], F32)
        qkg = sb("qkg", [128, 2], F32)
        epsc = sb("epsc", [128, 1], F32)
        wc = [sb(f"wc{i}", [128, KC, 128], BF16) for i in range(3)]
        wv = [sb(f"wv{i}", [128, KC, 512], BF16) for i in range(2)]
        stg = [sb(f"stg{i}", [128, S + 128], BF16) for i in range(2)]
        vstgB = sb("vstgB", [128, 32, 256], BF16)
        vstgA = [sb(f"vstgA{i}", [128, 768], BF16) for i in range(3)]
        sq = [sb(f"sq{i}", [128, 512], BF16) for i in range(4)]
        t1 = [sb(f"t1{i}", [128, 512], F32) for i in range(4)]
        t2 = [sb(f"t2{i}", [128, 512], F32) for i in range(4)]
        t3 = [sb(f"t3{i}", [128, 512], F32) for i in range(4)]
        qnb = [sb(f"qnb{i}", [128, 512], BF16) for i in range(4)]
        bank = [pst(f"bk{i}", [128, 512], F32) for i in range(8)]
        pm = bank[0:2]
        pv = bank[2:4]

        dc = kb.dsem("p2const")
        hT_tok = []
        for blk in range(8):
            dsb_ = kb.dsem(f"p2hT{blk}")
            hT_tok.append(kb.dma(sp, hTs[:, :, blk * 512:(blk + 1) * 512],
                                 dr["hT"][:, :, blk * 512:(blk + 1) * 512].rearrange("kc p t -> p kc t"), dsb_))
        kb.dma(sp, bg[:], dr["bg_col"], dc)
        kb.dma(sp, qkg[:], dr["qkg_col"], dc)
        drope = kb.dsem("p2rope")
        kb.dma(sp, ropeC[:], dr["ropeC"], drope)
        kb.dma(sp, ropeS[:], dr["ropeS"], drope)
        tok_rope = drope.tok()
        dcp = kb.dsem("p2constp")
        kb.dma(pool, rotT[:], dr["rotT"], dcp)
        tok_const = (dc.tok(), dcp.tok())
        t_m0 = pool.mark(nc.gpsimd.memset(ones_bf[:], 1.0))
        nc.gpsimd.memset(epsc[:], EPS)
        t_m1 = pool.mark(nc.gpsimd.memset(hTs[:, :, S:S + 64], 0.0))
        toks_small = [tok_const, t_m0, t_m1]
        for i in range(3):
            toks_small.append(pool.mark(nc.gpsimd.memset(vstgA[i][:], 1.0)))
        toks_init = toks_small + hT_tok

        w_in_v = dr["w_in"].rearrange("(kc p) n -> p kc n", p=128)
        wc_ld = [kb.dsem(f"p2wc{i}") for i in range(3)]
        wc_free = [None] * 3
        wv_ld = [kb.dsem(f"p2wv{i}") for i in range(2)]
        wv_free = [None] * 2
        stg_st = [kb.dsem(f"p2stg{i}") for i in range(2)]
        stg_free = [None] * 2
        vstgA_st = [kb.dsem(f"p2vsa{i}") for i in range(3)]
        vstgA_free = [None] * 3
        bank_free = [None] * 8

        class _View:
            def __init__(self, off):
                self.off = off

            def __getitem__(self, i):
                return bank_free[self.off + i]

            def __setitem__(self, i, v):
                bank_free[self.off + i] = v
        pm_free = _View(0)
        pv_free = _View(2)
        sq_free = [None] * 4
        t1_free = [None] * 4
        t2_free = [None] * 4
        t3_free = [None] * 4
        qnb_free = [None] * 4
        ctr = {"wc": 0, "stg": 0, "pm": 0, "wv": 0, "pv": 0, "vsa": 0, "b": 0, "ev": 0}
        final_toks = []

        def tok_rhs(g, kc, blk):
            _, d = GROUPS[g]
            base = hTs[:, kc, 0:S]
            if d == 1:
                return base[:, blk * 512:(blk + 1) * 512], None
            v = perm_view(base, d)
            if d == 4:
                return v[:, blk // 2, (blk % 2) * 512:(blk % 2) * 512 + 512], None
            return v[:, 2 * blk:2 * blk + 2, :], 2

        def fm_chunk(col0, kind, g, dst, arg=None):
            k = ctr["wc"]; ctr["wc"] += 1
            ws = k % 3
            t_w = kb.dma(pool, wc[ws][:], w_in_v[:, :, col0:col0 + 128], wc_ld[ws], waits=(wc_free[ws],))
            ks = ctr["stg"]; ctr["stg"] += 1
            ss = ks % 2
            off = 64 if kind == "ak" else 0
            evs = {}
            if kind == "ak":
                dve.wait(stg_free[ss])
                nc.vector.memset(stg[ss][:, 0:64], 0.0)
                evs["pad"] = dve.mark(nc.vector.memset(stg[ss][:, S + 64:S + 128], 0.0))
            for blk in range(8):
                j = ctr["pm"]; ctr["pm"] += 1
                pb_ = j % 2
                pe.wait(t_w, toks_init, pm_free[pb_])
                for kc in range(KC):
                    rhs, two = tok_rhs(g, kc, blk)
                    o = pm[pb_][:]
                    if two:
                        o = o.rearrange("p (a b) -> p a b", a=2)
                    ins = nc.tensor.matmul(o, lhsT=wc[ws][:, kc, :], rhs=rhs, start=(kc == 0), stop=(kc == KC - 1))
                t_mm = pe.mark(ins)
                if blk == 7:
                    wc_free[ws] = t_mm
                dstap = stg[ss][:, off + blk * 512: off + (blk + 1) * 512]
                if kind in ("aq", "ak"):
                    e = ctr["ev"]; ctr["ev"] += 1
                    if e % 2 == 0:
                        dve.wait(t_mm, stg_free[ss])
                        t_ev = dve.mark(nc.vector.tensor_copy(out=dstap, in_=pm[pb_][:]))
                        evs["dve"] = t_ev
                    else:
                        act.wait(t_mm, stg_free[ss])
                        t_ev = act.mark(nc.scalar.activation(out=dstap, in_=pm[pb_][:], func=AF.Copy))
                        evs["act"] = t_ev
                    pm_free[pb_] = t_ev
                elif kind == "gate":
                    act.wait(t_mm, toks_init, stg_free[ss])
                    t_ev = act.mark(nc.scalar.activation(out=dstap, in_=pm[pb_][:], func=AF.Sigmoid,
                                                         bias=bg[:, arg:arg + 1]))
                    pm_free[pb_] = t_ev
                    evs["act"] = t_ev
            width = S + 128 if kind == "ak" else S
            tk = kb.dma(sp, dst, stg[ss][:, 0:width], stg_st[ss], waits=list(evs.values()))
            stg_free[ss] = tk
            final_toks.append(tk)


        def bqk_pipeline():
            chunks = [(B_K0 + j * 128, 1, dr["kTB"][j]) for j in range(2)] + \
                     [(B_Q0 + h * 128, 0, dr["qTB"][h]) for h in range(8)]
            N = len(chunks) * 8
            T = {}
            wtok = {}

            def load_w(c):
                ws = c % 3
                col0 = chunks[c][0]
                wtok[c] = kb.dma(pool, wc[ws][:], w_in_v[:, :, col0:col0 + 128], wc_ld[ws], waits=(wc_free[ws],))

            def S0(i):
                c, blk = divmod(i, 8)
                if blk == 0 and c + 1 < len(chunks):
                    load_w(c + 1)
                pe.wait(wtok[c], toks_small, hT_tok[blk], bank_free[i % 4])
                for kc in range(KC):
                    ins = nc.tensor.matmul(bank[i % 4][:], lhsT=wc[c % 3][:, kc, :],
                                           rhs=hTs[:, kc, blk * 512:(blk + 1) * 512],
                                           start=(kc == 0), stop=(kc == KC - 1))
                T["mm", i] = pe.mark(ins)
                if blk == 7:
                    wc_free[c % 3] = T["mm", i]

            def S1(i):
                act.wait(T["mm", i], sq_free[i % 4])
                T["sq", i] = act.mark(nc.scalar.activation(out=sq[i % 4][:], in_=bank[i % 4][:], func=AF.Square))

            def S2(i):
                pe.wait(T["sq", i], bank_free[4 + i % 2])
                T["ss", i] = pe.mark(nc.tensor.matmul(bank[4 + i % 2][:], lhsT=ones_bf[:], rhs=sq[i % 4][:],
                                                      start=True, stop=True))
                sq_free[i % 4] = T["ss", i]

            def S3(i):
                act.wait(T["ss", i], t1_free[i % 4], toks_init)
                tl = act.mark(nc.scalar.activation(out=t1[i % 4][:], in_=bank[4 + i % 2][:], func=AF.Ln,
                                                   scale=1.0 / 128, bias=epsc[:, 0:1]))
                bank_free[4 + i % 2] = tl
                act.wait(tl)
                T["rs", i] = act.mark(nc.scalar.activation(out=t1[i % 4][:], in_=t1[i % 4][:], func=AF.Exp, scale=-0.5))

            def S5(i):
                c = i // 8
                dve.wait(T["rs", i], t2_free[i % 4], toks_init)
                T["qn", i] = dve.mark(nc.vector.scalar_tensor_tensor(
                    out=t2[i % 4][:], in0=bank[i % 4][:], scalar=qkg[:, chunks[c][1]:chunks[c][1] + 1],
                    in1=t1[i % 4][:], op0=ALU.mult, op1=ALU.mult))
                bank_free[i % 4] = T["qn", i]
                t1_free[i % 4] = T["qn", i]

            def S6(i):
                act.wait(T["qn", i], qnb_free[i % 4])
                T["qb", i] = act.mark(nc.scalar.activation(out=qnb[i % 4][:], in_=t2[i % 4][:], func=AF.Copy))

            def S7(i):
                pe.wait(T["qb", i], bank_free[6 + i % 2])
                T["rot", i] = pe.mark(nc.tensor.matmul(bank[6 + i % 2][:], lhsT=rotT[:], rhs=qnb[i % 4][:],
                                                       start=True, stop=True))
                qnb_free[i % 4] = T["rot", i]

            def S8(i):
                blk = i % 8
                tsl = slice(blk * 512, (blk + 1) * 512)
                dve.wait(T["qb", i], T["qn", i], tok_rope)
                T["c", i] = dve.mark(nc.vector.tensor_tensor(out=t2[i % 4][:], in0=t2[i % 4][:], in1=ropeC[:, tsl],
                                                             op=ALU.mult))
                dve.wait(T["rot", i], t3_free[i % 4])
                T["s", i] = dve.mark(nc.vector.tensor_tensor(out=t3[i % 4][:], in0=bank[6 + i % 2][:],
                                                             in1=ropeS[:, tsl], op=ALU.mult))
                bank_free[6 + i % 2] = T["s", i]

            def S10(i):
                c, blk = divmod(i, 8)
                ss = c % 2
                pool.wait(T["c", i], T["s", i], stg_free[ss])
                T["o", i] = pool.mark(nc.gpsimd.tensor_tensor(out=stg[ss][:, blk * 512:(blk + 1) * 512],
                                                              in0=t2[i % 4][:], in1=t3[i % 4][:], op=ALU.add))
                t2_free[i % 4] = T["o", i]
                t3_free[i % 4] = T["o", i]
                if blk == 7:
                    tk = kb.dma(sp, chunks[c][2], stg[ss][:, 0:S], stg_st[ss], waits=(T["o", i],))
                    stg_free[ss] = tk
                    final_toks.append(tk)

            load_w(0)
            for step in range(N + 5):
                if step < N:
                    S0(step)
                    S1(step)
                if 0 <= step - 1 < N:
                    S2(step - 1)
                    S3(step - 1)
                if 0 <= step - 2 < N:
                    S5(step - 2)
                    S6(step - 2)
                if 0 <= step - 3 < N:
                    S7(step - 3)
                    S8(step - 3)
                if 0 <= step - 4 < N:
                    S10(step - 4)
            ctr["wc"] = len(chunks)
            ctr["stg"] = len(chunks)
            ctr["pm"] = 0

        if "bqk" in P2PARTS:
            bqk_pipeline()
        t_wv = kb.dma(pool, wv[0][:, :, 0:256], w_in_v[:, :, B_V0:B_V0 + 256], wv_ld[0])
        ctr["wv"] = 1
        evb = {}
        for tt in range(32 if "bv" in P2PARTS else 0):
            j = ctr["pv"]; ctr["pv"] += 1
            pb_ = j % 2
            pe.wait(t_wv, toks_init, pv_free[pb_])
            for kc in range(KC):
                ins = nc.tensor.matmul(pv[pb_][:, 0:256], lhsT=hTs[:, kc, tt * 128:(tt + 1) * 128],
                                       rhs=wv[0][:, kc, 0:256], start=(kc == 0), stop=(kc == KC - 1))
            t_mm = pe.mark(ins)
            if tt % 2 == 0:
                dve.wait(t_mm)
                t_ev = dve.mark(nc.vector.tensor_copy(out=vstgB[:, tt, :], in_=pv[pb_][:, 0:256]))
                evb["dve"] = t_ev
            else:
                act.wait(t_mm)
                t_ev = act.mark(nc.scalar.activation(out=vstgB[:, tt, :], in_=pv[pb_][:, 0:256], func=AF.Copy))
                evb["act"] = t_ev
            pv_free[pb_] = t_ev
            if tt == 31:
                wv_free[0] = t_mm
        dsb = kb.dsem("p2vB")
        if "bv" in P2PARTS:
          tk = kb.dma(sp, dr["vB"].rearrange("(t p) c -> p t c", p=128), vstgB[:], dsb, waits=list(evb.values()))
          final_toks.append(tk)
        for c in range(16 if "gate" in P2PARTS else 0):
            fm_chunk(G0 + c * 128, "gate", 0, dr["gT"][c], arg=c)
        for g in range(3):
            for hp in range(4 if "aqk" in P2PARTS else 0):
                fm_chunk(A_Q0 + g * 512 + hp * 128, "aq", g, dr["qTA"][g * 4 + hp])
                fm_chunk(A_K0 + g * 512 + hp * 128, "ak", g, dr["kTA"][g * 4 + hp])
            if "av" not in P2PARTS or g not in AVG:
                continue
            k = ctr["wv"]; ctr["wv"] += 1
            ws = k % 2
            t_wv = kb.dma(pool, wv[ws][:], w_in_v[:, :, A_V0 + g * 512:A_V0 + (g + 1) * 512], wv_ld[ws],
                          waits=(wv_free[ws],))
            for kt in AVKT:
                j = ctr["pv"]; ctr["pv"] += 1
                pb_ = j % 2
                pe.wait(t_wv, toks_init, pv_free[pb_])
                for (p0, cnt, start, d) in window_pieces(g, kt):
                    for kc in range(KC):
                        if start is None:
                            lhsT = hTs[:, kc, S:S + cnt]
                        elif d == 1:
                            lhsT = hTs[:, kc, start:start + cnt]
                        else:
                            r = start % d
                            m0 = start // d
                            lhsT = perm_view(hTs[:, kc, 0:S], d)[:, r, m0:m0 + cnt]
                        ins = nc.tensor.matmul(pv[pb_][p0:p0 + cnt, :], lhsT=lhsT, rhs=wv[ws][:, kc, :],
                                               start=(kc == 0), stop=(kc == KC - 1))
                t_mm = pe.mark(ins)
                if kt == AVKT[-1]:
                    wv_free[ws] = t_mm
                kv_ = ctr["vsa"]; ctr["vsa"] += 1
                vs_ = kv_ % 3
                src = pv[pb_][:].rearrange("p (hp eo d) -> p hp eo d", hp=4, eo=2)
                dstv = vstgA[vs_][:].rearrange("p (hp x) -> p hp x", x=192)
                if kt % 2 == 0:
                    dve.wait(t_mm, vstgA_free[vs_], toks_init)
                    nc.vector.tensor_copy(out=dstv[:, :, 0:64], in_=src[:, :, 0, :])
                    t_e0 = dve.mark(nc.vector.tensor_copy(out=dstv[:, :, 128:192], in_=src[:, :, 1, :]))
                else:
                    act.wait(t_mm, vstgA_free[vs_], toks_init)
                    nc.scalar.activation(out=dstv[:, :, 0:64], in_=src[:, :, 0, :], func=AF.Copy)
                    t_e0 = act.mark(nc.scalar.activation(out=dstv[:, :, 128:192], in_=src[:, :, 1, :], func=AF.Copy))
                t_e1 = t_e0
                pv_free[pb_] = (t_e0, t_e1)
                if AVNOSTORE:
                    vstgA_free[vs_] = (t_e0, t_e1)
                    continue
                tk = kb.dma(sp, dr["vA"][g * 33 + kt], vstgA[vs_][:], vstgA_st[vs_], waits=(t_e0, t_e1))
                vstgA_free[vs_] = tk
                final_toks.append(tk)
        last = {}
        for t in final_toks:
            if id(t[0]) not in last or last[id(t[0])][1] < t[1]:
                last[id(t[0])] = t
        kb.barrier(list(last.values()))


def _final_barrier(kb, final_toks):
    last = {}
    for t in final_toks:
        if id(t[0]) not in last or last[id(t[0])][1] < t[1]:
            last[id(t[0])] = t
    kb.barrier(list(last.values()))


def attn_a_phase(kb, dr):
    nc = kb.nc
    pe, act, dve, pool, sp = kb.pe, kb.act, kb.dve, kb.pool, kb.sp
    with ExitStack() as ph:
        def sb(name, shape, dt):
            return ph.enter_context(nc.sbuf_tensor("p3" + name, shape, dt))

        def pst(name, shape, dt):
            return ph.enter_context(nc.psum_tensor("p3" + name, shape, dt))

        NB = 6
        qs = [sb(f"qs{i}", [128, S], BF16) for i in range(2)]
        ks = [sb(f"ks{i}", [128, S + 128], BF16) for i in range(2)]
        vs = [sb(f"vs{i}", [128, 33, 192], BF16) for i in range(2)]
        tb = [sb(f"tb{i}", [128, 2, 2, 256], F32) for i in range(2)]
        wt = [sb(f"wt{i}", [128, 2, 2, 256], BF16) for i in range(2)]
        acc = [[sb(f"acc{j}{i}", [128, S], F32) for i in range(2)] for j in range(2)]
        den2 = sb("den2", [128, S], F32)
        ost = sb("ost", [128, S], BF16)
        pex = [sb(f"pex{i}", [128, 256], BF16) for i in range(NB)]
        pT = [sb(f"pT{i}", [128, 256], BF16) for i in range(NB)]
        psT = [pst(f"psT{i}", [128, 512], F32) for i in range(4)]
        pU = [[pst(f"pU{e}{i}", [128, 512], F32) for i in range(2)] for e in range(2)]

        ld = [kb.dsem(f"p3ld{i}") for i in range(2)]
        slot_free = [None, None]
        wt_free = [None, None]
        sT_free = [None] * 4
        pex_free = [None] * NB
        pT_free = [None] * NB
        pU_free = [[None, None], [None, None]]
        acc_free = [[None, None], [None, None]]
        den_ds = kb.dsem("p3den")
        ost_ds = kb.dsem("p3ost")
        final_toks = []
        LAG = 2
        it = 0
        pending_norm = []
        nst = {"ost_free": None, "den_free": None}

        def do_norm_dma(hp_, aj_, al_):
            kb.dma(sp, den2[0:64, :], acc[aj_][0][64:128, :], den_ds, waits=(al_[0], al_[1], nst["den_free"]))
            t2_ = kb.dma(sp, den2[64:128, :], acc[aj_][1][0:64, :], den_ds)
            pending_norm2.append((hp_, aj_, t2_))

        def do_norm_act(hp_, aj_, t2_, c):
            cs = slice(c * 1024, (c + 1) * 1024)
            act.wait(t2_)
            t = act.mark(nc.scalar.activation(out=den2[:, cs], in_=den2[:, cs], func=AF.Ln))
            act.wait(t)
            nst["r", c] = act.mark(nc.scalar.activation(out=den2[:, cs], in_=den2[:, cs], func=AF.Exp, scale=-1.0))

        def do_norm_dve(hp_, aj_, t2_, c):
            cs = slice(c * 1024, (c + 1) * 1024)
            dve.wait(nst["r", c], nst["ost_free"])
            nc.vector.tensor_tensor(out=ost[0:64, cs], in0=acc[aj_][0][0:64, cs], in1=den2[0:64, cs], op=ALU.mult)
            t_o = dve.mark(nc.vector.tensor_tensor(out=ost[64:128, cs], in0=acc[aj_][1][64:128, cs],
                                                   in1=den2[64:128, cs], op=ALU.mult))
            if c == 3:
                acc_free[aj_] = [t_o, t_o]
                nst["den_free"] = t_o
                nst["ost_free"] = kb.dma(sp, dr["oaT"][hp_], ost[:], ost_ds, waits=(t_o,))
                final_toks.append(nst["ost_free"])

        pending_norm2 = []
        for hp in range(4):
            aj = hp % 2
            acc_last = [None, None]
            for g in range(3):
                _, d = GROUPS[g]
                L = S // d
                sl_ = it % 2
                it += 1
                pidx = g * 4 + hp
                kb.dma(sp, qs[sl_][:], dr["qTA"][pidx], ld[sl_], waits=(slot_free[sl_],))
                kb.dma(sp, ks[sl_][:], dr["kTA"][pidx], ld[sl_])
                kb.dma(sp, vs[sl_][:], dr["vA"][g * 33:(g + 1) * 33, :, hp * 192:(hp + 1) * 192].rearrange("kt p c -> p kt c"),
                       ld[sl_])
                h0 = g * 8 + 2 * hp
                t_ld = kb.dma(sp, tb[sl_][:], dr["tabA"][h0:h0 + 2].rearrange("h v p c -> p h v c"), ld[sl_])
                act.wait(t_ld, wt_free[sl_])
                t_wt = act.mark(nc.scalar.activation(out=wt[sl_][:], in_=tb[sl_][:], func=AF.Exp))
                pend = []
                last_pv = [None]

                def evac_bank(b, e):
                    bank = pU[e][b % 2]
                    if d == 1:
                        dst = acc[aj][e][:, b * 512:(b + 1) * 512]
                        src = bank[:]
                    elif d == 4:
                        dst = perm_view(acc[aj][e][:], 4)[:, b // 2, (b % 2) * 512:(b % 2) * 512 + 512]
                        src = bank[:]
                    else:
                        dst = perm_view(acc[aj][e][:], 16)[:, 2 * b:2 * b + 2, :]
                        src = bank[:].rearrange("p (a b) -> p a b", a=2)
                    dve.wait(last_pv[0], acc_free[aj][e] if g == 0 else None)
                    if g == 0:
                        t = dve.mark(nc.vector.tensor_copy(out=dst, in_=src))
                    else:
                        t = dve.mark(nc.vector.tensor_tensor(out=dst, in0=dst, in1=src, op=ALU.add))
                    pU_free[e][b % 2] = t
                    acc_last[e] = t

                def do_pv(kt, e, bufi, t_p):
                    lhsT = vs[sl_][:, kt, 64 * e:64 * e + 128]
                    pe.wait(t_p)
                    ins = None
                    if kt >= 1:
                        a_ = kt - 1
                        ins = nc.tensor.matmul(pU[e][(a_ // 4) % 2][:, (a_ % 4) * 128:(a_ % 4) * 128 + 128], lhsT=lhsT,
                                               rhs=pT[bufi][:, 0:128], start=False, stop=True)
                    if kt <= 31:
                        a_ = kt
                        if a_ % 4 == 0:
                            pe.wait(pU_free[e][(a_ // 4) % 2])
                        ins = nc.tensor.matmul(pU[e][(a_ // 4) % 2][:, (a_ % 4) * 128:(a_ % 4) * 128 + 128], lhsT=lhsT,
                                               rhs=pT[bufi][:, 128:256], start=True, stop=False)
                    t = pe.mark(ins)
                    pT_free[bufi] = t
                    last_pv[0] = t
                    if kt >= 4 and kt % 4 == 0:
                        evac_bank(kt // 4 - 1, e)

                for step in range(33 + LAG):
                    if step < 33:
                        kt = step
                        var = 1 if (128 * kt) % L == 0 else 0
                        lo = 128 if kt == 0 else 0
                        hi = 128 if kt == 32 else 256
                        c0 = 128 * (kt - 1) + lo
                        for e in range(2):
                            rows = slice(64 * e, 64 * e + 64)
                            bufi = (kt % 3) * 2 + e
                            si = (kt % 2) * 2 + e
                            sTt = psT[si][:, 0:256]
                            pe.wait(t_ld, sT_free[si])
                            t_s = pe.mark(nc.tensor.matmul(sTt[:, lo:hi], lhsT=ks[sl_][rows, 128 * kt:128 * kt + 128],
                                                           rhs=qs[sl_][rows, c0:c0 + (hi - lo)], start=True, stop=True))
                            act.wait(t_s, pex_free[bufi])
                            t_x = act.mark(nc.scalar.activation(out=pex[bufi][:, lo:hi], in_=sTt[:, lo:hi], func=AF.Exp,
                                                                scale=0.125))
                            sT_free[si] = t_x
                            dve.wait(t_x, pT_free[bufi], t_wt)
                            t_p = dve.mark(nc.vector.tensor_tensor(out=pT[bufi][:, lo:hi], in0=pex[bufi][:, lo:hi],
                                                                   in1=wt[sl_][:, e, var, lo:hi], op=ALU.mult))
                            pex_free[bufi] = t_p
                            pend.append((kt, e, bufi, t_p))
                    if step >= LAG:
                        for _ in range(2):
                            do_pv(*pend.pop(0))
                    if step == 2 and pending_norm:
                        do_norm_dma(*pending_norm.pop(0))
                    for c_ in range(4):
                        if step == 14 + 4 * c_ and pending_norm2:
                            do_norm_act(*pending_norm2[0], c_)
                        if step == 17 + 4 * c_ and pending_norm2:
                            do_norm_dve(*pending_norm2[0], c_)
                            if c_ == 3:
                                pending_norm2.pop(0)
                slot_free[sl_] = (last_pv[0], acc_last[0], acc_last[1])
                wt_free[sl_] = acc_last[1]
            pending_norm.append((hp, aj, list(acc_last)))
        while pending_norm:
            do_norm_dma(*pending_norm.pop(0))
        while pending_norm2:
            for c_ in range(4):
                do_norm_act(*pending_norm2[0], c_)
                do_norm_dve(*pending_norm2[0], c_)
            pending_norm2.pop(0)
        _final_barrier(kb, final_toks)


def attn_b_consts(kb, dr, es):
    nc = kb.nc
    kTs = es.enter_context(nc.sbuf_tensor("p4kTs", [128, 2, S], BF16))
    vBs = es.enter_context(nc.sbuf_tensor("p4vBs", [128, 32, 256], BF16))
    qs = [es.enter_context(nc.sbuf_tensor("p4qs0", [128, S], BF16))]
    dc = kb.dsem("p4const")
    for j in range(2):
        kb.dma(kb.sp, kTs[:, j, :], dr["kTB"][j], dc)
    kb.dma(kb.sp, vBs[:], dr["vB"].rearrange("(t p) c -> p t c", p=128), dc)
    q_ld = [kb.dsem(f"p4q{i}") for i in range(2)]
    t_q0 = kb.dma(kb.sp, qs[0][:], dr["qTB"][0], q_ld[0])
    return kTs, vBs, qs, dc.tok(), q_ld, t_q0


def attn_b_phase(kb, dr, pre):
    nc = kb.nc
    pe, act, dve, pool, sp = kb.pe, kb.act, kb.dve, kb.pool, kb.sp
    SCALE = 128 ** -0.5
    with ExitStack() as ph:
        def sb(name, shape, dt):
            return ph.enter_context(nc.sbuf_tensor("p4" + name, shape, dt))

        def pst(name, shape, dt):
            return ph.enter_context(nc.psum_tensor("p4" + name, shape, dt))

        NP = 4
        NB = 3
        kTs, vBs, qs, tok_const, q_ld, t_q0 = pre
        qs = [qs[0], sb("qs1", [128, S], BF16)]
        ones_bf = sb("ones", [128, 128], BF16)
        ost = [sb(f"ost{i}", [128, S], BF16) for i in range(2)]
        pT = [sb(f"pT{i}", [128, 1024], BF16) for i in range(NP)]
        xx = [sb(f"xx{i}", [128, 1024], BF16) for i in range(2)]
        xx_free = [None, None]
        prev_tp = [None]
        qd = [sb(f"qd{i}", [128, 512], BF16) for i in range(3)]
        racc = [sb(f"racc{i}", [128, 512], F32) for i in range(2)]
        raccb = [sb(f"raccb{i}", [128, 512], BF16) for i in range(2)]
        rD = [sb(f"rD{i}", [128, 512], F32) for i in range(2)]
        psT = [pst(f"psT{i}", [128, 1024], F32) for i in range(NB)]
        pO = [pst(f"pO{i}", [128, 512], F32) for i in range(2)]

        t_ones = pool.mark(nc.gpsimd.memset(ones_bf[:], 1.0))
        q_free = [None, None]
        o_st = [kb.dsem(f"p4o{i}") for i in range(2)]
        o_free = [None, None]
        sT_free = [None] * NB
        tick = [0]
        pT_free = [None] * NP
        qd_free = [None] * 3
        pO_free = [None, None]
        racc_free = [None, None]
        raccb_free = [None, None]
        rD_free = [None, None]
        final_toks = []
        t_q = {}
        items = [(h, qb, pj) for h in range(8) for qb in range(8) for pj in range(16)]
        pend = []
        pend_fin = []
        last_acc = {}
        last_pv = {}
        npp = [0]

        def load_q(h):
            b = h % 2
            t_q[h] = kb.dma(sp, qs[b][:], dr["qTB"][h], q_ld[b], waits=(q_free[b],))

        def do_pv(j, h, qb, pj, t_p):
            ob = (h * 8 + qb) % 2
            kv = h // 4
            pe.wait(t_p)
            if pj == 0:
                pe.wait(pO_free[ob])
            for u in range(2):
                kt = 2 * pj + u
                ins = nc.tensor.matmul(pO[ob][:], lhsT=vBs[:, kt, kv * 128:(kv + 1) * 128],
                                       rhs=pT[j % NP][:, u * 512:(u + 1) * 512],
                                       start=(kt == 0), stop=(kt == 31))
            t = pe.mark(ins)
            last_pv[(h, qb)] = t
            return t

        fin = {}

        def fin_cast(h, qb):
            ob = (h * 8 + qb) % 2
            dve.wait(last_acc[(h, qb)], raccb_free[ob])
            t_c = dve.mark(nc.vector.tensor_copy(out=raccb[ob][:], in_=racc[ob][:]))
            racc_free[ob] = t_c
            fin["c"] = t_c

        def fin_dmm(h, qb):
            ob = (h * 8 + qb) % 2
            bt = tick[0] % NB
            tick[0] += 1
            pe.wait(fin["c"], t_ones, sT_free[bt])
            t_d = pe.mark(nc.tensor.matmul(psT[bt][:, 0:512], lhsT=ones_bf[:], rhs=raccb[ob][:], start=True, stop=True))
            raccb_free[ob] = t_d
            fin["d"] = (t_d, bt)

        def fin_act(h, qb):
            ob = (h * 8 + qb) % 2
            t_d, bt = fin["d"]
            act.wait(t_d, rD_free[ob])
            t_l = act.mark(nc.scalar.activation(out=rD[ob][:], in_=psT[bt][:, 0:512], func=AF.Ln))
            sT_free[bt] = t_l
            act.wait(t_l)
            fin["r"] = act.mark(nc.scalar.activation(out=rD[ob][:], in_=rD[ob][:], func=AF.Exp, scale=-1.0))

        def fin_mul(h, qb):
            ob = (h * 8 + qb) % 2
            dve.wait(fin["r"], last_pv[(h, qb)], o_free[h % 2] if qb == 0 else None)
            t_o = dve.mark(nc.vector.tensor_tensor(out=ost[h % 2][:, qb * 512:(qb + 1) * 512], in0=pO[ob][:],
                                                   in1=rD[ob][:], op=ALU.mult))
            pO_free[ob] = t_o
            rD_free[ob] = t_o
            if qb == 7:
                tk = kb.dma(sp, dr["obT"][h], ost[h % 2][:], o_st[h % 2], waits=(t_o,))
                o_free[h % 2] = tk
                final_toks.append(tk)

        FIN = ((1, fin_cast), (7, fin_dmm), (9, fin_act), (13, fin_mul))

        t_q[0] = t_q0
        for j, (h, qb, pj) in enumerate(items):
            if qb == 0 and pj == 4 and h + 1 < 8:
                load_q(h + 1)
            kv = h // 4
            ob = (h * 8 + qb) % 2
            bt = tick[0] % NB
            tick[0] += 1
            pe.wait(tok_const, t_q[h], sT_free[bt])
            for u in range(2):
                kt = 2 * pj + u
                ins = nc.tensor.matmul(psT[bt][:, u * 512:(u + 1) * 512], lhsT=kTs[:, kv, kt * 128:(kt + 1) * 128],
                                       rhs=qs[h % 2][:, qb * 512:(qb + 1) * 512], start=True, stop=True)
            t_s = pe.mark(ins)
            if qb == 7 and pj == 15:
                q_free[h % 2] = t_s
            act.wait(t_s, pT_free[j % NP])
            t_p = act.mark(nc.scalar.activation(out=pT[j % NP][:], in_=psT[bt][:], func=AF.Exp, scale=SCALE))
            sT_free[bt] = t_p
            t_pp = None
            if pj % 2 == 1:
                xi = (j // 2) % 2
                qi = npp[0] % 3
                npp[0] += 1
                dve.wait(prev_tp[0], t_p, xx_free[xi])
                t_pp = dve.mark(nc.vector.tensor_tensor(out=xx[xi][:], in0=pT[(j - 1) % NP][:], in1=pT[j % NP][:],
                                                        op=ALU.add))
                for k_, it_ in enumerate(pend):
                    if it_[0] == j - 1:
                        pend[k_] = it_[:5] + (t_pp,)
                dve.wait(t_pp, qd_free[qi])
                t_qd = dve.mark(nc.vector.tensor_tensor(out=qd[qi][:], in0=xx[xi][:, 0:512], in1=xx[xi][:, 512:1024],
                                                        op=ALU.add))
                xx_free[xi] = t_qd
                dve.wait(t_qd, racc_free[ob] if pj == 1 else last_acc.get((h, qb)))
                if pj == 1:
                    t_a = dve.mark(nc.vector.tensor_copy(out=racc[ob][:], in_=qd[qi][:]))
                else:
                    t_a = dve.mark(nc.vector.tensor_tensor(out=racc[ob][:], in0=racc[ob][:], in1=qd[qi][:], op=ALU.add))
                qd_free[qi] = t_a
                last_acc[(h, qb)] = t_a
            prev_tp[0] = t_p
            pend.append((j, h, qb, pj, t_p, t_pp))
            if len(pend) > 2:
                (j_, h_, qb_, pj_, tp_, tpp_) = pend.pop(0)
                t = do_pv(j_, h_, qb_, pj_, tp_)
                pT_free[j_ % NP] = (t, tpp_)
                if pj_ == 15:
                    pend_fin.append((h_, qb_))
            for (pjx, fn) in FIN:
                if pj == pjx and pend_fin:
                    fn(*pend_fin[0])
                    if fn is fin_mul:
                        pend_fin.pop(0)
        while pend:
            (j_, h_, qb_, pj_, tp_, tpp_) = pend.pop(0)
            t = do_pv(j_, h_, qb_, pj_, tp_)
            pT_free[j_ % NP] = (t, tpp_)
            if pj_ == 15:
                pend_fin.append((h_, qb_))
        while pend_fin:
            for (_, fn) in FIN:
                fn(*pend_fin[0])
            pend_fin.pop(0)
        _final_barrier(kb, final_toks)


def merge_weights(kb, dr, es):
    nc = kb.nc
    was = es.enter_context(nc.sbuf_tensor("p5was", [128, 4, D], BF16))
    wbs = es.enter_context(nc.sbuf_tensor("p5wbs", [128, 8, D], BF16))
    wos = es.enter_context(nc.sbuf_tensor("p5wos", [128, 8, D], BF16))
    dw = kb.dsem("p5w")
    kb.dma(kb.pool, was[:], dr["w_ba"].rearrange("(kc p) n -> p kc n", p=128), dw)
    kb.dma(kb.pool, wbs[:], dr["w_bb"].rearrange("(kc p) n -> p kc n", p=128), dw)
    kb.dma(kb.pool, wos[:], dr["w_out"].rearrange("(kc p) n -> p kc n", p=128), dw)
    return was, wbs, wos, dw.tok()


def merge_phase(kb, dr, pre):
    nc = kb.nc
    pe, act, dve, pool, sp = kb.pe, kb.act, kb.dve, kb.pool, kb.sp
    with ExitStack() as ph:
        def sb(name, shape, dt):
            return ph.enter_context(nc.sbuf_tensor("p5" + name, shape, dt))

        def pst(name, shape, dt):
            return ph.enter_context(nc.psum_tensor("p5" + name, shape, dt))

        was, wbs, wos, tok_w = pre
        oa = [sb(f"oa{i}", [128, 4, 512], BF16) for i in range(2)]
        ob = [sb(f"ob{i}", [128, 8, 512], BF16) for i in range(2)]
        gt = [sb(f"gt{i}", [128, 16, 512], BF16) for i in range(2)]
        mT = sb("mT", [128, 8, 512], BF16)
        ta = [sb(f"ta{i}", [128, 512], F32) for i in range(2)]
        tbb = [sb(f"tbb{i}", [128, 512], F32) for i in range(2)]
        xr = [sb(f"xr{i}", [128, D], F32) for i in range(3)]
        pA = [pst(f"pA{i}", [128, 512], F32) for i in range(2)]
        pB = [pst(f"pB{i}", [128, 512], F32) for i in range(2)]
        py = [pst(f"py{i}", [128, 512], F32) for i in range(2)]

        ld = [kb.dsem(f"p5ld{i}") for i in range(2)]
        in_free = [None, None]
        xr_ld = [kb.dsem(f"p5xl{i}") for i in range(3)]
        xr_st = [kb.dsem(f"p5xs{i}") for i in range(3)]
        xr_free = [None] * 3
        pA_free = [None] * 2
        pB_free = [None] * 2
        ta_free = [None] * 2
        tb_free = [None] * 2
        py_free = [None] * 2
        final_toks = []
        t_in = {}

        def load_in(i):
            b = i % 2
            tsl = slice(i * 512, (i + 1) * 512)
            kb.dma(sp, oa[b][:], dr["oaT"][:, :, tsl].rearrange("c p t -> p c t"), ld[b], waits=(in_free[b],))
            kb.dma(sp, ob[b][:], dr["obT"][:, :, tsl].rearrange("c p t -> p c t"), ld[b])
            t_in[i] = kb.dma(sp, gt[b][:], dr["gT"][:, :, tsl].rearrange("c p t -> p c t"), ld[b])

        load_in(0)
        cy = 0
        cx = 0
        mT_free = None
        for i in range(NT):
            if i + 1 < NT:
                load_in(i + 1)
            b = i % 2
            t_m = None
            for c in range(8):
                pb_ = c % 2
                pe.wait(tok_w, t_in[i], pA_free[pb_], pB_free[pb_])
                for kc in range(4):
                    ins = nc.tensor.matmul(pA[pb_][:], lhsT=was[:, kc, c * 128:(c + 1) * 128], rhs=oa[b][:, kc, :],
                                           start=(kc == 0), stop=(kc == 3))
                t_a = pe.mark(ins)
                for kc in range(8):
                    ins = nc.tensor.matmul(pB[pb_][:], lhsT=wbs[:, kc, c * 128:(c + 1) * 128], rhs=ob[b][:, kc, :],
                                           start=(kc == 0), stop=(kc == 7))
                t_b = pe.mark(ins)
                dve.wait(t_a, ta_free[pb_], t_in[i])
                t1_ = dve.mark(nc.vector.tensor_tensor(out=ta[pb_][:], in0=pA[pb_][:], in1=gt[b][:, c, :], op=ALU.mult))
                pA_free[pb_] = t1_
                dve.wait(t_b, tb_free[pb_])
                t2_ = dve.mark(nc.vector.tensor_tensor(out=tbb[pb_][:], in0=pB[pb_][:], in1=gt[b][:, 8 + c, :], op=ALU.mult))
                pB_free[pb_] = t2_
                pool.wait(t1_, t2_, mT_free if c == 0 else None)
                t_m = pool.mark(nc.gpsimd.tensor_tensor(out=mT[:, c, :], in0=ta[pb_][:], in1=tbb[pb_][:], op=ALU.add))
                ta_free[pb_] = t_m
                tb_free[pb_] = t_m
            t_lastmm = None
            for s in range(4):
                rb = cx % 3
                cx += 1
                r0 = i * 512 + s * 128
                tl = kb.dma(sp, xr[rb][:], dr["x1"][r0:r0 + 128, :], xr_ld[rb], waits=(xr_free[rb],))
                t_res = None
                for hf in range(2):
                    yb = cy % 2
                    cy += 1
                    pe.wait(t_m, py_free[yb])
                    for c in range(8):
                        ins = nc.tensor.matmul(py[yb][:], lhsT=mT[:, c, s * 128:(s + 1) * 128],
                                               rhs=wos[:, c, hf * 512:(hf + 1) * 512], start=(c == 0), stop=(c == 7))
                    t_y = pe.mark(ins)
                    t_lastmm = t_y
                    dve.wait(t_y, tl)
                    t_res = dve.mark(nc.vector.tensor_tensor(out=xr[rb][:, hf * 512:(hf + 1) * 512], in0=py[yb][:],
                                                             in1=xr[rb][:, hf * 512:(hf + 1) * 512], op=ALU.add))
                    py_free[yb] = t_res
                tst = kb.dma(sp, dr["x2"][r0:r0 + 128, :], xr[rb][:], xr_st[rb], waits=(t_res,))
                xr_free[rb] = tst
                final_toks.append(tst)
            mT_free = t_lastmm
            in_free[b] = t_lastmm
        _final_barrier(kb, final_toks)


SCR_PHASE = {"x1": 1, "hT": 1, "qTA": 2, "kTA": 2, "vA": 2, "qTB": 2, "kTB": 2, "vB": 2, "gT": 2,
             "oaT": 3, "obT": 4, "x2": 5}


def build_program(stop_after=99, dbg=False, start_at=1):
    kb = KB()
    nc = kb.nc

    def din(name, shape, dt=F32):
        return nc.dram_tensor(name, shape, dt, kind="ExternalInput").ap()

    def scr(name, shape, dt):
        kind = "ExternalOutput" if dbg else "Internal"
        if SCR_PHASE[name] < start_at:
            kind = "ExternalInput"
        return nc.dram_tensor(name, shape, dt, kind=kind).ap()

    dr = {}
    dr["x"] = din("x", [S, D])
    for p in ("ffn1", "ffn2"):
        dr[p + "_w1"] = din(p + "_w1", [D, DFF])
        dr[p + "_w3"] = din(p + "_w3", [D, DFF])
        dr[p + "_w2"] = din(p + "_w2", [DFF, D])
        dr[p + "_gbc"] = din(p + "_gbc", [128, KC, 128])
    dr["mix_gbc"] = din("mix_gbc", [128, KC, 128])
    dr["fin_bc"] = din("fin_bc", [128, D])
    dr["w_in"] = din("w_in", [D, 8192])
    dr["bg_col"] = din("bg_col", [128, 16])
    dr["qkg_col"] = din("qkg_col", [128, 2])
    dr["ropeC"] = din("ropeC", [128, S])
    dr["ropeS"] = din("ropeS", [128, S])
    dr["rotT"] = din("rotT", [128, 128])
    dr["ident"] = din("ident", [128, 128])
    dr["tabA"] = din("tabA", [24, 2, 128, 256])
    dr["w_ba"] = din("w_ba", [512, D])
    dr["w_bb"] = din("w_bb", [D, D])
    dr["w_out"] = din("w_out", [D, D])
    out = nc.dram_tensor("out", [S, D], F32, kind="ExternalOutput").ap()
    dr["x1"] = scr("x1", [S, D], F32)
    dr["hT"] = scr("hT", [KC, 128, S], BF16)
    dr["qTA"] = scr("qTA", [12, 128, S], BF16)
    dr["kTA"] = scr("kTA", [12, 128, S + 128], BF16)
    dr["vA"] = scr("vA", [99, 128, 768], BF16)
    dr["qTB"] = scr("qTB", [8, 128, S], BF16)
    dr["kTB"] = scr("kTB", [2, 128, S], BF16)
    dr["vB"] = scr("vB", [S, 256], BF16)
    dr["gT"] = scr("gT", [16, 128, S], BF16)
    dr["oaT"] = scr("oaT", [4, 128, S], BF16)
    dr["obT"] = scr("obT", [8, 128, S], BF16)
    dr["x2"] = scr("x2", [S, D], F32)

    with kb.es:
        if start_at <= 1:
            ffn_phase(kb, "f1", dr["x"], dr["ffn1_gbc"], dr["ffn1_w1"], dr["ffn1_w3"], dr["ffn1_w2"], dr["ident"],
                      "ffn1", x_dst=dr["x1"], g2bc_d=dr["mix_gbc"], hT_dst=dr["hT"])
        if start_at <= 2 <= stop_after:
            proj_phase(kb, dr)
        with ExitStack() as w4:
            pre4 = attn_b_consts(kb, dr, w4) if (start_at <= 4 <= stop_after) else None
            if start_at <= 3 <= stop_after:
                attn_a_phase(kb, dr)
            with ExitStack() as w5:
                pre5 = merge_weights(kb, dr, w5) if (start_at <= 5 <= stop_after) else None
                if start_at <= 4 <= stop_after:
                    attn_b_phase(kb, dr, pre4)
                if start_at <= 5 <= stop_after:
                    merge_phase(kb, dr, pre5)
        if start_at <= 6 <= stop_after:
            ffn_phase(kb, "f2", dr["x2"], dr["ffn2_gbc"], dr["ffn2_w1"], dr["ffn2_w3"], dr["ffn2_w2"], dr["ident"],
                      "final", x_dst=out, fin_d=dr["fin_bc"])
    return nc


def _gbc(g):
    return np.ascontiguousarray(np.broadcast_to(g.reshape(KC, 128).T[:, :, None], (128, KC, 128))).astype(np.float32)


def make_in_maps(inp):
    f = np.float32
    C, Sn = rope_tables()
    shared = {
        "ffn1_w1": inp["ffn1_w1"][0], "ffn1_w3": inp["ffn1_w3"][0], "ffn1_w2": inp["ffn1_w2"][0],
        "ffn2_w1": inp["ffn2_w1"][0], "ffn2_w3": inp["ffn2_w3"][0], "ffn2_w2": inp["ffn2_w2"][0],
        "ffn1_gbc": _gbc(inp["ffn1_norm"][0]), "ffn2_gbc": _gbc(inp["ffn2_norm"][0]),
        "mix_gbc": _gbc(inp["mix_norm"][0]),
        "fin_bc": np.ascontiguousarray(np.broadcast_to(inp["final_norm"][None, :], (128, D))).astype(f),
        "w_in": inp["w_in"][0],
        "bg_col": np.ascontiguousarray(inp["b_gate"][0].reshape(16, 128).T).astype(f),
        "qkg_col": np.ascontiguousarray(np.stack([inp["q_norm"][0], inp["k_norm"][0]], axis=1)).astype(f),
        "ropeC": C, "ropeS": Sn, "rotT": rot_lhsT(), "ident": np.eye(128, dtype=f),
        "tabA": host_tables(np.asarray(inp["rel_bias"])),
        "w_ba": inp["w_branch_a"][0], "w_bb": inp["w_branch_b"][0], "w_out": inp["w_out"][0],
    }
    shared = {k: np.ascontiguousarray(v, dtype=f) for k, v in shared.items()}
    maps = []
    for b in range(8):
        m = dict(shared)
        m["x"] = np.ascontiguousarray(inp["x"][b], dtype=f)
        maps.append(m)
    return maps


def kernel(**inputs):
    inp = {k: np.asarray(v) for k, v in inputs.items()}
    nc = build_program()
    res = run_bass_kernel_spmd(nc, make_in_maps(inp), core_ids=list(range(8)))
    return np.stack([np.asarray(r["out"]) for r in res.results], axis=0).astype(np.float32)
```

```python
import numpy as np
from contextlib import ExitStack
import concourse.bass as bass
import concourse.mybir as mybir
from concourse.bass_utils import run_bass_kernel_spmd

F32 = mybir.dt.float32
BF16 = mybir.dt.bfloat16
AF = mybir.ActivationFunctionType
ALU = mybir.AluOpType

S = 4096
D = 1024
DFF = 2816
NCH = DFF // 128
KC = D // 128
NT = S // 512
EPS = 1e-6
GROUPS = ((128, 1), (512, 4), (2048, 16))
WBLK = ((0, 6), (6, 12), (12, 17), (17, 22))


class EngQ:
    def __init__(self, kb, eng, name):
        self.eng = eng
        self.sem = kb.sem("q_" + name)
        self.n = 0
        self.waited = {}

    def wait(self, *toks):
        for t in toks:
            if t is None:
                continue
            if isinstance(t, (list, tuple)) and not (len(t) == 2 and isinstance(t[1], int)):
                self.wait(*t)
                continue
            sem, val = t
            key = id(sem)
            if self.waited.get(key, 0) >= val:
                continue
            self.eng.wait_ge(sem, val)
            self.waited[key] = val

    def mark(self, inst):
        self.n += 1
        inst.then_inc(self.sem, 1)
        return (self.sem, self.n)


class DSem:
    def __init__(self, sem):
        self.sem = sem
        self.val = 0

    def tok(self):
        return (self.sem, self.val)


class KB:
    def __init__(self):
        self.nc = bass.Bass("TRN2", target_bir_lowering=False)
        self.es = ExitStack()
        nc = self.nc
        self.pe = EngQ(self, nc.tensor, "pe")
        self.act = EngQ(self, nc.scalar, "act")
        self.dve = EngQ(self, nc.vector, "dve")
        self.pool = EngQ(self, nc.gpsimd, "pool")
        self.sp = EngQ(self, nc.sync, "sp")
        self.engs = [self.pe, self.act, self.dve, self.pool, self.sp]
        self._ds = {}

    def sem(self, name):
        return self.es.enter_context(self.nc.semaphore(name))

    def dsem(self, name):
        if name not in self._ds:
            self._ds[name] = DSem(self.sem("d_" + name))
        return self._ds[name]

    def dma(self, q, out, in_, ds, waits=()):
        q.wait(*waits)
        inst = q.eng.dma_start(out=out, in_=in_)
        inst.then_inc(ds.sem, 16)
        ds.val += 16
        return (ds.sem, ds.val)

    def barrier(self, toks):
        for e in self.engs:
            e.wait(*toks)


def perm_view(ap3, d):
    return ap3.rearrange("p (m r) -> p r m", r=d)


def ffn_phase(kb, tag, x_src, gbc_d, w1d, w3d, w2d, ident_d, mode, x_dst=None, g2bc_d=None,
              hT_dst=None, fin_d=None):
    nc = kb.nc
    pe, act, dve, pool, sp = kb.pe, kb.act, kb.dve, kb.pool, kb.sp
    with ExitStack() as ph:
        def sb(name, shape, dt):
            return ph.enter_context(nc.sbuf_tensor(tag + name, shape, dt))

        def pst(name, shape, dt):
            return ph.enter_context(nc.psum_tensor(tag + name, shape, dt))

        w1s = sb("w1s", [128, KC, DFF], BF16)
        w3s = sb("w3s", [128, KC, DFF], BF16)
        w2s = sb("w2s", [128, NCH, D], BF16)
        gbc = sb("gbc", [128, KC, 128], F32)
        ident = sb("ident", [128, 128], BF16)
        xin = [sb(f"xin{i}", [128, D], F32) for i in range(2)]
        xr = [sb(f"xr{i}", [128, D], F32) for i in range(2)]
        xn = [sb(f"xn{i}", [128, D], BF16) for i in range(4)]
        xnT = sb("xnT", [128, KC, 512], BF16)
        gT = sb("gT", [128, NCH, 512], BF16)
        sl = [sb(f"sl{i}", [128, 512], F32) for i in range(2)]
        ssx = sb("ssx", [128, 8], F32)
        epsc = sb("epsc", [128, 1], F32)
        t_eps = dve.mark(nc.vector.memset(epsc[:], EPS))
        if mode == "ffn1":
            g2bc = sb("g2bc", [128, KC, 128], F32)
            xn2 = [sb(f"xn2{i}", [128, D], BF16) for i in range(2)]
            hst = sb("hst", [128, KC, 256], BF16)
            ss2 = sb("ss2", [128, 8], F32)
        else:
            finbc = sb("finbc", [128, D], F32)
            ss2 = sb("ss2", [128, 8], F32)
            junkb = [sb(f"junkb{i}", [128, D], BF16) for i in range(2)]
        pa = [pst(f"pa{i}", [128, 512], F32) for i in range(2)]
        pb = [pst(f"pb{i}", [128, 512], F32) for i in range(2)]
        py = [pst(f"py{i}", [128, 512], F32) for i in range(2)]
        ptp = [pst(f"ptp{i}", [128, KC, 128], BF16) for i in range(2)]

        dc = kb.dsem(tag + "const")
        kb.dma(sp, gbc[:], gbc_d, dc)
        dcp = kb.dsem(tag + "constp")
        kb.dma(pool, ident[:], ident_d, dcp)
        if mode == "ffn1":
            kb.dma(sp, g2bc[:], g2bc_d, dc)
        else:
            kb.dma(sp, finbc[:], fin_d, dc)
        tok_const = (dc.tok(), dcp.tok())
        w1v = w1d.rearrange("(kc p) n -> p kc n", p=128)
        w3v = w3d.rearrange("(kc p) n -> p kc n", p=128)
        w2v = w2d.rearrange("(c p) n -> p c n", p=128)
        tok_w13 = []
        tok_w2 = []
        for bi, (c0, c1) in enumerate(WBLK):
            ds = kb.dsem(tag + f"w13_{bi}")
            kb.dma(pool, w1s[:, :, c0 * 128:c1 * 128], w1v[:, :, c0 * 128:c1 * 128], ds)
            kb.dma(pool, w3s[:, :, c0 * 128:c1 * 128], w3v[:, :, c0 * 128:c1 * 128], ds)
            tok_w13.append(ds.tok())
        for bi, (c0, c1) in enumerate(WBLK):
            ds = kb.dsem(tag + f"w2_{bi}")
            kb.dma(pool, w2s[:, c0:c1, :], w2v[:, c0:c1, :], ds)
            tok_w2.append(ds.tok())

        def wblk_of(c):
            for bi, (c0, c1) in enumerate(WBLK):
                if c0 <= c < c1:
                    return bi

        def rstd_chain(src, junk, ssap, waits):
            act.wait(waits, t_eps)
            t = act.mark(nc.scalar.activation(out=junk, in_=src, func=AF.Square, accum_out=ssap))
            act.wait(t)
            t = act.mark(nc.scalar.activation(out=ssap, in_=ssap, func=AF.Ln, scale=1.0 / D, bias=epsc[:, 0:1]))
            act.wait(t)
            t = act.mark(nc.scalar.activation(out=ssap, in_=ssap, func=AF.Exp, scale=-0.5))
            return t

        xin_ld = [kb.dsem(tag + f"xin_ld{i}") for i in range(2)]
        xin_free = [None, None]
        xn_free = [None] * 4
        xr_ld = [kb.dsem(tag + f"xr_ld{i}") for i in range(2)]
        xr_st = [kb.dsem(tag + f"xr_st{i}") for i in range(2)]
        xr_free = [None, None]
        ptp_free = [None, None]
        pa_free = [None, None]
        pb_free = [None, None]
        sl_free = [None, None]
        py_free = [None, None]
        st = {"xnT_ready": None, "xnT_parts": [], "hst_free": None, "ctr_xin": 0, "ctr_tp": 0,
              "ctr_xr": 0, "ctr_y": 0, "ctr_xn2": 0}
        hst_ds = kb.dsem(tag + "hst_st") if mode == "ffn1" else None
        xn2_free = [None, None]
        pending_h = []
        final_toks = []

        def norm_chain(i, s):
            k = st["ctr_xin"]; st["ctr_xin"] += 1
            b = k % 2
            r0 = i * 512 + s * 128
            tl = kb.dma(sp, xin[b][:], x_src[r0:r0 + 128, :], xin_ld[b], waits=(xin_free[b],))
            t_r = rstd_chain(xin[b][:], xn[s][:], ssx[:, s:s + 1], (tl, xn_free[s]))
            dve.wait(t_r)
            t_xn = dve.mark(nc.vector.tensor_scalar(out=xn[s][:], in0=xin[b][:], scalar1=ssx[:, s:s + 1],
                                                    scalar2=None, op0=ALU.mult))
            xin_free[b] = t_xn
            st["t_xn", s] = t_xn

        def norm_tp(i, s):
            t_xn = st["t_xn", s]
            j = st["ctr_tp"]; st["ctr_tp"] += 1
            pbk = j % 2
            pe.wait(t_xn, ptp_free[pbk], tok_const)
            for kc in range(KC):
                ins = nc.tensor.transpose(ptp[pbk][:, kc, :], xn[s][:, kc * 128:(kc + 1) * 128], ident[:])
            t_tp = pe.mark(ins)
            xn_free[s] = t_tp
            dve.wait(t_tp, tok_const)
            t_ev = dve.mark(nc.vector.tensor_tensor(out=xnT[:, :, s * 128:(s + 1) * 128], in0=ptp[pbk][:],
                                                    in1=gbc[:], op=ALU.mult))
            ptp_free[pbk] = t_ev
            return t_ev

        def h_transposes():
            while pending_h:
                (bi2, t_rdy, ti, s) = pending_h.pop(0)
                j = st["ctr_tp"]; st["ctr_tp"] += 1
                pbk = j % 2
                pe.wait(t_rdy, ptp_free[pbk])
                for kc in range(KC):
                    ins = nc.tensor.transpose(ptp[pbk][:, kc, :], xn2[bi2][:, kc * 128:(kc + 1) * 128], ident[:])
                t_tp = pe.mark(ins)
                xn2_free[bi2] = t_tp
                waits = [t_tp]
                if s % 2 == 0:
                    waits.append(st["hst_free"])
                dve.wait(*waits)
                s2 = s % 2
                t_ev = dve.mark(nc.vector.tensor_tensor(out=hst[:, :, s2 * 128:(s2 + 1) * 128], in0=ptp[pbk][:],
                                                        in1=g2bc[:], op=ALU.mult))
                ptp_free[pbk] = t_ev
                if s2 == 1:
                    c0 = ti * 512 + (s // 2) * 256
                    tk = kb.dma(sp, hT_dst[:, :, c0:c0 + 256].rearrange("kc p t -> p kc t"), hst[:],
                                hst_ds, waits=(t_ev,))
                    st["hst_free"] = tk
                    final_toks.append(tk)

        def h_stage(i, t_xnT):
            toks = []
            for c in range(NCH):
                if i + 1 < NT and c in (3, 8, 13, 18):
                    norm_chain(i + 1, (3, 8, 13, 18).index(c))
                b = c % 2
                pe.wait(t_xnT, tok_w13[wblk_of(c)], pa_free[b], pb_free[b])
                for kc in range(KC):
                    ins = nc.tensor.matmul(pa[b][:], lhsT=w1s[:, kc, c * 128:(c + 1) * 128], rhs=xnT[:, kc, :],
                                           start=(kc == 0), stop=(kc == KC - 1))
                t_a = pe.mark(ins)
                for kc in range(KC):
                    ins = nc.tensor.matmul(pb[b][:], lhsT=w3s[:, kc, c * 128:(c + 1) * 128], rhs=xnT[:, kc, :],
                                           start=(kc == 0), stop=(kc == KC - 1))
                t_b = pe.mark(ins)
                act.wait(t_a, sl_free[b])
                t_s = act.mark(nc.scalar.activation(out=sl[b][:], in_=pa[b][:], func=AF.Silu))
                pa_free[b] = t_s
                dve.wait(t_s, t_b)
                t_g = dve.mark(nc.vector.tensor_tensor(out=gT[:, c, :], in0=sl[b][:], in1=pb[b][:], op=ALU.mult))
                pb_free[b] = t_g
                sl_free[b] = t_g
                toks.append(t_g)
            return toks[-1]

        def y_stage(i, t_g):
            for s in range(4):
                k = st["ctr_xr"]; st["ctr_xr"] += 1
                rb = k % 2
                r0 = i * 512 + s * 128
                tl = kb.dma(sp, xr[rb][:], x_src[r0:r0 + 128, :], xr_ld[rb], waits=(xr_free[rb],))
                t_res = None
                for hf in range(2):
                    j = st["ctr_y"]; st["ctr_y"] += 1
                    yb = j % 2
                    pe.wait(t_g, py_free[yb], *tok_w2)
                    for c in range(NCH):
                        ins = nc.tensor.matmul(py[yb][:], lhsT=gT[:, c, s * 128:(s + 1) * 128],
                                               rhs=w2s[:, c, hf * 512:(hf + 1) * 512],
                                               start=(c == 0), stop=(c == NCH - 1))
                    t_y = pe.mark(ins)
                    dve.wait(t_y, tl)
                    t_res = dve.mark(nc.vector.scalar_tensor_tensor(
                        out=xr[rb][:, hf * 512:(hf + 1) * 512], in0=py[yb][:], scalar=0.5,
                        in1=xr[rb][:, hf * 512:(hf + 1) * 512], op0=ALU.mult, op1=ALU.add))
                    py_free[yb] = t_res
                if mode == "ffn1":
                    tst = kb.dma(sp, x_dst[r0:r0 + 128, :], xr[rb][:], xr_st[rb], waits=(t_res,))
                    final_toks.append(tst)
                    k2 = st["ctr_xn2"]; st["ctr_xn2"] += 1
                    b2 = k2 % 2
                    t_r = rstd_chain(xr[rb][:], xn2[b2][:], ss2[:, b2:b2 + 1], (t_res, xn2_free[b2]))
                    dve.wait(t_r)
                    t_x2 = dve.mark(nc.vector.tensor_scalar(out=xn2[b2][:], in0=xr[rb][:], scalar1=ss2[:, b2:b2 + 1],
                                                            scalar2=None, op0=ALU.mult))
                    xr_free[rb] = (t_x2, tst)
                    pending_h.append((b2, t_x2, i, s))
                    if len(pending_h) > 1:
                        keep = pending_h.pop()
                        h_transposes()
                        pending_h.append(keep)
                else:
                    jb = k % 2
                    t_r = rstd_chain(xr[rb][:], junkb[jb][:], ss2[:, rb:rb + 1], (t_res,))
                    dve.wait(t_r, tok_const)
                    t_o = dve.mark(nc.vector.scalar_tensor_tensor(
                        out=xr[rb][:], in0=xr[rb][:], scalar=ss2[:, rb:rb + 1], in1=finbc[:],
                        op0=ALU.mult, op1=ALU.mult))
                    tst = kb.dma(sp, x_dst[r0:r0 + 128, :], xr[rb][:], xr_st[rb], waits=(t_o,))
                    final_toks.append(tst)
                    xr_free[rb] = (tst,)
            return

        for s_ in range(4):
            norm_chain(0, s_)
        for s_ in range(4):
            t_xnT = norm_tp(0, s_)
        for i in range(NT):
            t_g = h_stage(i, t_xnT)
            if i + 1 < NT:
                for s_ in range(4):
                    t_xnT = norm_tp(i + 1, s_)
            if mode == "ffn1":
                h_transposes()
            y_stage(i, t_g)
        if mode == "ffn1":
            h_transposes()
        last = {}
        for t in final_toks:
            last[id(t[0])] = t if (id(t[0]) not in last or last[id(t[0])][1] < t[1]) else last[id(t[0])]
        kb.barrier(list(last.values()))


def _t5_bucket_np(rel):
    n = 16
    max_exact = 8
    ret = np.where(rel > 0, n, 0)
    a = np.abs(rel)
    af = np.maximum(a, 1).astype(np.float32)
    large = max_exact + (np.log(af / np.float32(max_exact)) / np.float32(np.log(1024 / max_exact))
                         * np.float32(n - max_exact)).astype(np.int32)
    large = np.minimum(large, n - 1)
    return ret + np.where(a < max_exact, a, large)


def host_tables(rel_bias):
    NEG = np.float32(-30000.0)
    row = np.arange(128)[:, None]
    col = np.arange(256)[None, :]
    rel = np.where(col < 128, 64 + row - col, row - 64 - (col - 128))
    valid_int = np.abs(rel) <= 64
    bnd_ok = np.where(col < 128, row < 64, row >= 64)
    tab = np.empty((24, 2, 128, 256), np.float32)
    for g, (_, d) in enumerate(GROUPS):
        bucket = _t5_bucket_np((rel * d).astype(np.int32))
        for h in range(8):
            bias = rel_bias[bucket, g * 8 + h].astype(np.float32)
            tab[g * 8 + h, 0] = np.where(valid_int, bias, NEG)
            tab[g * 8 + h, 1] = np.where(valid_int & bnd_ok, bias, NEG)
    return tab


def rope_tables():
    t = np.arange(S)
    rowi = (t // 64).astype(np.float32)
    coli = (t % 64).astype(np.float32)
    nf = 32
    freq = (np.float32(10000.0) ** (-np.arange(nf, dtype=np.float32) / np.float32(nf))).astype(np.float32)
    ang = np.concatenate([rowi[:, None] * freq, coli[:, None] * freq], axis=-1).astype(np.float32)
    c = np.cos(ang).astype(np.float32)
    s = np.sin(ang).astype(np.float32)
    C = np.repeat(c.T, 2, axis=0)
    Sn = np.repeat(s.T, 2, axis=0)
    return np.ascontiguousarray(C), np.ascontiguousarray(Sn)


def rot_lhsT():
    m = np.zeros((128, 128), np.float32)
    for i in range(64):
        m[2 * i + 1, 2 * i] = -1.0
        m[2 * i, 2 * i + 1] = 1.0
    return m


P2PARTS = ("bqk", "bv", "gate", "aqk", "av")
AVG = (0, 1, 2)
AVKT = range(33)
AVSIMPLE = 0
AVNOSTORE = 0
AVALIGN = 0
A_Q0, A_K0, A_V0 = 0, 1536, 3072
B_Q0, B_K0, B_V0, G0 = 4608, 5632, 5888, 6144


def window_pieces(g, kt):
    _, d = GROUPS[g]
    L = S // d
    halves = []
    for j in (2 * kt - 1, 2 * kt):
        if j < 0 or j >= S // 64:
            halves.append(None)
        else:
            r, m0 = divmod(64 * j, L)
            halves.append((m0 * d + r, r, m0))
    h0, h1 = halves
    if h0 is not None and h1 is not None and h0[1] == h1[1]:
        return [(0, 128, h0[0], d)]
    out = []
    for i, h in enumerate(halves):
        out.append((64 * i, 64, None if h is None else h[0], d))
    return out


def proj_phase(kb, dr):
    nc = kb.nc
    pe, act, dve, pool, sp = kb.pe, kb.act, kb.dve, kb.pool, kb.sp
    with ExitStack() as ph:
        def sb(name, shape, dt):
            return ph.enter_context(nc.sbuf_tensor("p2" + name, shape, dt))

        def pst(name, shape, dt):
            return ph.enter_context(nc.psum_tensor("p2" + name, shape, dt))

        hTs = sb("hTs", [128, KC, S + 64], BF16)
        ropeC = sb("ropeC", [128, S], F32)
        ropeS = sb("ropeS", [128, S], F32)
        ones_bf = sb("ones", [128, 128], BF16)
        rotT = sb("rotT", [128, 128], BF16)
        bg = sb("bg", [128, 16], F32)
        qkg = sb("qkg", [128, 2], F32)
        epsc = sb("epsc", [128, 1], F32)
        wc = [sb(f"wc{i}", [128, KC, 128], BF16) for i in range(3)]
        wv = [sb(f"wv{i}", [128, KC, 512], BF16) for i in range(2)]
        stg = [sb(f"stg{i}", [128, S + 128], BF16) for i in range(2)]
        vstgB = sb("vstgB", [128, 32, 256], BF16)
        vstgA = [sb(f"vstgA{i}", [128, 768], BF16) for i in range(3)]
        sq = [sb(f"sq{i}", [128, 512], BF16) for i in range(4)]
        t1 = [sb(f"t1{i}", [128, 512], F32) for i in range(4)]
        t2 = [sb(f"t2{i}", [128, 512], F32) for i in range(4)]
        t3 = [sb(f"t3{i}", [128, 512], F32) for i in range(4)]
        qnb = [sb(f"qnb{i}", [128, 512], BF16) for i in range(4)]
        bank = [pst(f"bk{i}", [128, 512], F32) for i in range(8)]
        pm = bank[0:2]
        pv = bank[2:4]

        dc = kb.dsem("p2const")
        hT_tok = []
        for blk in range(8):
            dsb_ = kb.dsem(f"p2hT{blk}")
            hT_tok.append(kb.dma(sp, hTs[:, :, blk * 512:(blk + 1) * 512],
                                 dr["hT"][:, :, blk * 512:(blk + 1) * 512].rearrange("kc p t -> p kc t"), dsb_))
        kb.dma(sp, bg[:], dr["bg_col"], dc)
        kb.dma(sp, qkg[:], dr["qkg_col"], dc)
        drope = kb.dsem("p2rope")
        kb.dma(sp, ropeC[:], dr["ropeC"], drope)
        kb.dma(sp, ropeS[:], dr["ropeS"], drope)
        tok_rope = drope.tok()
        dcp = kb.dsem("p2constp")
        kb.dma(pool, rotT[:], dr["rotT"], dcp)
        tok_const = (dc.tok(), dcp.tok())
        t_m0 = pool.mark(nc.gpsimd.memset(ones_bf[:], 1.0))
        nc.gpsimd.memset(epsc[:], EPS)
        t_m1 = pool.mark(nc.gpsimd.memset(hTs[:, :, S:S + 64], 0.0))
        toks_small = [tok_const, t_m0, t_m1]
        for i in range(3):
            toks_small.append(pool.mark(nc.gpsimd.memset(vstgA[i][:], 1.0)))
        toks_init = toks_small + hT_tok

        w_in_v = dr["w_in"].rearrange("(kc p) n -> p kc n", p=128)
        wc_ld = [kb.dsem(f"p2wc{i}") for i in range(3)]
        wc_free = [None] * 3
        wv_ld = [kb.dsem(f"p2wv{i}") for i in range(2)]
        wv_free = [None] * 2
        stg_st = [kb.dsem(f"p2stg{i}") for i in range(2)]
        stg_free = [None] * 2
        vstgA_st = [kb.dsem(f"p2vsa{i}") for i in range(3)]
        vstgA_free = [None] * 3
        bank_free = [None] * 8

        class _View:
            def __init__(self, off):
                self.off = off

            def __getitem__(self, i):
                return bank_free[self.off + i]

            def __setitem__(self, i, v):
                bank_free[self.off + i] = v
        pm_free = _View(0)
        pv_free = _View(2)
        sq_free = [None] * 4
        t1_free = [None] * 4
        t2_free = [None] * 4
        t3_free = [None] * 4
        qnb_free = [None] * 4
        ctr = {"wc": 0, "stg": 0, "pm": 0, "wv": 0, "pv": 0, "vsa": 0, "b": 0, "ev": 0}
        final_toks = []

        def tok_rhs(g, kc, blk):
            _, d = GROUPS[g]
            base = hTs[:, kc, 0:S]
            if d == 1:
                return base[:, blk * 512:(blk + 1) * 512], None
            v = perm_view(base, d)
            if d == 4:
                return v[:, blk // 2, (blk % 2) * 512:(blk % 2) * 512 + 512], None
            return v[:, 2 * blk:2 * blk + 2, :], 2

        def fm_chunk(col0, kind, g, dst, arg=None):
            k = ctr["wc"]; ctr["wc"] += 1
            ws = k % 3
            t_w = kb.dma(pool, wc[ws][:], w_in_v[:, :, col0:col0 + 128], wc_ld[ws], waits=(wc_free[ws],))
            ks = ctr["stg"]; ctr["stg"] += 1
            ss = ks % 2
            off = 64 if kind == "ak" else 0
            evs = {}
            if kind == "ak":
                dve.wait(stg_free[ss])
                nc.vector.memset(stg[ss][:, 0:64], 0.0)
                evs["pad"] = dve.mark(nc.vector.memset(stg[ss][:, S + 64:S + 128], 0.0))
            for blk in range(8):
                j = ctr["pm"]; ctr["pm"] += 1
                pb_ = j % 2
                pe.wait(t_w, toks_init, pm_free[pb_])
                for kc in range(KC):
                    rhs, two = tok_rhs(g, kc, blk)
                    o = pm[pb_][:]
                    if two:
                        o = o.rearrange("p (a b) -> p a b", a=2)
                    ins = nc.tensor.matmul(o, lhsT=wc[ws][:, kc, :], rhs=rhs, start=(kc == 0), stop=(kc == KC - 1))
                t_mm = pe.mark(ins)
                if blk == 7:
                    wc_free[ws] = t_mm
                dstap = stg[ss][:, off + blk * 512: off + (blk + 1) * 512]
                if kind in ("aq", "ak"):
                    e = ctr["ev"]; ctr["ev"] += 1
                    if e % 2 == 0:
                        dve.wait(t_mm, stg_free[ss])
                        t_ev = dve.mark(nc.vector.tensor_copy(out=dstap, in_=pm[pb_][:]))
                        evs["dve"] = t_ev
                    else:
                        act.wait(t_mm, stg_free[ss])
                        t_ev = act.mark(nc.scalar.activation(out=dstap, in_=pm[pb_][:], func=AF.Copy))
                        evs["act"] = t_ev
                    pm_free[pb_] = t_ev
                elif kind == "gate":
                    act.wait(t_mm, toks_init, stg_free[ss])
                    t_ev = act.mark(nc.scalar.activation(out=dstap, in_=pm[pb_][:], func=AF.Sigmoid,
                                                         bias=bg[:, arg:arg + 1]))
                    pm_free[pb_] = t_ev
                    evs["act"] = t_ev
            width = S + 128 if kind == "ak" else S
            tk = kb.dma(sp, dst, stg[ss][:, 0:width], stg_st[ss], waits=list(evs.values()))
            stg_free[ss] = tk
            final_toks.append(tk)


        def bqk_pipeline():
            chunks = [(B_K0 + j * 128, 1, dr["kTB"][j]) for j in range(2)] + \
                     [(B_Q0 + h * 128, 0, dr["qTB"][h]) for h in range(8)]
            N = len(chunks) * 8
            T = {}
            wtok = {}

            def load_w(c):
                ws = c % 3
                col0 = chunks[c][0]
                wtok[c] = kb.dma(pool, wc[ws][:], w_in_v[:, :, col0:col0 + 128], wc_ld[ws], waits=(wc_free[ws],))

            def S0(i):
                c, blk = divmod(i, 8)
                if blk == 0 and c + 1 < len(chunks):
                    load_w(c + 1)
                pe.wait(wtok[c], toks_small, hT_tok[blk], bank_free[i % 4])
                for kc in range(KC):
                    ins = nc.tensor.matmul(bank[i % 4][:], lhsT=wc[c % 3][:, kc, :],
                                           rhs=hTs[:, kc, blk * 512:(blk + 1) * 512],
                                           start=(kc == 0), stop=(kc == KC - 1))
                T["mm", i] = pe.mark(ins)
                if blk == 7:
                    wc_free[c % 3] = T["mm", i]

            def S1(i):
                act.wait(T["mm", i], sq_free[i % 4])
                T["sq", i] = act.mark(nc.scalar.activation(out=sq[i % 4][:], in_=bank[i % 4][:], func=AF.Square))

            def S2(i):
                pe.wait(T["sq", i], bank_free[4 + i % 2])
                T["ss", i] = pe.mark(nc.tensor.matmul(bank[4 + i % 2][:], lhsT=ones_bf[:], rhs=sq[i % 4][:],
                                                      start=True, stop=True))
                sq_free[i % 4] = T["ss", i]

            def S3(i):
                act.wait(T["ss", i], t1_free[i % 4], toks_init)
                tl = act.mark(nc.scalar.activation(out=t1[i % 4][:], in_=bank[4 + i % 2][:], func=AF.Ln,
                                                   scale=1.0 / 128, bias=epsc[:, 0:1]))
                bank_free[4 + i % 2] = tl
                act.wait(tl)
                T["rs", i] = act.mark(nc.scalar.activation(out=t1[i % 4][:], in_=t1[i % 4][:], func=AF.Exp, scale=-0.5))

            def S5(i):
                c = i // 8
                dve.wait(T["rs", i], t2_free[i % 4], toks_init)
                T["qn", i] = dve.mark(nc.vector.scalar_tensor_tensor(
                    out=t2[i % 4][:], in0=bank[i % 4][:], scalar=qkg[:, chunks[c][1]:chunks[c][1] + 1],
                    in1=t1[i % 4][:], op0=ALU.mult, op1=ALU.mult))
                bank_free[i % 4] = T["qn", i]
                t1_free[i % 4] = T["qn", i]

            def S6(i):
                act.wait(T["qn", i], qnb_free[i % 4])
                T["qb", i] = act.mark(nc.scalar.activation(out=qnb[i % 4][:], in_=t2[i % 4][:], func=AF.Copy))

            def S7(i):
                pe.wait(T["qb", i], bank_free[6 + i % 2])
                T["rot", i] = pe.mark(nc.tensor.matmul(bank[6 + i % 2][:], lhsT=rotT[:], rhs=qnb[i % 4][:],
                                                       start=True, stop=True))
                qnb_free[i % 4] = T["rot", i]

            def S8(i):
                blk = i % 8
                tsl = slice(blk * 512, (blk + 1) * 512)
                dve.wait(T["qb", i], T["qn", i], tok_rope)
                T["c", i] = dve.mark(nc.vector.tensor_tensor(out=t2[i % 4][:], in0=t2[i % 4][:], in1=ropeC[:, tsl],
                                                             op=ALU.mult))
                dve.wait(T["rot", i], t3_free[i % 4])
                T["s", i] = dve.mark(nc.vector.tensor_tensor(out=t3[i % 4][:], in0=bank[6 + i % 2][:],
                                                             in1=ropeS[:, tsl], op=ALU.mult))
                bank_free[6 + i % 2] = T["s", i]

            def S10(i):
                c, blk = divmod(i, 8)
                ss = c % 2
                pool.wait(T["c", i], T["s", i], stg_free[ss])
                T["o", i] = pool.mark(nc.gpsimd.tensor_tensor(out=stg[ss][:, blk * 512:(blk + 1) * 512],
                                                              in0=t2[i % 4][:], in1=t3[i % 4][:], op=ALU.add))
                t2_free[i % 4] = T["o", i]
                t3_free[i % 4] = T["o", i]
                if blk == 7:
                    tk = kb.dma(sp, chunks[c][2], stg[ss][:, 0:S], stg_st[ss], waits=(T["o", i],))
                    stg_free[ss] = tk
                    final_toks.append(tk)

            load_w(0)
            for step in range(N + 5):
                if step < N:
                    S0(step)
                    S1(step)
                if 0 <= step - 1 < N:
                    S2(step - 1)
                    S3(step - 1)
                if 0 <= step - 2 < N:
                    S5(step - 2)
                    S6(step - 2)
                if 0 <= step - 3 < N:
                    S7(step - 3)
                    S8(step - 3)
                if 0 <= step - 4 < N:
                    S10(step - 4)
            ctr["wc"] = len(chunks)
            ctr["stg"] = len(chunks)
            ctr["pm"] = 0

        if "bqk" in P2PARTS:
            bqk_pipeline()
        t_wv = kb.dma(pool, wv[0][:, :, 0:256], w_in_v[:, :, B_V0:B_V0 + 256], wv_ld[0])
        ctr["wv"] = 1
        evb = {}
        for tt in range(32 if "bv" in P2PARTS else 0):
            j = ctr["pv"]; ctr["pv"] += 1
            pb_ = j % 2
            pe.wait(t_wv, toks_init, pv_free[pb_])
            for kc in range(KC):
                ins = nc.tensor.matmul(pv[pb_][:, 0:256], lhsT=hTs[:, kc, tt * 128:(tt + 1) * 128],
                                       rhs=wv[0][:, kc, 0:256], start=(kc == 0), stop=(kc == KC - 1))
            t_mm = pe.mark(ins)
            if tt % 2 == 0:
                dve.wait(t_mm)
                t_ev = dve.mark(nc.vector.tensor_copy(out=vstgB[:, tt, :], in_=pv[pb_][:, 0:256]))
                evb["dve"] = t_ev
            else:
                act.wait(t_mm)
                t_ev = act.mark(nc.scalar.activation(out=vstgB[:, tt, :], in_=pv[pb_][:, 0:256], func=AF.Copy))
                evb["act"] = t_ev
            pv_free[pb_] = t_ev
            if tt == 31:
                wv_free[0] = t_mm
        dsb = kb.dsem("p2vB")
        if "bv" in P2PARTS:
          tk = kb.dma(sp, dr["vB"].rearrange("(t p) c -> p t c", p=128), vstgB[:], dsb, waits=list(evb.values()))
          final_toks.append(tk)
        for c in range(16 if "gate" in P2PARTS else 0):
            fm_chunk(G0 + c * 128, "gate", 0, dr["gT"][c], arg=c)
        for g in range(3):
            for hp in range(4 if "aqk" in P2PARTS else 0):
                fm_chunk(A_Q0 + g * 512 + hp * 128, "aq", g, dr["qTA"][g * 4 + hp])
                fm_chunk(A_K0 + g * 512 + hp * 128, "ak", g, dr["kTA"][g * 4 + hp])
            if "av" not in P2PARTS or g not in AVG:
                continue
            k = ctr["wv"]; ctr["wv"] += 1
            ws = k % 2
            t_wv = kb.dma(pool, wv[ws][:], w_in_v[:, :, A_V0 + g * 512:A_V0 + (g + 1) * 512], wv_ld[ws],
                          waits=(wv_free[ws],))
            for kt in AVKT:
                j = ctr["pv"]; ctr["pv"] += 1
                pb_ = j % 2
                pe.wait(t_wv, toks_init, pv_free[pb_])
                for (p0, cnt, start, d) in window_pieces(g, kt):
                    for kc in range(KC):
                        if start is None:
                            lhsT = hTs[:, kc, S:S + cnt]
                        elif d == 1:
                            lhsT = hTs[:, kc, start:start + cnt]
                        else:
                            r = start % d
                            m0 = start // d
                            lhsT = perm_view(hTs[:, kc, 0:S], d)[:, r, m0:m0 + cnt]
                        ins = nc.tensor.matmul(pv[pb_][p0:p0 + cnt, :], lhsT=lhsT, rhs=wv[ws][:, kc, :],
                                               start=(kc == 0), stop=(kc == KC - 1))
                t_mm = pe.mark(ins)
                if kt == AVKT[-1]:
                    wv_free[ws] = t_mm
                kv_ = ctr["vsa"]; ctr["vsa"] += 1
                vs_ = kv_ % 3
                src = pv[pb_][:].rearrange("p (hp eo d) -> p hp eo d", hp=4, eo=2)
                dstv = vstgA[vs_][:].rearrange("p (hp x) -> p hp x", x=192)
                if kt % 2 == 0:
                    dve.wait(t_mm, vstgA_free[vs_], toks_init)
                    nc.vector.tensor_copy(out=dstv[:, :, 0:64], in_=src[:, :, 0, :])
                    t_e0 = dve.mark(nc.vector.tensor_copy(out=dstv[:, :, 128:192], in_=src[:, :, 1, :]))
                else:
                    act.wait(t_mm, vstgA_free[vs_], toks_init)
                    nc.scalar.activation(out=dstv[:, :, 0:64], in_=src[:, :, 0, :], func=AF.Copy)
                    t_e0 = act.mark(nc.scalar.activation(out=dstv[:, :, 128:192], in_=src[:, :, 1, :], func=AF.Copy))
                t_e1 = t_e0
                pv_free[pb_] = (t_e0, t_e1)
                if AVNOSTORE:
                    vstgA_free[vs_] = (t_e0, t_e1)
                    continue
                tk = kb.dma(sp, dr["vA"][g * 33 + kt], vstgA[vs_][:], vstgA_st[vs_], waits=(t_e0, t_e1))
                vstgA_free[vs_] = tk
                final_toks.append(tk)
        last = {}
        for t in final_toks:
            if id(t[0]) not in last or last[id(t[0])][1] < t[1]:
                last[id(t[0])] = t
        kb.barrier(list(last.values()))


def _final_barrier(kb, final_toks):
    last = {}
    for t in final_toks:
        if id(t[0]) not in last or last[id(t[0])][1] < t[1]:
            last[id(t[0])] = t
    kb.barrier(list(last.values()))


def attn_a_phase(kb, dr):
    nc = kb.nc
    pe, act, dve, pool, sp = kb.pe, kb.act, kb.dve, kb.pool, kb.sp
    with ExitStack() as ph:
        def sb(name, shape, dt):
            return ph.enter_context(nc.sbuf_tensor("p3" + name, shape, dt))

        def pst(name, shape, dt):
            return ph.enter_context(nc.psum_tensor("p3" + name, shape, dt))

        NB = 8
        qs = [sb(f"qs{i}", [128, S], BF16) for i in range(2)]
        ks = [sb(f"ks{i}", [128, S + 128], BF16) for i in range(2)]
        vs = [sb(f"vs{i}", [128, 33, 192], BF16) for i in range(2)]
        tb = [sb(f"tb{i}", [128, 2, 2, 256], F32) for i in range(2)]
        wt = [sb(f"wt{i}", [128, 2, 2, 256], BF16) for i in range(2)]
        acc = [[sb(f"acc{j}{i}", [128, S], F32) for i in range(2)] for j in range(2)]
        den2 = sb("den2", [128, S], F32)
        ost = sb("ost", [128, S], BF16)
        pex = [sb(f"pex{i}", [128, 256], BF16) for i in range(NB)]
        pT = [sb(f"pT{i}", [128, 256], BF16) for i in range(NB)]
        psT = [pst(f"psT{i}", [128, 512], F32) for i in range(4)]
        pU = [[pst(f"pU{e}{i}", [128, 512], F32) for i in range(2)] for e in range(2)]

        ld = [kb.dsem(f"p3ld{i}") for i in range(2)]
        slot_free = [None, None]
        wt_free = [None, None]
        sT_free = [None] * 4
        pex_free = [None] * NB
        pT_free = [None] * NB
        pU_free = [[None, None], [None, None]]
        acc_free = [[None, None], [None, None]]
        den_ds = kb.dsem("p3den")
        ost_ds = kb.dsem("p3ost")
        final_toks = []
        LAG = 3
        it = 0
        pending_norm = []
        nst = {"ost_free": None, "den_free": None}

        def do_norm_dma(hp_, aj_, al_):
            kb.dma(sp, den2[0:64, :], acc[aj_][0][64:128, :], den_ds, waits=(al_[0], al_[1], nst["den_free"]))
            t2_ = kb.dma(sp, den2[64:128, :], acc[aj_][1][0:64, :], den_ds)
            pending_norm2.append((hp_, aj_, t2_))

        def do_norm_act(hp_, aj_, t2_, c):
            cs = slice(c * 1024, (c + 1) * 1024)
            act.wait(t2_)
            t = act.mark(nc.scalar.activation(out=den2[:, cs], in_=den2[:, cs], func=AF.Ln))
            act.wait(t)
            nst["r", c] = act.mark(nc.scalar.activation(out=den2[:, cs], in_=den2[:, cs], func=AF.Exp, scale=-1.0))

        def do_norm_dve(hp_, aj_, t2_, c):
            cs = slice(c * 1024, (c + 1) * 1024)
            dve.wait(nst["r", c], nst["ost_free"])
            nc.vector.tensor_tensor(out=ost[0:64, cs], in0=acc[aj_][0][0:64, cs], in1=den2[0:64, cs], op=ALU.mult)
            t_o = dve.mark(nc.vector.tensor_tensor(out=ost[64:128, cs], in0=acc[aj_][1][64:128, cs],
                                                   in1=den2[64:128, cs], op=ALU.mult))
            if c == 3:
                acc_free[aj_] = [t_o, t_o]
                nst["den_free"] = t_o
                nst["ost_free"] = kb.dma(sp, dr["oaT"][hp_], ost[:], ost_ds, waits=(t_o,))
                final_toks.append(nst["ost_free"])

        pending_norm2 = []
        for hp in range(4):
            aj = hp % 2
            acc_last = [None, None]
            for g in range(3):
                _, d = GROUPS[g]
                L = S // d
                sl_ = it % 2
                it += 1
                pidx = g * 4 + hp
                kb.dma(sp, qs[sl_][:], dr["qTA"][pidx], ld[sl_], waits=(slot_free[sl_],))
                kb.dma(sp, ks[sl_][:], dr["kTA"][pidx], ld[sl_])
                kb.dma(sp, vs[sl_][:], dr["vA"][g * 33:(g + 1) * 33, :, hp * 192:(hp + 1) * 192].rearrange("kt p c -> p kt c"),
                       ld[sl_])
                h0 = g * 8 + 2 * hp
                t_ld = kb.dma(sp, tb[sl_][:], dr["tabA"][h0:h0 + 2].rearrange("h v p c -> p h v c"), ld[sl_])
                act.wait(t_ld, wt_free[sl_])
                t_wt = act.mark(nc.scalar.activation(out=wt[sl_][:], in_=tb[sl_][:], func=AF.Exp))
                pend = []
                last_pv = [None]

                def evac_bank(b, e):
                    bank = pU[e][b % 2]
                    if d == 1:
                        dst = acc[aj][e][:, b * 512:(b + 1) * 512]
                        src = bank[:]
                    elif d == 4:
                        dst = perm_view(acc[aj][e][:], 4)[:, b // 2, (b % 2) * 512:(b % 2) * 512 + 512]
                        src = bank[:]
                    else:
                        dst = perm_view(acc[aj][e][:], 16)[:, 2 * b:2 * b + 2, :]
                        src = bank[:].rearrange("p (a b) -> p a b", a=2)
                    if g == 0:
                        act.wait(last_pv[0], acc_free[aj][e])
                        t = act.mark(nc.scalar.activation(out=dst, in_=src, func=AF.Copy))
                    else:
                        dve.wait(last_pv[0], acc_last[e])
                        t = dve.mark(nc.vector.tensor_tensor(out=dst, in0=dst, in1=src, op=ALU.add))
                    pU_free[e][b % 2] = t
                    acc_last[e] = t

                def do_pv(kt, e, bufi, t_p):
                    lhsT = vs[sl_][:, kt, 64 * e:64 * e + 128]
                    pe.wait(t_p)
                    ins = None
                    if kt >= 1:
                        a_ = kt - 1
                        ins = nc.tensor.matmul(pU[e][(a_ // 4) % 2][:, (a_ % 4) * 128:(a_ % 4) * 128 + 128], lhsT=lhsT,
                                               rhs=pT[bufi][:, 0:128], start=False, stop=True)
                    if kt <= 31:
                        a_ = kt
                        if a_ % 4 == 0:
                            pe.wait(pU_free[e][(a_ // 4) % 2])
                        ins = nc.tensor.matmul(pU[e][(a_ // 4) % 2][:, (a_ % 4) * 128:(a_ % 4) * 128 + 128], lhsT=lhsT,
                                               rhs=pT[bufi][:, 128:256], start=True, stop=False)
                    t = pe.mark(ins)
                    pT_free[bufi] = t
                    last_pv[0] = t
                    if kt >= 4 and kt % 4 == 0:
                        evac_bank(kt // 4 - 1, e)

                for step in range(33 + LAG):
                    if step < 33:
                        kt = step
                        var = 1 if (128 * kt) % L == 0 else 0
                        lo = 128 if kt == 0 else 0
                        hi = 128 if kt == 32 else 256
                        c0 = 128 * (kt - 1) + lo
                        for e in range(2):
                            rows = slice(64 * e, 64 * e + 64)
                            bufi = (kt % 4) * 2 + e
                            si = (kt % 2) * 2 + e
                            sTt = psT[si][:, 0:256]
                            pe.wait(t_ld, sT_free[si])
                            t_s = pe.mark(nc.tensor.matmul(sTt[:, lo:hi], lhsT=ks[sl_][rows, 128 * kt:128 * kt + 128],
                                                           rhs=qs[sl_][rows, c0:c0 + (hi - lo)], start=True, stop=True))
                            act.wait(t_s, pex_free[bufi])
                            t_x = act.mark(nc.scalar.activation(out=pex[bufi][:, lo:hi], in_=sTt[:, lo:hi], func=AF.Exp,
                                                                scale=0.125))
                            sT_free[si] = t_x
                            dve.wait(t_x, pT_free[bufi], t_wt)
                            t_p = dve.mark(nc.vector.tensor_tensor(out=pT[bufi][:, lo:hi], in0=pex[bufi][:, lo:hi],
                                                                   in1=wt[sl_][:, e, var, lo:hi], op=ALU.mult))
                            pex_free[bufi] = t_p
                            pend.append((kt, e, bufi, t_p))
                    if step >= LAG:
                        for _ in range(2):
                            do_pv(*pend.pop(0))
                    if step == 2 and pending_norm:
                        do_norm_dma(*pending_norm.pop(0))
                    for c_ in range(4):
                        if step == 14 + 4 * c_ and pending_norm2:
                            do_norm_act(*pending_norm2[0], c_)
                        if step == 17 + 4 * c_ and pending_norm2:
                            do_norm_dve(*pending_norm2[0], c_)
                            if c_ == 3:
                                pending_norm2.pop(0)
                slot_free[sl_] = (last_pv[0], acc_last[0], acc_last[1])
                wt_free[sl_] = acc_last[1]
            pending_norm.append((hp, aj, list(acc_last)))
        while pending_norm:
            do_norm_dma(*pending_norm.pop(0))
        while pending_norm2:
            for c_ in range(4):
                do_norm_act(*pending_norm2[0], c_)
                do_norm_dve(*pending_norm2[0], c_)
            pending_norm2.pop(0)
        _final_barrier(kb, final_toks)


def attn_b_consts(kb, dr, es):
    nc = kb.nc
    kTs = es.enter_context(nc.sbuf_tensor("p4kTs", [128, 2, S], BF16))
    vBs = es.enter_context(nc.sbuf_tensor("p4vBs", [128, 32, 256], BF16))
    qs = [es.enter_context(nc.sbuf_tensor("p4qs0", [128, S], BF16))]
    dc = kb.dsem("p4const")
    for j in range(2):
        kb.dma(kb.sp, kTs[:, j, :], dr["kTB"][j], dc)
    kb.dma(kb.sp, vBs[:], dr["vB"].rearrange("(t p) c -> p t c", p=128), dc)
    q_ld = [kb.dsem(f"p4q{i}") for i in range(2)]
    t_q0 = kb.dma(kb.sp, qs[0][:], dr["qTB"][0], q_ld[0])
    return kTs, vBs, qs, dc.tok(), q_ld, t_q0


def attn_b_phase(kb, dr, pre):
    nc = kb.nc
    pe, act, dve, pool, sp = kb.pe, kb.act, kb.dve, kb.pool, kb.sp
    SCALE = 128 ** -0.5
    with ExitStack() as ph:
        def sb(name, shape, dt):
            return ph.enter_context(nc.sbuf_tensor("p4" + name, shape, dt))

        def pst(name, shape, dt):
            return ph.enter_context(nc.psum_tensor("p4" + name, shape, dt))

        NP = 4
        NB = 3
        kTs, vBs, qs, tok_const, q_ld, t_q0 = pre
        qs = [qs[0], sb("qs1", [128, S], BF16)]
        ones_bf = sb("ones", [128, 128], BF16)
        ost = [sb(f"ost{i}", [128, S], BF16) for i in range(2)]
        pT = [sb(f"pT{i}", [128, 1024], BF16) for i in range(NP)]
        xx = [sb(f"xx{i}", [128, 1024], BF16) for i in range(2)]
        xx_free = [None, None]
        prev_tp = [None]
        qd = [sb(f"qd{i}", [128, 512], BF16) for i in range(3)]
        racc = [sb(f"racc{i}", [128, 512], F32) for i in range(2)]
        raccb = [sb(f"raccb{i}", [128, 512], BF16) for i in range(2)]
        rD = [sb(f"rD{i}", [128, 512], F32) for i in range(2)]
        psT = [pst(f"psT{i}", [128, 1024], F32) for i in range(NB)]
        pO = [pst(f"pO{i}", [128, 512], F32) for i in range(2)]

        t_ones = pool.mark(nc.gpsimd.memset(ones_bf[:], 1.0))
        q_free = [None, None]
        o_st = [kb.dsem(f"p4o{i}") for i in range(2)]
        o_free = [None, None]
        sT_free = [None] * NB
        tick = [0]
        pT_free = [None] * NP
        qd_free = [None] * 3
        pO_free = [None, None]
        racc_free = [None, None]
        raccb_free = [None, None]
        rD_free = [None, None]
        final_toks = []
        t_q = {}
        items = [(h, qb, pj) for h in range(8) for qb in range(8) for pj in range(16)]
        pend = []
        pend_fin = []
        last_acc = {}
        last_pv = {}
        npp = [0]

        def load_q(h):
            b = h % 2
            t_q[h] = kb.dma(sp, qs[b][:], dr["qTB"][h], q_ld[b], waits=(q_free[b],))

        def do_pv(j, h, qb, pj, t_p):
            ob = (h * 8 + qb) % 2
            kv = h // 4
            pe.wait(t_p)
            if pj == 0:
                pe.wait(pO_free[ob])
            for u in range(2):
                kt = 2 * pj + u
                ins = nc.tensor.matmul(pO[ob][:], lhsT=vBs[:, kt, kv * 128:(kv + 1) * 128],
                                       rhs=pT[j % NP][:, u * 512:(u + 1) * 512],
                                       start=(kt == 0), stop=(kt == 31))
            t = pe.mark(ins)
            last_pv[(h, qb)] = t
            return t

        fin = {}

        def fin_cast(h, qb):
            ob = (h * 8 + qb) % 2
            dve.wait(last_acc[(h, qb)], raccb_free[ob])
            t_c = dve.mark(nc.vector.tensor_copy(out=raccb[ob][:], in_=racc[ob][:]))
            racc_free[ob] = t_c
            fin["c"] = t_c

        def fin_dmm(h, qb):
            ob = (h * 8 + qb) % 2
            bt = tick[0] % NB
            tick[0] += 1
            pe.wait(fin["c"], t_ones, sT_free[bt])
            t_d = pe.mark(nc.tensor.matmul(psT[bt][:, 0:512], lhsT=ones_bf[:], rhs=raccb[ob][:], start=True, stop=True))
            raccb_free[ob] = t_d
            fin["d"] = (t_d, bt)

        def fin_act(h, qb):
            ob = (h * 8 + qb) % 2
            t_d, bt = fin["d"]
            act.wait(t_d, rD_free[ob])
            t_l = act.mark(nc.scalar.activation(out=rD[ob][:], in_=psT[bt][:, 0:512], func=AF.Ln))
            sT_free[bt] = t_l
            act.wait(t_l)
            fin["r"] = act.mark(nc.scalar.activation(out=rD[ob][:], in_=rD[ob][:], func=AF.Exp, scale=-1.0))

        def fin_mul(h, qb):
            ob = (h * 8 + qb) % 2
            dve.wait(fin["r"], last_pv[(h, qb)], o_free[h % 2] if qb == 0 else None)
            t_o = dve.mark(nc.vector.tensor_tensor(out=ost[h % 2][:, qb * 512:(qb + 1) * 512], in0=pO[ob][:],
                                                   in1=rD[ob][:], op=ALU.mult))
            pO_free[ob] = t_o
            rD_free[ob] = t_o
            if qb == 7:
                tk = kb.dma(sp, dr["obT"][h], ost[h % 2][:], o_st[h % 2], waits=(t_o,))
                o_free[h % 2] = tk
                final_toks.append(tk)

        FIN = ((1, fin_cast), (7, fin_dmm), (9, fin_act), (13, fin_mul))

        t_q[0] = t_q0
        for j, (h, qb, pj) in enumerate(items):
            if qb == 0 and pj == 4 and h + 1 < 8:
                load_q(h + 1)
            kv = h // 4
            ob = (h * 8 + qb) % 2
            bt = tick[0] % NB
            tick[0] += 1
            pe.wait(tok_const, t_q[h], sT_free[bt])
            for u in range(2):
                kt = 2 * pj + u
                ins = nc.tensor.matmul(psT[bt][:, u * 512:(u + 1) * 512], lhsT=kTs[:, kv, kt * 128:(kt + 1) * 128],
                                       rhs=qs[h % 2][:, qb * 512:(qb + 1) * 512], start=True, stop=True)
            t_s = pe.mark(ins)
            if qb == 7 and pj == 15:
                q_free[h % 2] = t_s
            act.wait(t_s, pT_free[j % NP])
            t_p = act.mark(nc.scalar.activation(out=pT[j % NP][:], in_=psT[bt][:], func=AF.Exp, scale=SCALE))
            sT_free[bt] = t_p
            t_pp = None
            if pj % 2 == 1:
                xi = (j // 2) % 2
                qi = npp[0] % 3
                npp[0] += 1
                dve.wait(prev_tp[0], t_p, xx_free[xi])
                t_pp = dve.mark(nc.vector.tensor_tensor(out=xx[xi][:], in0=pT[(j - 1) % NP][:], in1=pT[j % NP][:],
                                                        op=ALU.add))
                for k_, it_ in enumerate(pend):
                    if it_[0] == j - 1:
                        pend[k_] = it_[:5] + (t_pp,)
                dve.wait(t_pp, qd_free[qi])
                t_qd = dve.mark(nc.vector.tensor_tensor(out=qd[qi][:], in0=xx[xi][:, 0:512], in1=xx[xi][:, 512:1024],
                                                        op=ALU.add))
                xx_free[xi] = t_qd
                dve.wait(t_qd, racc_free[ob] if pj == 1 else last_acc.get((h, qb)))
                if pj == 1:
                    t_a = dve.mark(nc.vector.tensor_copy(out=racc[ob][:], in_=qd[qi][:]))
                else:
                    t_a = dve.mark(nc.vector.tensor_tensor(out=racc[ob][:], in0=racc[ob][:], in1=qd[qi][:], op=ALU.add))
                qd_free[qi] = t_a
                last_acc[(h, qb)] = t_a
            prev_tp[0] = t_p
            pend.append((j, h, qb, pj, t_p, t_pp))
            if len(pend) > 2:
                (j_, h_, qb_, pj_, tp_, tpp_) = pend.pop(0)
                t = do_pv(j_, h_, qb_, pj_, tp_)
                pT_free[j_ % NP] = (t, tpp_)
                if pj_ == 15:
                    pend_fin.append((h_, qb_))
            for (pjx, fn) in FIN:
                if pj == pjx and pend_fin:
                    fn(*pend_fin[0])
                    if fn is fin_mul:
                        pend_fin.pop(0)
        while pend:
            (j_, h_, qb_, pj_, tp_, tpp_) = pend.pop(0)
            t = do_pv(j_, h_, qb_, pj_, tp_)
            pT_free[j_ % NP] = (t, tpp_)
            if pj_ == 15:
                pend_fin.append((h_, qb_))
        while pend_fin:
            for (_, fn) in FIN:
                fn(*pend_fin[0])
            pend_fin.pop(0)
        _final_barrier(kb, final_toks)


def merge_weights(kb, dr, es):
    nc = kb.nc
    was = es.enter_context(nc.sbuf_tensor("p5was", [128, 4, D], BF16))
    wbs = es.enter_context(nc.sbuf_tensor("p5wbs", [128, 8, D], BF16))
    wos = es.enter_context(nc.sbuf_tensor("p5wos", [128, 8, D], BF16))
    dw = kb.dsem("p5w")
    kb.dma(kb.pool, was[:], dr["w_ba"].rearrange("(kc p) n -> p kc n", p=128), dw)
    kb.dma(kb.pool, wbs[:], dr["w_bb"].rearrange("(kc p) n -> p kc n", p=128), dw)
    kb.dma(kb.pool, wos[:], dr["w_out"].rearrange("(kc p) n -> p kc n", p=128), dw)
    return was, wbs, wos, dw.tok()


def merge_phase(kb, dr, pre):
    nc = kb.nc
    pe, act, dve, pool, sp = kb.pe, kb.act, kb.dve, kb.pool, kb.sp
    with ExitStack() as ph:
        def sb(name, shape, dt):
            return ph.enter_context(nc.sbuf_tensor("p5" + name, shape, dt))

        def pst(name, shape, dt):
            return ph.enter_context(nc.psum_tensor("p5" + name, shape, dt))

        was, wbs, wos, tok_w = pre
        oa = [sb(f"oa{i}", [128, 4, 512], BF16) for i in range(2)]
        ob = [sb(f"ob{i}", [128, 8, 512], BF16) for i in range(2)]
        gt = [sb(f"gt{i}", [128, 16, 512], BF16) for i in range(2)]
        mT = sb("mT", [128, 8, 512], BF16)
        ta = [sb(f"ta{i}", [128, 512], F32) for i in range(2)]
        tbb = [sb(f"tbb{i}", [128, 512], F32) for i in range(2)]
        xr = [sb(f"xr{i}", [128, D], F32) for i in range(3)]
        pA = [pst(f"pA{i}", [128, 512], F32) for i in range(2)]
        pB = [pst(f"pB{i}", [128, 512], F32) for i in range(2)]
        py = [pst(f"py{i}", [128, 512], F32) for i in range(2)]

        ld = [kb.dsem(f"p5ld{i}") for i in range(2)]
        in_free = [None, None]
        xr_ld = [kb.dsem(f"p5xl{i}") for i in range(3)]
        xr_st = [kb.dsem(f"p5xs{i}") for i in range(3)]
        xr_free = [None] * 3
        pA_free = [None] * 2
        pB_free = [None] * 2
        ta_free = [None] * 2
        tb_free = [None] * 2
        py_free = [None] * 2
        final_toks = []
        t_in = {}

        def load_in(i):
            b = i % 2
            tsl = slice(i * 512, (i + 1) * 512)
            kb.dma(sp, oa[b][:], dr["oaT"][:, :, tsl].rearrange("c p t -> p c t"), ld[b], waits=(in_free[b],))
            kb.dma(sp, ob[b][:], dr["obT"][:, :, tsl].rearrange("c p t -> p c t"), ld[b])
            t_in[i] = kb.dma(sp, gt[b][:], dr["gT"][:, :, tsl].rearrange("c p t -> p c t"), ld[b])

        load_in(0)
        cy = 0
        cx = 0
        mT_free = None
        for i in range(NT):
            if i + 1 < NT:
                load_in(i + 1)
            b = i % 2
            t_m = None
            for c in range(8):
                pb_ = c % 2
                pe.wait(tok_w, t_in[i], pA_free[pb_], pB_free[pb_])
                for kc in range(4):
                    ins = nc.tensor.matmul(pA[pb_][:], lhsT=was[:, kc, c * 128:(c + 1) * 128], rhs=oa[b][:, kc, :],
                                           start=(kc == 0), stop=(kc == 3))
                t_a = pe.mark(ins)
                for kc in range(8):
                    ins = nc.tensor.matmul(pB[pb_][:], lhsT=wbs[:, kc, c * 128:(c + 1) * 128], rhs=ob[b][:, kc, :],
                                           start=(kc == 0), stop=(kc == 7))
                t_b = pe.mark(ins)
                dve.wait(t_a, ta_free[pb_], t_in[i])
                t1_ = dve.mark(nc.vector.tensor_tensor(out=ta[pb_][:], in0=pA[pb_][:], in1=gt[b][:, c, :], op=ALU.mult))
                pA_free[pb_] = t1_
                dve.wait(t_b, tb_free[pb_])
                t2_ = dve.mark(nc.vector.tensor_tensor(out=tbb[pb_][:], in0=pB[pb_][:], in1=gt[b][:, 8 + c, :], op=ALU.mult))
                pB_free[pb_] = t2_
                pool.wait(t1_, t2_, mT_free if c == 0 else None)
                t_m = pool.mark(nc.gpsimd.tensor_tensor(out=mT[:, c, :], in0=ta[pb_][:], in1=tbb[pb_][:], op=ALU.add))
                ta_free[pb_] = t_m
                tb_free[pb_] = t_m
            t_lastmm = None
            for s in range(4):
                rb = cx % 3
                cx += 1
                r0 = i * 512 + s * 128
                tl = kb.dma(sp, xr[rb][:], dr["x1"][r0:r0 + 128, :], xr_ld[rb], waits=(xr_free[rb],))
                t_res = None
                for hf in range(2):
                    yb = cy % 2
                    cy += 1
                    pe.wait(t_m, py_free[yb])
                    for c in range(8):
                        ins = nc.tensor.matmul(py[yb][:], lhsT=mT[:, c, s * 128:(s + 1) * 128],
                                               rhs=wos[:, c, hf * 512:(hf + 1) * 512], start=(c == 0), stop=(c == 7))
                    t_y = pe.mark(ins)
                    t_lastmm = t_y
                    dve.wait(t_y, tl)
                    t_res = dve.mark(nc.vector.tensor_tensor(out=xr[rb][:, hf * 512:(hf + 1) * 512], in0=py[yb][:],
                                                             in1=xr[rb][:, hf * 512:(hf + 1) * 512], op=ALU.add))
                    py_free[yb] = t_res
                tst = kb.dma(sp, dr["x2"][r0:r0 + 128, :], xr[rb][:], xr_st[rb], waits=(t_res,))
                xr_free[rb] = tst
                final_toks.append(tst)
            mT_free = t_lastmm
            in_free[b] = t_lastmm
        _final_barrier(kb, final_toks)


SCR_PHASE = {"x1": 1, "hT": 1, "qTA": 2, "kTA": 2, "vA": 2, "qTB": 2, "kTB": 2, "vB": 2, "gT": 2,
             "oaT": 3, "obT": 4, "x2": 5}


def build_program(stop_after=99, dbg=False, start_at=1):
    kb = KB()
    nc = kb.nc

    def din(name, shape, dt=F32):
        return nc.dram_tensor(name, shape, dt, kind="ExternalInput").ap()

    def scr(name, shape, dt):
        kind = "ExternalOutput" if dbg else "Internal"
        if SCR_PHASE[name] < start_at:
            kind = "ExternalInput"
        return nc.dram_tensor(name, shape, dt, kind=kind).ap()

    dr = {}
    dr["x"] = din("x", [S, D])
    for p in ("ffn1", "ffn2"):
        dr[p + "_w1"] = din(p + "_w1", [D, DFF])
        dr[p + "_w3"] = din(p + "_w3", [D, DFF])
        dr[p + "_w2"] = din(p + "_w2", [DFF, D])
        dr[p + "_gbc"] = din(p + "_gbc", [128, KC, 128])
    dr["mix_gbc"] = din("mix_gbc", [128, KC, 128])
    dr["fin_bc"] = din("fin_bc", [128, D])
    dr["w_in"] = din("w_in", [D, 8192])
    dr["bg_col"] = din("bg_col", [128, 16])
    dr["qkg_col"] = din("qkg_col", [128, 2])
    dr["ropeC"] = din("ropeC", [128, S])
    dr["ropeS"] = din("ropeS", [128, S])
    dr["rotT"] = din("rotT", [128, 128])
    dr["ident"] = din("ident", [128, 128])
    dr["tabA"] = din("tabA", [24, 2, 128, 256])
    dr["w_ba"] = din("w_ba", [512, D])
    dr["w_bb"] = din("w_bb", [D, D])
    dr["w_out"] = din("w_out", [D, D])
    out = nc.dram_tensor("out", [S, D], F32, kind="ExternalOutput").ap()
    dr["x1"] = scr("x1", [S, D], F32)
    dr["hT"] = scr("hT", [KC, 128, S], BF16)
    dr["qTA"] = scr("qTA", [12, 128, S], BF16)
    dr["kTA"] = scr("kTA", [12, 128, S + 128], BF16)
    dr["vA"] = scr("vA", [99, 128, 768], BF16)
    dr["qTB"] = scr("qTB", [8, 128, S], BF16)
    dr["kTB"] = scr("kTB", [2, 128, S], BF16)
    dr["vB"] = scr("vB", [S, 256], BF16)
    dr["gT"] = scr("gT", [16, 128, S], BF16)
    dr["oaT"] = scr("oaT", [4, 128, S], BF16)
    dr["obT"] = scr("obT", [8, 128, S], BF16)
    dr["x2"] = scr("x2", [S, D], F32)

    with kb.es:
        if start_at <= 1:
            ffn_phase(kb, "f1", dr["x"], dr["ffn1_gbc"], dr["ffn1_w1"], dr["ffn1_w3"], dr["ffn1_w2"], dr["ident"],
                      "ffn1", x_dst=dr["x1"], g2bc_d=dr["mix_gbc"], hT_dst=dr["hT"])
        if start_at <= 2 <= stop_after:
            proj_phase(kb, dr)
        with ExitStack() as w4:
            pre4 = attn_b_consts(kb, dr, w4) if (start_at <= 4 <= stop_after) else None
            if start_at <= 3 <= stop_after:
                attn_a_phase(kb, dr)
            with ExitStack() as w5:
                pre5 = merge_weights(kb, dr, w5) if (start_at <= 5 <= stop_after) else None
                if start_at <= 4 <= stop_after:
                    attn_b_phase(kb, dr, pre4)
                if start_at <= 5 <= stop_after:
                    merge_phase(kb, dr, pre5)
        if start_at <= 6 <= stop_after:
            ffn_phase(kb, "f2", dr["x2"], dr["ffn2_gbc"], dr["ffn2_w1"], dr["ffn2_w3"], dr["ffn2_w2"], dr["ident"],
                      "final", x_dst=out, fin_d=dr["fin_bc"])
    return nc


def _gbc(g):
    return np.ascontiguousarray(np.broadcast_to(g.reshape(KC, 128).T[:, :, None], (128, KC, 128))).astype(np.float32)


def make_in_maps(inp):
    f = np.float32
    C, Sn = rope_tables()
    shared = {
        "ffn1_w1": inp["ffn1_w1"][0], "ffn1_w3": inp["ffn1_w3"][0], "ffn1_w2": inp["ffn1_w2"][0],
        "ffn2_w1": inp["ffn2_w1"][0], "ffn2_w3": inp["ffn2_w3"][0], "ffn2_w2": inp["ffn2_w2"][0],
        "ffn1_gbc": _gbc(inp["ffn1_norm"][0]), "ffn2_gbc": _gbc(inp["ffn2_norm"][0]),
        "mix_gbc": _gbc(inp["mix_norm"][0]),
        "fin_bc": np.ascontiguousarray(np.broadcast_to(inp["final_norm"][None, :], (128, D))).astype(f),
        "w_in": inp["w_in"][0],
        "bg_col": np.ascontiguousarray(inp["b_gate"][0].reshape(16, 128).T).astype(f),
        "qkg_col": np.ascontiguousarray(np.stack([inp["q_norm"][0], inp["k_norm"][0]], axis=1)).astype(f),
        "ropeC": C, "ropeS": Sn, "rotT": rot_lhsT(), "ident": np.eye(128, dtype=f),
        "tabA": host_tables(np.asarray(inp["rel_bias"])),
        "w_ba": inp["w_branch_a"][0], "w_bb": inp["w_branch_b"][0], "w_out": inp["w_out"][0],
    }
    shared = {k: np.ascontiguousarray(v, dtype=f) for k, v in shared.items()}
    maps = []
    for b in range(8):
        m = dict(shared)
        m["x"] = np.ascontiguousarray(inp["x"][b], dtype=f)
        maps.append(m)
    return maps


def kernel(**inputs):
    inp = {k: np.asarray(v) for k, v in inputs.items()}
    nc = build_program()
    res = run_bass_kernel_spmd(nc, make_in_maps(inp), core_ids=list(range(8)))
    return np.stack([np.asarray(r["out"]) for r in res.results], axis=0).astype(np.float32)
```

```python
import numpy as np
from contextlib import ExitStack
import concourse.bass as bass
import concourse.mybir as mybir
from concourse.bass_utils import run_bass_kernel_spmd

F32 = mybir.dt.float32
BF16 = mybir.dt.bfloat16
AF = mybir.ActivationFunctionType
ALU = mybir.AluOpType

S = 4096
D = 1024
DFF = 2816
NCH = DFF // 128
KC = D // 128
NT = S // 512
EPS = 1e-6
GROUPS = ((128, 1), (512, 4), (2048, 16))
WBLK = ((0, 6), (6, 12), (12, 17), (17, 22))


class EngQ:
    def __init__(self, kb, eng, name):
        self.eng = eng
        self.sem = kb.sem("q_" + name)
        self.n = 0
        self.waited = {}

    def wait(self, *toks):
        for t in toks:
            if t is None:
                continue
            if isinstance(t, (list, tuple)) and not (len(t) == 2 and isinstance(t[1], int)):
                self.wait(*t)
                continue
            sem, val = t
            key = id(sem)
            if self.waited.get(key, 0) >= val:
                continue
            self.eng.wait_ge(sem, val)
            self.waited[key] = val

    def mark(self, inst):
        self.n += 1
        inst.then_inc(self.sem, 1)
        return (self.sem, self.n)


class DSem:
    def __init__(self, sem):
        self.sem = sem
        self.val = 0

    def tok(self):
        return (self.sem, self.val)


class KB:
    def __init__(self):
        self.nc = bass.Bass("TRN2", target_bir_lowering=False)
        self.es = ExitStack()
        nc = self.nc
        self.pe = EngQ(self, nc.tensor, "pe")
        self.act = EngQ(self, nc.scalar, "act")
        self.dve = EngQ(self, nc.vector, "dve")
        self.pool = EngQ(self, nc.gpsimd, "pool")
        self.sp = EngQ(self, nc.sync, "sp")
        self.engs = [self.pe, self.act, self.dve, self.pool, self.sp]
        self._ds = {}

    def sem(self, name):
        return self.es.enter_context(self.nc.semaphore(name))

    def dsem(self, name):
        if name not in self._ds:
            self._ds[name] = DSem(self.sem("d_" + name))
        return self._ds[name]

    def dma(self, q, out, in_, ds, waits=()):
        q.wait(*waits)
        inst = q.eng.dma_start(out=out, in_=in_)
        inst.then_inc(ds.sem, 16)
        ds.val += 16
        return (ds.sem, ds.val)

    def barrier(self, toks):
        for e in self.engs:
            e.wait(*toks)


def perm_view(ap3, d):
    return ap3.rearrange("p (m r) -> p r m", r=d)


def ffn_phase(kb, tag, x_src, gbc_d, w1d, w3d, w2d, ident_d, mode, x_dst=None, g2bc_d=None,
              hT_dst=None, fin_d=None):
    nc = kb.nc
    pe, act, dve, pool, sp = kb.pe, kb.act, kb.dve, kb.pool, kb.sp
    with ExitStack() as ph:
        def sb(name, shape, dt):
            return ph.enter_context(nc.sbuf_tensor(tag + name, shape, dt))

        def pst(name, shape, dt):
            return ph.enter_context(nc.psum_tensor(tag + name, shape, dt))

        w1s = sb("w1s", [128, KC, DFF], BF16)
        w3s = sb("w3s", [128, KC, DFF], BF16)
        w2s = sb("w2s", [128, NCH, D], BF16)
        gbc = sb("gbc", [128, KC, 128], F32)
        ident = sb("ident", [128, 128], BF16)
        xin = [sb(f"xin{i}", [128, D], F32) for i in range(2)]
        xr = [sb(f"xr{i}", [128, D], F32) for i in range(2)]
        xn = [sb(f"xn{i}", [128, D], BF16) for i in range(4)]
        xnT = sb("xnT", [128, KC, 512], BF16)
        gT = sb("gT", [128, NCH, 512], BF16)
        sl = [sb(f"sl{i}", [128, 512], F32) for i in range(2)]
        ssx = sb("ssx", [128, 8], F32)
        epsc = sb("epsc", [128, 1], F32)
        t_eps = dve.mark(nc.vector.memset(epsc[:], EPS))
        if mode == "ffn1":
            g2bc = sb("g2bc", [128, KC, 128], F32)
            xn2 = [sb(f"xn2{i}", [128, D], BF16) for i in range(2)]
            hst = sb("hst", [128, KC, 256], BF16)
            ss2 = sb("ss2", [128, 8], F32)
        else:
            finbc = sb("finbc", [128, D], F32)
            ss2 = sb("ss2", [128, 8], F32)
            junkb = [sb(f"junkb{i}", [128, D], BF16) for i in range(2)]
        pa = [pst(f"pa{i}", [128, 512], F32) for i in range(2)]
        pb = [pst(f"pb{i}", [128, 512], F32) for i in range(2)]
        py = [pst(f"py{i}", [128, 512], F32) for i in range(2)]
        ptp = [pst(f"ptp{i}", [128, KC, 128], BF16) for i in range(2)]

        dc = kb.dsem(tag + "const")
        kb.dma(sp, gbc[:], gbc_d, dc)
        dcp = kb.dsem(tag + "constp")
        kb.dma(pool, ident[:], ident_d, dcp)
        if mode == "ffn1":
            kb.dma(sp, g2bc[:], g2bc_d, dc)
        else:
            kb.dma(sp, finbc[:], fin_d, dc)
        tok_const = (dc.tok(), dcp.tok())
        w1v = w1d.rearrange("(kc p) n -> p kc n", p=128)
        w3v = w3d.rearrange("(kc p) n -> p kc n", p=128)
        w2v = w2d.rearrange("(c p) n -> p c n", p=128)
        tok_w13 = []
        tok_w2 = []
        for bi, (c0, c1) in enumerate(WBLK):
            ds = kb.dsem(tag + f"w13_{bi}")
            kb.dma(pool, w1s[:, :, c0 * 128:c1 * 128], w1v[:, :, c0 * 128:c1 * 128], ds)
            kb.dma(pool, w3s[:, :, c0 * 128:c1 * 128], w3v[:, :, c0 * 128:c1 * 128], ds)
            tok_w13.append(ds.tok())
        for bi, (c0, c1) in enumerate(WBLK):
            ds = kb.dsem(tag + f"w2_{bi}")
            kb.dma(pool, w2s[:, c0:c1, :], w2v[:, c0:c1, :], ds)
            tok_w2.append(ds.tok())

        def wblk_of(c):
            for bi, (c0, c1) in enumerate(WBLK):
                if c0 <= c < c1:
                    return bi

        def rstd_chain(src, junk, ssap, waits):
            act.wait(waits, t_eps)
            t = act.mark(nc.scalar.activation(out=junk, in_=src, func=AF.Square, accum_out=ssap))
            act.wait(t)
            t = act.mark(nc.scalar.activation(out=ssap, in_=ssap, func=AF.Ln, scale=1.0 / D, bias=epsc[:, 0:1]))
            act.wait(t)
            t = act.mark(nc.scalar.activation(out=ssap, in_=ssap, func=AF.Exp, scale=-0.5))
            return t

        xin_ld = [kb.dsem(tag + f"xin_ld{i}") for i in range(2)]
        xin_free = [None, None]
        xn_free = [None] * 4
        xr_ld = [kb.dsem(tag + f"xr_ld{i}") for i in range(2)]
        xr_st = [kb.dsem(tag + f"xr_st{i}") for i in range(2)]
        xr_free = [None, None]
        ptp_free = [None, None]
        pa_free = [None, None]
        pb_free = [None, None]
        sl_free = [None, None]
        py_free = [None, None]
        st = {"xnT_ready": None, "xnT_parts": [], "hst_free": None, "ctr_xin": 0, "ctr_tp": 0,
              "ctr_xr": 0, "ctr_y": 0, "ctr_xn2": 0}
        hst_ds = kb.dsem(tag + "hst_st") if mode == "ffn1" else None
        xn2_free = [None, None]
        pending_h = []
        final_toks = []

        def norm_chain(i, s):
            k = st["ctr_xin"]; st["ctr_xin"] += 1
            b = k % 2
            r0 = i * 512 + s * 128
            tl = kb.dma(sp, xin[b][:], x_src[r0:r0 + 128, :], xin_ld[b], waits=(xin_free[b],))
            t_r = rstd_chain(xin[b][:], xn[s][:], ssx[:, s:s + 1], (tl, xn_free[s]))
            dve.wait(t_r)
            t_xn = dve.mark(nc.vector.tensor_scalar(out=xn[s][:], in0=xin[b][:], scalar1=ssx[:, s:s + 1],
                                                    scalar2=None, op0=ALU.mult))
            xin_free[b] = t_xn
            st["t_xn", s] = t_xn

        def norm_tp(i, s):
            t_xn = st["t_xn", s]
            j = st["ctr_tp"]; st["ctr_tp"] += 1
            pbk = j % 2
            pe.wait(t_xn, ptp_free[pbk], tok_const)
            for kc in range(KC):
                ins = nc.tensor.transpose(ptp[pbk][:, kc, :], xn[s][:, kc * 128:(kc + 1) * 128], ident[:])
            t_tp = pe.mark(ins)
            xn_free[s] = t_tp
            dve.wait(t_tp, tok_const)
            t_ev = dve.mark(nc.vector.tensor_tensor(out=xnT[:, :, s * 128:(s + 1) * 128], in0=ptp[pbk][:],
                                                    in1=gbc[:], op=ALU.mult))
            ptp_free[pbk] = t_ev
            return t_ev

        def h_transposes():
            while pending_h:
                (bi2, t_rdy, ti, s) = pending_h.pop(0)
                j = st["ctr_tp"]; st["ctr_tp"] += 1
                pbk = j % 2
                pe.wait(t_rdy, ptp_free[pbk])
                for kc in range(KC):
                    ins = nc.tensor.transpose(ptp[pbk][:, kc, :], xn2[bi2][:, kc * 128:(kc + 1) * 128], ident[:])
                t_tp = pe.mark(ins)
                xn2_free[bi2] = t_tp
                waits = [t_tp]
                if s % 2 == 0:
                    waits.append(st["hst_free"])
                dve.wait(*waits)
                s2 = s % 2
                t_ev = dve.mark(nc.vector.tensor_tensor(out=hst[:, :, s2 * 128:(s2 + 1) * 128], in0=ptp[pbk][:],
                                                        in1=g2bc[:], op=ALU.mult))
                ptp_free[pbk] = t_ev
                if s2 == 1:
                    c0 = ti * 512 + (s // 2) * 256
                    tk = kb.dma(sp, hT_dst[:, :, c0:c0 + 256].rearrange("kc p t -> p kc t"), hst[:],
                                hst_ds, waits=(t_ev,))
                    st["hst_free"] = tk
                    final_toks.append(tk)

        def h_stage(i, t_xnT):
            toks = []
            for c in range(NCH):
                if i + 1 < NT and c in (3, 8, 13, 18):
                    norm_chain(i + 1, (3, 8, 13, 18).index(c))
                b = c % 2
                pe.wait(t_xnT, tok_w13[wblk_of(c)], pa_free[b], pb_free[b])
                for kc in range(KC):
                    ins = nc.tensor.matmul(pa[b][:], lhsT=w1s[:, kc, c * 128:(c + 1) * 128], rhs=xnT[:, kc, :],
                                           start=(kc == 0), stop=(kc == KC - 1))
                t_a = pe.mark(ins)
                for kc in range(KC):
                    ins = nc.tensor.matmul(pb[b][:], lhsT=w3s[:, kc, c * 128:(c + 1) * 128], rhs=xnT[:, kc, :],
                                           start=(kc == 0), stop=(kc == KC - 1))
                t_b = pe.mark(ins)
                act.wait(t_a, sl_free[b])
                t_s = act.mark(nc.scalar.activation(out=sl[b][:], in_=pa[b][:], func=AF.Silu))
                pa_free[b] = t_s
                dve.wait(t_s, t_b)
                t_g = dve.mark(nc.vector.tensor_tensor(out=gT[:, c, :], in0=sl[b][:], in1=pb[b][:], op=ALU.mult))
                pb_free[b] = t_g
                sl_free[b] = t_g
                toks.append(t_g)
            return toks[-1]

        def y_stage(i, t_g):
            for s in range(4):
                k = st["ctr_xr"]; st["ctr_xr"] += 1
                rb = k % 2
                r0 = i * 512 + s * 128
                tl = kb.dma(sp, xr[rb][:], x_src[r0:r0 + 128, :], xr_ld[rb], waits=(xr_free[rb],))
                t_res = None
                for hf in range(2):
                    j = st["ctr_y"]; st["ctr_y"] += 1
                    yb = j % 2
                    pe.wait(t_g, py_free[yb], *tok_w2)
                    for c in range(NCH):
                        ins = nc.tensor.matmul(py[yb][:], lhsT=gT[:, c, s * 128:(s + 1) * 128],
                                               rhs=w2s[:, c, hf * 512:(hf + 1) * 512],
                                               start=(c == 0), stop=(c == NCH - 1))
                    t_y = pe.mark(ins)
                    dve.wait(t_y, tl)
                    t_res = dve.mark(nc.vector.scalar_tensor_tensor(
                        out=xr[rb][:, hf * 512:(hf + 1) * 512], in0=py[yb][:], scalar=0.5,
                        in1=xr[rb][:, hf * 512:(hf + 1) * 512], op0=ALU.mult, op1=ALU.add))
                    py_free[yb] = t_res
                if mode == "ffn1":
                    tst = kb.dma(sp, x_dst[r0:r0 + 128, :], xr[rb][:], xr_st[rb], waits=(t_res,))
                    final_toks.append(tst)
                    k2 = st["ctr_xn2"]; st["ctr_xn2"] += 1
                    b2 = k2 % 2
                    t_r = rstd_chain(xr[rb][:], xn2[b2][:], ss2[:, b2:b2 + 1], (t_res, xn2_free[b2]))
                    dve.wait(t_r)
                    t_x2 = dve.mark(nc.vector.tensor_scalar(out=xn2[b2][:], in0=xr[rb][:], scalar1=ss2[:, b2:b2 + 1],
                                                            scalar2=None, op0=ALU.mult))
                    xr_free[rb] = (t_x2, tst)
                    pending_h.append((b2, t_x2, i, s))
                    if len(pending_h) > 1:
                        keep = pending_h.pop()
                        h_transposes()
                        pending_h.append(keep)
                else:
                    jb = k % 2
                    t_r = rstd_chain(xr[rb][:], junkb[jb][:], ss2[:, rb:rb + 1], (t_res,))
                    dve.wait(t_r, tok_const)
                    t_o = dve.mark(nc.vector.scalar_tensor_tensor(
                        out=xr[rb][:], in0=xr[rb][:], scalar=ss2[:, rb:rb + 1], in1=finbc[:],
                        op0=ALU.mult, op1=ALU.mult))
                    tst = kb.dma(sp, x_dst[r0:r0 + 128, :], xr[rb][:], xr_st[rb], waits=(t_o,))
                    final_toks.append(tst)
                    xr_free[rb] = (tst,)
            return

        for s_ in range(4):
            norm_chain(0, s_)
        for s_ in range(4):
            t_xnT = norm_tp(0, s_)
        for i in range(NT):
            t_g = h_stage(i, t_xnT)
            if i + 1 < NT:
                for s_ in range(4):
                    t_xnT = norm_tp(i + 1, s_)
            if mode == "ffn1":
                h_transposes()
            y_stage(i, t_g)
        if mode == "ffn1":
            h_transposes()
        last = {}
        for t in final_toks:
            last[id(t[0])] = t if (id(t[0]) not in last or last[id(t[0])][1] < t[1]) else last[id(t[0])]
        kb.barrier(list(last.values()))


def _t5_bucket_np(rel):
    n = 16
    max_exact = 8
    ret = np.where(rel > 0, n, 0)
    a = np.abs(rel)
    af = np.maximum(a, 1).astype(np.float32)
    large = max_exact + (np.log(af / np.float32(max_exact)) / np.float32(np.log(1024 / max_exact))
                         * np.float32(n - max_exact)).astype(np.int32)
    large = np.minimum(large, n - 1)
    return ret + np.where(a < max_exact, a, large)


def host_tables(rel_bias):
    NEG = np.float32(-30000.0)
    row = np.arange(128)[:, None]
    col = np.arange(256)[None, :]
    rel = np.where(col < 128, 64 + row - col, row - 64 - (col - 128))
    valid_int = np.abs(rel) <= 64
    bnd_ok = np.where(col < 128, row < 64, row >= 64)
    tab = np.empty((24, 2, 128, 256), np.float32)
    for g, (_, d) in enumerate(GROUPS):
        bucket = _t5_bucket_np((rel * d).astype(np.int32))
        for h in range(8):
            bias = rel_bias[bucket, g * 8 + h].astype(np.float32)
            tab[g * 8 + h, 0] = np.where(valid_int, bias, NEG)
            tab[g * 8 + h, 1] = np.where(valid_int & bnd_ok, bias, NEG)
    return tab


def rope_tables():
    t = np.arange(S)
    rowi = (t // 64).astype(np.float32)
    coli = (t % 64).astype(np.float32)
    nf = 32
    freq = (np.float32(10000.0) ** (-np.arange(nf, dtype=np.float32) / np.float32(nf))).astype(np.float32)
    ang = np.concatenate([rowi[:, None] * freq, coli[:, None] * freq], axis=-1).astype(np.float32)
    c = np.cos(ang).astype(np.float32)
    s = np.sin(ang).astype(np.float32)
    C = np.repeat(c.T, 2, axis=0)
    Sn = np.repeat(s.T, 2, axis=0)
    return np.ascontiguousarray(C), np.ascontiguousarray(Sn)


def rot_lhsT():
    m = np.zeros((128, 128), np.float32)
    for i in range(64):
        m[2 * i + 1, 2 * i] = -1.0
        m[2 * i, 2 * i + 1] = 1.0
    return m


P2PARTS = ("bqk", "bv", "gate", "aqk", "av")
AVG = (0, 1, 2)
AVKT = range(33)
AVSIMPLE = 0
AVNOSTORE = 0
AVALIGN = 0
A_Q0, A_K0, A_V0 = 0, 1536, 3072
B_Q0, B_K0, B_V0, G0 = 4608, 5632, 5888, 6144


def window_pieces(g, kt):
    _, d = GROUPS[g]
    L = S // d
    halves = []
    for j in (2 * kt - 1, 2 * kt):
        if j < 0 or j >= S // 64:
            halves.append(None)
        else:
            r, m0 = divmod(64 * j, L)
            halves.append((m0 * d + r, r, m0))
    h0, h1 = halves
    if h0 is not None and h1 is not None and h0[1] == h1[1]:
        return [(0, 128, h0[0], d)]
    out = []
    for i, h in enumerate(halves):
        out.append((64 * i, 64, None if h is None else h[0], d))
    return out


def proj_phase(kb, dr):
    nc = kb.nc
    pe, act, dve, pool, sp = kb.pe, kb.act, kb.dve, kb.pool, kb.sp
    with ExitStack() as ph:
        def sb(name, shape, dt):
            return ph.enter_context(nc.sbuf_tensor("p2" + name, shape, dt))

        def pst(name, shape, dt):
            return ph.enter_context(nc.psum_tensor("p2" + name, shape, dt))

        hTs = sb("hTs", [128, KC, S + 64], BF16)
        ropeC = sb("ropeC", [128, S], F32)
        ropeS = sb("ropeS", [128, S], F32)
        ones_bf = sb("ones", [128, 128], BF16)
        rotT = sb("rotT", [128, 128], BF16)
        bg = sb("bg", [128, 16], F32)
        qkg = sb("qkg", [128, 2], F32)
        epsc = sb("epsc", [128, 1], F32)
        wc = [sb(f"wc{i}", [128, KC, 128], BF16) for i in range(3)]
        wv = [sb(f"wv{i}", [128, KC, 512], BF16) for i in range(2)]
        stg = [sb(f"stg{i}", [128, S + 128], BF16) for i in range(2)]
        vstgB = sb("vstgB", [128, 32, 256], BF16)
        vstgA = [sb(f"vstgA{i}", [128, 768], BF16) for i in range(3)]
        sq = [sb(f"sq{i}", [128, 512], BF16) for i in range(4)]
        t1 = [sb(f"t1{i}", [128, 512], F32) for i in range(4)]
        t2 = [sb(f"t2{i}", [128, 512], F32) for i in range(4)]
        t3 = [sb(f"t3{i}", [128, 512], F32) for i in range(4)]
        qnb = [sb(f"qnb{i}", [128, 512], BF16) for i in range(4)]
        bank = [pst(f"bk{i}", [128, 512], F32) for i in range(8)]
        pm = bank[0:2]
        pv = bank[2:4]

        dc = kb.dsem("p2const")
        hT_tok = []
        for blk in range(8):
            dsb_ = kb.dsem(f"p2hT{blk}")
            hT_tok.append(kb.dma(sp, hTs[:, :, blk * 512:(blk + 1) * 512],
                                 dr["hT"][:, :, blk * 512:(blk + 1) * 512].rearrange("kc p t -> p kc t"), dsb_))
        kb.dma(sp, bg[:], dr["bg_col"], dc)
        kb.dma(sp, qkg[:], dr["qkg_col"], dc)
        drope = kb.dsem("p2rope")
        kb.dma(sp, ropeC[:], dr["ropeC"], drope)
        kb.dma(sp, ropeS[:], dr["ropeS"], drope)
        tok_rope = drope.tok()
        dcp = kb.dsem("p2constp")
        kb.dma(pool, rotT[:], dr["rotT"], dcp)
        tok_const = (dc.tok(), dcp.tok())
        t_m0 = pool.mark(nc.gpsimd.memset(ones_bf[:], 1.0))
        nc.gpsimd.memset(epsc[:], EPS)
        t_m1 = pool.mark(nc.gpsimd.memset(hTs[:, :, S:S + 64], 0.0))
        toks_small = [tok_const, t_m0, t_m1]
        for i in range(3):
            toks_small.append(pool.mark(nc.gpsimd.memset(vstgA[i][:], 1.0)))
        toks_init = toks_small + hT_tok

        w_in_v = dr["w_in"].rearrange("(kc p) n -> p kc n", p=128)
        wc_ld = [kb.dsem(f"p2wc{i}") for i in range(3)]
        wc_free = [None] * 3
        wv_ld = [kb.dsem(f"p2wv{i}") for i in range(2)]
        wv_free = [None] * 2
        stg_st = [kb.dsem(f"p2stg{i}") for i in range(2)]
        stg_free = [None] * 2
        vstgA_st = [kb.dsem(f"p2vsa{i}") for i in range(3)]
        vstgA_free = [None] * 3
        bank_free = [None] * 8

        class _View:
            def __init__(self, off):
                self.off = off

            def __getitem__(self, i):
                return bank_free[self.off + i]

            def __setitem__(self, i, v):
                bank_free[self.off + i] = v
        pm_free = _View(0)
        pv_free = _View(2)
        sq_free = [None] * 4
        t1_free = [None] * 4
        t2_free = [None] * 4
        t3_free = [None] * 4
        qnb_free = [None] * 4
        ctr = {"wc": 0, "stg": 0, "pm": 0, "wv": 0, "pv": 0, "vsa": 0, "b": 0, "ev": 0}
        final_toks = []

        def tok_rhs(g, kc, blk):
            _, d = GROUPS[g]
            base = hTs[:, kc, 0:S]
            if d == 1:
                return base[:, blk * 512:(blk + 1) * 512], None
            v = perm_view(base, d)
            if d == 4:
                return v[:, blk // 2, (blk % 2) * 512:(blk % 2) * 512 + 512], None
            return v[:, 2 * blk:2 * blk + 2, :], 2

        def fm_chunk(col0, kind, g, dst, arg=None):
            k = ctr["wc"]; ctr["wc"] += 1
            ws = k % 3
            t_w = kb.dma(pool, wc[ws][:], w_in_v[:, :, col0:col0 + 128], wc_ld[ws], waits=(wc_free[ws],))
            ks = ctr["stg"]; ctr["stg"] += 1
            ss = ks % 2
            off = 64 if kind == "ak" else 0
            evs = {}
            if kind == "ak":
                dve.wait(stg_free[ss])
                nc.vector.memset(stg[ss][:, 0:64], 0.0)
                evs["pad"] = dve.mark(nc.vector.memset(stg[ss][:, S + 64:S + 128], 0.0))
            for blk in range(8):
                j = ctr["pm"]; ctr["pm"] += 1
                pb_ = j % 2
                pe.wait(t_w, toks_init, pm_free[pb_])
                for kc in range(KC):
                    rhs, two = tok_rhs(g, kc, blk)
                    o = pm[pb_][:]
                    if two:
                        o = o.rearrange("p (a b) -> p a b", a=2)
                    ins = nc.tensor.matmul(o, lhsT=wc[ws][:, kc, :], rhs=rhs, start=(kc == 0), stop=(kc == KC - 1))
                t_mm = pe.mark(ins)
                if blk == 7:
                    wc_free[ws] = t_mm
                dstap = stg[ss][:, off + blk * 512: off + (blk + 1) * 512]
                if kind in ("aq", "ak"):
                    e = ctr["ev"]; ctr["ev"] += 1
                    if e % 2 == 0:
                        dve.wait(t_mm, stg_free[ss])
                        t_ev = dve.mark(nc.vector.tensor_copy(out=dstap, in_=pm[pb_][:]))
                        evs["dve"] = t_ev
                    else:
                        act.wait(t_mm, stg_free[ss])
                        t_ev = act.mark(nc.scalar.activation(out=dstap, in_=pm[pb_][:], func=AF.Copy))
                        evs["act"] = t_ev
                    pm_free[pb_] = t_ev
                elif kind == "gate":
                    act.wait(t_mm, toks_init, stg_free[ss])
                    t_ev = act.mark(nc.scalar.activation(out=dstap, in_=pm[pb_][:], func=AF.Sigmoid,
                                                         bias=bg[:, arg:arg + 1]))
                    pm_free[pb_] = t_ev
                    evs["act"] = t_ev
            width = S + 128 if kind == "ak" else S
            tk = kb.dma(sp, dst, stg[ss][:, 0:width], stg_st[ss], waits=list(evs.values()))
            stg_free[ss] = tk
            final_toks.append(tk)


        def bqk_pipeline():
            chunks = [(B_K0 + j * 128, 1, dr["kTB"][j]) for j in range(2)] + \
                     [(B_Q0 + h * 128, 0, dr["qTB"][h]) for h in range(8)]
            N = len(chunks) * 8
            T = {}
            wtok = {}

            def load_w(c):
                ws = c % 3
                col0 = chunks[c][0]
                wtok[c] = kb.dma(pool, wc[ws][:], w_in_v[:, :, col0:col0 + 128], wc_ld[ws], waits=(wc_free[ws],))

            def S0(i):
                c, blk = divmod(i, 8)
                if blk == 0 and c + 1 < len(chunks):
                    load_w(c + 1)
                pe.wait(wtok[c], toks_small, hT_tok[blk], bank_free[i % 4])
                for kc in range(KC):
                    ins = nc.tensor.matmul(bank[i % 4][:], lhsT=wc[c % 3][:, kc, :],
                                           rhs=hTs[:, kc, blk * 512:(blk + 1) * 512],
                                           start=(kc == 0), stop=(kc == KC - 1))
                T["mm", i] = pe.mark(ins)
                if blk == 7:
                    wc_free[c % 3] = T["mm", i]

            def S1(i):
                act.wait(T["mm", i], sq_free[i % 4])
                T["sq", i] = act.mark(nc.scalar.activation(out=sq[i % 4][:], in_=bank[i % 4][:], func=AF.Square))

            def S2(i):
                pe.wait(T["sq", i], bank_free[4 + i % 2])
                T["ss", i] = pe.mark(nc.tensor.matmul(bank[4 + i % 2][:], lhsT=ones_bf[:], rhs=sq[i % 4][:],
                                                      start=True, stop=True))
                sq_free[i % 4] = T["ss", i]

            def S3(i):
                act.wait(T["ss", i], t1_free[i % 4], toks_init)
                tl = act.mark(nc.scalar.activation(out=t1[i % 4][:], in_=bank[4 + i % 2][:], func=AF.Ln,
                                                   scale=1.0 / 128, bias=epsc[:, 0:1]))
                bank_free[4 + i % 2] = tl
                act.wait(tl)
                T["rs", i] = act.mark(nc.scalar.activation(out=t1[i % 4][:], in_=t1[i % 4][:], func=AF.Exp, scale=-0.5))

            def S5(i):
                c = i // 8
                dve.wait(T["rs", i], t2_free[i % 4], toks_init)
                T["qn", i] = dve.mark(nc.vector.scalar_tensor_tensor(
                    out=t2[i % 4][:], in0=bank[i % 4][:], scalar=qkg[:, chunks[c][1]:chunks[c][1] + 1],
                    in1=t1[i % 4][:], op0=ALU.mult, op1=ALU.mult))
                bank_free[i % 4] = T["qn", i]
                t1_free[i % 4] = T["qn", i]

            def S6(i):
                act.wait(T["qn", i], qnb_free[i % 4])
                T["qb", i] = act.mark(nc.scalar.activation(out=qnb[i % 4][:], in_=t2[i % 4][:], func=AF.Copy))

            def S7(i):
                pe.wait(T["qb", i], bank_free[6 + i % 2])
                T["rot", i] = pe.mark(nc.tensor.matmul(bank[6 + i % 2][:], lhsT=rotT[:], rhs=qnb[i % 4][:],
                                                       start=True, stop=True))
                qnb_free[i % 4] = T["rot", i]

            def S8(i):
                blk = i % 8
                tsl = slice(blk * 512, (blk + 1) * 512)
                dve.wait(T["qb", i], T["qn", i], tok_rope)
                T["c", i] = dve.mark(nc.vector.tensor_tensor(out=t2[i % 4][:], in0=t2[i % 4][:], in1=ropeC[:, tsl],
                                                             op=ALU.mult))
                dve.wait(T["rot", i], t3_free[i % 4])
                T["s", i] = dve.mark(nc.vector.tensor_tensor(out=t3[i % 4][:], in0=bank[6 + i % 2][:],
                                                             in1=ropeS[:, tsl], op=ALU.mult))
                bank_free[6 + i % 2] = T["s", i]

            def S10(i):
                c, blk = divmod(i, 8)
                ss = c % 2
                pool.wait(T["c", i], T["s", i], stg_free[ss])
                T["o", i] = pool.mark(nc.gpsimd.tensor_tensor(out=stg[ss][:, blk * 512:(blk + 1) * 512],
                                                              in0=t2[i % 4][:], in1=t3[i % 4][:], op=ALU.add))
                t2_free[i % 4] = T["o", i]
                t3_free[i % 4] = T["o", i]
                if blk == 7:
                    tk = kb.dma(sp, chunks[c][2], stg[ss][:, 0:S], stg_st[ss], waits=(T["o", i],))
                    stg_free[ss] = tk
                    final_toks.append(tk)

            load_w(0)
            for step in range(N + 5):
                if step < N:
                    S0(step)
                    S1(step)
                if 0 <= step - 1 < N:
                    S2(step - 1)
                    S3(step - 1)
                if 0 <= step - 2 < N:
                    S5(step - 2)
                    S6(step - 2)
                if 0 <= step - 3 < N:
                    S7(step - 3)
                    S8(step - 3)
                if 0 <= step - 4 < N:
                    S10(step - 4)
            ctr["wc"] = len(chunks)
            ctr["stg"] = len(chunks)
            ctr["pm"] = 0

        if "bqk" in P2PARTS:
            bqk_pipeline()
        t_wv = kb.dma(pool, wv[0][:, :, 0:256], w_in_v[:, :, B_V0:B_V0 + 256], wv_ld[0])
        ctr["wv"] = 1
        evb = {}
        for tt in range(32 if "bv" in P2PARTS else 0):
            j = ctr["pv"]; ctr["pv"] += 1
            pb_ = j % 2
            pe.wait(t_wv, toks_init, pv_free[pb_])
            for kc in range(KC):
                ins = nc.tensor.matmul(pv[pb_][:, 0:256], lhsT=hTs[:, kc, tt * 128:(tt + 1) * 128],
                                       rhs=wv[0][:, kc, 0:256], start=(kc == 0), stop=(kc == KC - 1))
            t_mm = pe.mark(ins)
            if tt % 2 == 0:
                dve.wait(t_mm)
                t_ev = dve.mark(nc.vector.tensor_copy(out=vstgB[:, tt, :], in_=pv[pb_][:, 0:256]))
                evb["dve"] = t_ev
            else:
                act.wait(t_mm)
                t_ev = act.mark(nc.scalar.activation(out=vstgB[:, tt, :], in_=pv[pb_][:, 0:256], func=AF.Copy))
                evb["act"] = t_ev
            pv_free[pb_] = t_ev
            if tt == 31:
                wv_free[0] = t_mm
        dsb = kb.dsem("p2vB")
        if "bv" in P2PARTS:
          tk = kb.dma(sp, dr["vB"].rearrange("(t p) c -> p t c", p=128), vstgB[:], dsb, waits=list(evb.values()))
          final_toks.append(tk)
        for c in range(16 if "gate" in P2PARTS else 0):
            fm_chunk(G0 + c * 128, "gate", 0, dr["gT"][c], arg=c)
        for g in range(3):
            for hp in range(4 if "aqk" in P2PARTS else 0):
                fm_chunk(A_Q0 + g * 512 + hp * 128, "aq", g, dr["qTA"][g * 4 + hp])
                fm_chunk(A_K0 + g * 512 + hp * 128, "ak", g, dr["kTA"][g * 4 + hp])
            if "av" not in P2PARTS or g not in AVG:
                continue
            k = ctr["wv"]; ctr["wv"] += 1
            ws = k % 2
            t_wv = kb.dma(pool, wv[ws][:], w_in_v[:, :, A_V0 + g * 512:A_V0 + (g + 1) * 512], wv_ld[ws],
                          waits=(wv_free[ws],))
            for kt in AVKT:
                j = ctr["pv"]; ctr["pv"] += 1
                pb_ = j % 2
                pe.wait(t_wv, toks_init, pv_free[pb_])
                for (p0, cnt, start, d) in window_pieces(g, kt):
                    for kc in range(KC):
                        if start is None:
                            lhsT = hTs[:, kc, S:S + cnt]
                        elif d == 1:
                            lhsT = hTs[:, kc, start:start + cnt]
                        else:
                            r = start % d
                            m0 = start // d
                            lhsT = perm_view(hTs[:, kc, 0:S], d)[:, r, m0:m0 + cnt]
                        ins = nc.tensor.matmul(pv[pb_][p0:p0 + cnt, :], lhsT=lhsT, rhs=wv[ws][:, kc, :],
                                               start=(kc == 0), stop=(kc == KC - 1))
                t_mm = pe.mark(ins)
                if kt == AVKT[-1]:
                    wv_free[ws] = t_mm
                kv_ = ctr["vsa"]; ctr["vsa"] += 1
                vs_ = kv_ % 3
                src = pv[pb_][:].rearrange("p (hp eo d) -> p hp eo d", hp=4, eo=2)
                dstv = vstgA[vs_][:].rearrange("p (hp x) -> p hp x", x=192)
                if kt % 2 == 0:
                    dve.wait(t_mm, vstgA_free[vs_], toks_init)
                    nc.vector.tensor_copy(out=dstv[:, :, 0:64], in_=src[:, :, 0, :])
                    t_e0 = dve.mark(nc.vector.tensor_copy(out=dstv[:, :, 128:192], in_=src[:, :, 1, :]))
                else:
                    act.wait(t_mm, vstgA_free[vs_], toks_init)
                    nc.scalar.activation(out=dstv[:, :, 0:64], in_=src[:, :, 0, :], func=AF.Copy)
                    t_e0 = act.mark(nc.scalar.activation(out=dstv[:, :, 128:192], in_=src[:, :, 1, :], func=AF.Copy))
                t_e1 = t_e0
                pv_free[pb_] = (t_e0, t_e1)
                if AVNOSTORE:
                    vstgA_free[vs_] = (t_e0, t_e1)
                    continue
                tk = kb.dma(sp, dr["vA"][g * 33 + kt], vstgA[vs_][:], vstgA_st[vs_], waits=(t_e0, t_e1))
                vstgA_free[vs_] = tk
                final_toks.append(tk)
        last = {}
        for t in final_toks:
            if id(t[0]) not in last or last[id(t[0])][1] < t[1]:
                last[id(t[0])] = t
        kb.barrier(list(last.values()))


def _final_barrier(kb, final_toks):
    last = {}
    for t in final_toks:
        if id(t[0]) not in last or last[id(t[0])][1] < t[1]:
            last[id(t[0])] = t
    kb.barrier(list(last.values()))


def attn_a_phase(kb, dr):
    nc = kb.nc
    pe, act, dve, pool, sp = kb.pe, kb.act, kb.dve, kb.pool, kb.sp
    with ExitStack() as ph:
        def sb(name, shape, dt):
            return ph.enter_context(nc.sbuf_tensor("p3" + name, shape, dt))

        def pst(name, shape, dt):
            return ph.enter_context(nc.psum_tensor("p3" + name, shape, dt))

        NB = 8
        qs = [sb(f"qs{i}", [128, S], BF16) for i in range(2)]
        ks = [sb(f"ks{i}", [128, S + 128], BF16) for i in range(2)]
        vs = [sb(f"vs{i}", [128, 33, 192], BF16) for i in range(2)]
        tb = [sb(f"tb{i}", [128, 2, 2, 256], F32) for i in range(2)]
        wt = [sb(f"wt{i}", [128, 2, 2, 256], BF16) for i in range(2)]
        acc = [[sb(f"acc{j}{i}", [128, S], F32) for i in range(2)] for j in range(2)]
        den2 = sb("den2", [128, S], F32)
        ost = sb("ost", [128, S], BF16)
        pex = [sb(f"pex{i}", [128, 256], BF16) for i in range(NB)]
        pT = [sb(f"pT{i}", [128, 256], BF16) for i in range(NB)]
        psT = [pst(f"psT{i}", [128, 512], F32) for i in range(4)]
        pU = [[pst(f"pU{e}{i}", [128, 512], F32) for i in range(2)] for e in range(2)]

        ld = [kb.dsem(f"p3ld{i}") for i in range(2)]
        slot_free = [None, None]
        wt_free = [None, None]
        sT_free = [None] * 4
        pex_free = [None] * NB
        pT_free = [None] * NB
        pU_free = [[None, None], [None, None]]
        acc_free = [[None, None], [None, None]]
        den_ds = kb.dsem("p3den")
        ost_ds = kb.dsem("p3ost")
        final_toks = []
        LAG = 3
        it = 0
        pending_norm = []
        nst = {"ost_free": None, "den_free": None}

        def do_norm_dma(hp_, aj_, al_):
            kb.dma(sp, den2[0:64, :], acc[aj_][0][64:128, :], den_ds, waits=(al_[0], al_[1], nst["den_free"]))
            t2_ = kb.dma(sp, den2[64:128, :], acc[aj_][1][0:64, :], den_ds)
            pending_norm2.append((hp_, aj_, t2_))

        def do_norm_act(hp_, aj_, t2_, c):
            cs = slice(c * 1024, (c + 1) * 1024)
            act.wait(t2_)
            t = act.mark(nc.scalar.activation(out=den2[:, cs], in_=den2[:, cs], func=AF.Ln))
            act.wait(t)
            nst["r", c] = act.mark(nc.scalar.activation(out=den2[:, cs], in_=den2[:, cs], func=AF.Exp, scale=-1.0))

        def do_norm_dve(hp_, aj_, t2_, c):
            cs = slice(c * 1024, (c + 1) * 1024)
            dve.wait(nst["r", c], nst["ost_free"])
            nc.vector.tensor_tensor(out=ost[0:64, cs], in0=acc[aj_][0][0:64, cs], in1=den2[0:64, cs], op=ALU.mult)
            t_o = dve.mark(nc.vector.tensor_tensor(out=ost[64:128, cs], in0=acc[aj_][1][64:128, cs],
                                                   in1=den2[64:128, cs], op=ALU.mult))
            if c == 3:
                acc_free[aj_] = [t_o, t_o]
                nst["den_free"] = t_o
                nst["ost_free"] = kb.dma(sp, dr["oaT"][hp_], ost[:], ost_ds, waits=(t_o,))
                final_toks.append(nst["ost_free"])

        pending_norm2 = []
        ldtok = {}

        def issue_loads(n):
            hp_, g_ = divmod(n, 3)
            sl = n % 2
            pidx = g_ * 4 + hp_
            kb.dma(sp, qs[sl][:], dr["qTA"][pidx], ld[sl], waits=(slot_free[sl],))
            kb.dma(sp, ks[sl][:], dr["kTA"][pidx], ld[sl])
            kb.dma(sp, vs[sl][:], dr["vA"][g_ * 33:(g_ + 1) * 33, :, hp_ * 192:(hp_ + 1) * 192].rearrange("kt p c -> p kt c"),
                   ld[sl])
            h0 = g_ * 8 + 2 * hp_
            ldtok[n] = kb.dma(sp, tb[sl][:], dr["tabA"][h0:h0 + 2].rearrange("h v p c -> p h v c"), ld[sl])

        for hp in range(4):
            aj = hp % 2
            acc_last = [None, None]
            for g in range(3):
                _, d = GROUPS[g]
                L = S // d
                sl_ = it % 2
                it += 1
                n_grp = it - 1
                if n_grp == 0:
                    issue_loads(0)
                t_ld = ldtok[n_grp]
                act.wait(t_ld, wt_free[sl_])
                t_wt = act.mark(nc.scalar.activation(out=wt[sl_][:], in_=tb[sl_][:], func=AF.Exp))
                pend = []
                last_pv = [None]

                def evac_bank(b, e):
                    bank = pU[e][b % 2]
                    if d == 1:
                        dst = acc[aj][e][:, b * 512:(b + 1) * 512]
                        src = bank[:]
                    elif d == 4:
                        dst = perm_view(acc[aj][e][:], 4)[:, b // 2, (b % 2) * 512:(b % 2) * 512 + 512]
                        src = bank[:]
                    else:
                        dst = perm_view(acc[aj][e][:], 16)[:, 2 * b:2 * b + 2, :]
                        src = bank[:].rearrange("p (a b) -> p a b", a=2)
                    if g == 0:
                        act.wait(last_pv[0], acc_free[aj][e])
                        t = act.mark(nc.scalar.activation(out=dst, in_=src, func=AF.Copy))
                    else:
                        dve.wait(last_pv[0], acc_last[e])
                        t = dve.mark(nc.vector.tensor_tensor(out=dst, in0=dst, in1=src, op=ALU.add))
                    pU_free[e][b % 2] = t
                    acc_last[e] = t

                def do_pv(kt, e, bufi, t_p):
                    lhsT = vs[sl_][:, kt, 64 * e:64 * e + 128]
                    pe.wait(t_p)
                    ins = None
                    if kt >= 1:
                        a_ = kt - 1
                        ins = nc.tensor.matmul(pU[e][(a_ // 4) % 2][:, (a_ % 4) * 128:(a_ % 4) * 128 + 128], lhsT=lhsT,
                                               rhs=pT[bufi][:, 0:128], start=False, stop=True)
                    if kt <= 31:
                        a_ = kt
                        if a_ % 4 == 0:
                            pe.wait(pU_free[e][(a_ // 4) % 2])
                        ins = nc.tensor.matmul(pU[e][(a_ // 4) % 2][:, (a_ % 4) * 128:(a_ % 4) * 128 + 128], lhsT=lhsT,
                                               rhs=pT[bufi][:, 128:256], start=True, stop=False)
                    t = pe.mark(ins)
                    pT_free[bufi] = t
                    last_pv[0] = t
                    if kt >= 4 and kt % 4 == 0:
                        evac_bank(kt // 4 - 1, e)

                for step in range(33 + LAG):
                    if step < 33:
                        kt = step
                        var = 1 if (128 * kt) % L == 0 else 0
                        lo = 128 if kt == 0 else 0
                        hi = 128 if kt == 32 else 256
                        c0 = 128 * (kt - 1) + lo
                        for e in range(2):
                            rows = slice(64 * e, 64 * e + 64)
                            bufi = (kt % 4) * 2 + e
                            si = (kt % 2) * 2 + e
                            sTt = psT[si][:, 0:256]
                            pe.wait(t_ld, sT_free[si])
                            t_s = pe.mark(nc.tensor.matmul(sTt[:, lo:hi], lhsT=ks[sl_][rows, 128 * kt:128 * kt + 128],
                                                           rhs=qs[sl_][rows, c0:c0 + (hi - lo)], start=True, stop=True))
                            act.wait(t_s, pex_free[bufi])
                            t_x = act.mark(nc.scalar.activation(out=pex[bufi][:, lo:hi], in_=sTt[:, lo:hi], func=AF.Exp,
                                                                scale=0.125))
                            sT_free[si] = t_x
                            dve.wait(t_x, pT_free[bufi], t_wt)
                            t_p = dve.mark(nc.vector.tensor_tensor(out=pT[bufi][:, lo:hi], in0=pex[bufi][:, lo:hi],
                                                                   in1=wt[sl_][:, e, var, lo:hi], op=ALU.mult))
                            pex_free[bufi] = t_p
                            pend.append((kt, e, bufi, t_p))
                    if step >= LAG:
                        for _ in range(2):
                            do_pv(*pend.pop(0))
                    if step == 1 and n_grp + 1 < 12:
                        issue_loads(n_grp + 1)
                    if step == 2 and pending_norm:
                        do_norm_dma(*pending_norm.pop(0))
                    for c_ in range(4):
                        if step == 14 + 4 * c_ and pending_norm2:
                            do_norm_act(*pending_norm2[0], c_)
                        if step == 17 + 4 * c_ and pending_norm2:
                            do_norm_dve(*pending_norm2[0], c_)
                            if c_ == 3:
                                pending_norm2.pop(0)
                slot_free[sl_] = (last_pv[0], acc_last[0], acc_last[1])
                wt_free[sl_] = acc_last[1]
            pending_norm.append((hp, aj, list(acc_last)))
        while pending_norm:
            do_norm_dma(*pending_norm.pop(0))
        while pending_norm2:
            for c_ in range(4):
                do_norm_act(*pending_norm2[0], c_)
                do_norm_dve(*pending_norm2[0], c_)
            pending_norm2.pop(0)
        _final_barrier(kb, final_toks)


def attn_b_consts(kb, dr, es):
    nc = kb.nc
    kTs = es.enter_context(nc.sbuf_tensor("p4kTs", [128, 2, S], BF16))
    vBs = es.enter_context(nc.sbuf_tensor("p4vBs", [128, 32, 256], BF16))
    qs = [es.enter_context(nc.sbuf_tensor("p4qs0", [128, S], BF16))]
    dc = kb.dsem("p4const")
    for j in range(2):
        kb.dma(kb.sp, kTs[:, j, :], dr["kTB"][j], dc)
    kb.dma(kb.sp, vBs[:], dr["vB"].rearrange("(t p) c -> p t c", p=128), dc)
    q_ld = [kb.dsem(f"p4q{i}") for i in range(2)]
    t_q0 = kb.dma(kb.sp, qs[0][:], dr["qTB"][0], q_ld[0])
    return kTs, vBs, qs, dc.tok(), q_ld, t_q0


def attn_b_phase(kb, dr, pre):
    nc = kb.nc
    pe, act, dve, pool, sp = kb.pe, kb.act, kb.dve, kb.pool, kb.sp
    SCALE = 128 ** -0.5
    with ExitStack() as ph:
        def sb(name, shape, dt):
            return ph.enter_context(nc.sbuf_tensor("p4" + name, shape, dt))

        def pst(name, shape, dt):
            return ph.enter_context(nc.psum_tensor("p4" + name, shape, dt))

        NP = 4
        NB = 3
        kTs, vBs, qs, tok_const, q_ld, t_q0 = pre
        qs = [qs[0], sb("qs1", [128, S], BF16)]
        ones_bf = sb("ones", [128, 128], BF16)
        ost = [sb(f"ost{i}", [128, S], BF16) for i in range(2)]
        pT = [sb(f"pT{i}", [128, 1024], BF16) for i in range(NP)]
        xx = [sb(f"xx{i}", [128, 1024], BF16) for i in range(2)]
        xx_free = [None, None]
        prev_tp = [None]
        qd = [sb(f"qd{i}", [128, 512], BF16) for i in range(3)]
        racc = [sb(f"racc{i}", [128, 512], F32) for i in range(2)]
        raccb = [sb(f"raccb{i}", [128, 512], BF16) for i in range(2)]
        rD = [sb(f"rD{i}", [128, 512], F32) for i in range(2)]
        psT = [pst(f"psT{i}", [128, 1024], F32) for i in range(NB)]
        pO = [pst(f"pO{i}", [128, 512], F32) for i in range(2)]

        t_ones = pool.mark(nc.gpsimd.memset(ones_bf[:], 1.0))
        q_free = [None, None]
        o_st = [kb.dsem(f"p4o{i}") for i in range(2)]
        o_free = [None, None]
        sT_free = [None] * NB
        tick = [0]
        pT_free = [None] * NP
        qd_free = [None] * 3
        pO_free = [None, None]
        racc_free = [None, None]
        raccb_free = [None, None]
        rD_free = [None, None]
        final_toks = []
        t_q = {}
        items = [(h, qb, pj) for h in range(8) for qb in range(8) for pj in range(16)]
        pend = []
        pend_fin = []
        last_acc = {}
        last_pv = {}
        npp = [0]

        def load_q(h):
            b = h % 2
            t_q[h] = kb.dma(sp, qs[b][:], dr["qTB"][h], q_ld[b], waits=(q_free[b],))

        def do_pv(j, h, qb, pj, t_p):
            ob = (h * 8 + qb) % 2
            kv = h // 4
            pe.wait(t_p)
            if pj == 0:
                pe.wait(pO_free[ob])
            for u in range(2):
                kt = 2 * pj + u
                ins = nc.tensor.matmul(pO[ob][:], lhsT=vBs[:, kt, kv * 128:(kv + 1) * 128],
                                       rhs=pT[j % NP][:, u * 512:(u + 1) * 512],
                                       start=(kt == 0), stop=(kt == 31))
            t = pe.mark(ins)
            last_pv[(h, qb)] = t
            return t

        fin = {}

        def fin_cast(h, qb):
            ob = (h * 8 + qb) % 2
            dve.wait(last_acc[(h, qb)], raccb_free[ob])
            t_c = dve.mark(nc.vector.tensor_copy(out=raccb[ob][:], in_=racc[ob][:]))
            racc_free[ob] = t_c
            fin["c"] = t_c

        def fin_dmm(h, qb):
            ob = (h * 8 + qb) % 2
            bt = tick[0] % NB
            tick[0] += 1
            pe.wait(fin["c"], t_ones, sT_free[bt])
            t_d = pe.mark(nc.tensor.matmul(psT[bt][:, 0:512], lhsT=ones_bf[:], rhs=raccb[ob][:], start=True, stop=True))
            raccb_free[ob] = t_d
            fin["d"] = (t_d, bt)

        def fin_act(h, qb):
            ob = (h * 8 + qb) % 2
            t_d, bt = fin["d"]
            act.wait(t_d, rD_free[ob])
            t_l = act.mark(nc.scalar.activation(out=rD[ob][:], in_=psT[bt][:, 0:512], func=AF.Ln))
            sT_free[bt] = t_l
            act.wait(t_l)
            fin["r"] = act.mark(nc.scalar.activation(out=rD[ob][:], in_=rD[ob][:], func=AF.Exp, scale=-1.0))

        def fin_mul(h, qb):
            ob = (h * 8 + qb) % 2
            dve.wait(fin["r"], last_pv[(h, qb)], o_free[h % 2] if qb == 0 else None)
            t_o = dve.mark(nc.vector.tensor_tensor(out=ost[h % 2][:, qb * 512:(qb + 1) * 512], in0=pO[ob][:],
                                                   in1=rD[ob][:], op=ALU.mult))
            pO_free[ob] = t_o
            rD_free[ob] = t_o
            if qb == 7:
                tk = kb.dma(sp, dr["obT"][h], ost[h % 2][:], o_st[h % 2], waits=(t_o,))
                o_free[h % 2] = tk
                final_toks.append(tk)

        FIN = ((1, fin_cast), (7, fin_dmm), (9, fin_act), (13, fin_mul))

        t_q[0] = t_q0
        for j, (h, qb, pj) in enumerate(items):
            if qb == 0 and pj == 4 and h + 1 < 8:
                load_q(h + 1)
            kv = h // 4
            ob = (h * 8 + qb) % 2
            bt = tick[0] % NB
            tick[0] += 1
            pe.wait(tok_const, t_q[h], sT_free[bt])
            for u in range(2):
                kt = 2 * pj + u
                ins = nc.tensor.matmul(psT[bt][:, u * 512:(u + 1) * 512], lhsT=kTs[:, kv, kt * 128:(kt + 1) * 128],
                                       rhs=qs[h % 2][:, qb * 512:(qb + 1) * 512], start=True, stop=True)
            t_s = pe.mark(ins)
            if qb == 7 and pj == 15:
                q_free[h % 2] = t_s
            act.wait(t_s, pT_free[j % NP])
            t_p = act.mark(nc.scalar.activation(out=pT[j % NP][:], in_=psT[bt][:], func=AF.Exp, scale=SCALE))
            sT_free[bt] = t_p
            t_pp = None
            if pj % 2 == 1:
                xi = (j // 2) % 2
                qi = npp[0] % 3
                npp[0] += 1
                dve.wait(prev_tp[0], t_p, xx_free[xi])
                t_pp = dve.mark(nc.vector.tensor_tensor(out=xx[xi][:], in0=pT[(j - 1) % NP][:], in1=pT[j % NP][:],
                                                        op=ALU.add))
                for k_, it_ in enumerate(pend):
                    if it_[0] == j - 1:
                        pend[k_] = it_[:5] + (t_pp,)
                dve.wait(t_pp, qd_free[qi])
                t_qd = dve.mark(nc.vector.tensor_tensor(out=qd[qi][:], in0=xx[xi][:, 0:512], in1=xx[xi][:, 512:1024],
                                                        op=ALU.add))
                xx_free[xi] = t_qd
                dve.wait(t_qd, racc_free[ob] if pj == 1 else last_acc.get((h, qb)))
                if pj == 1:
                    t_a = dve.mark(nc.vector.tensor_copy(out=racc[ob][:], in_=qd[qi][:]))
                else:
                    t_a = dve.mark(nc.vector.tensor_tensor(out=racc[ob][:], in0=racc[ob][:], in1=qd[qi][:], op=ALU.add))
                qd_free[qi] = t_a
                last_acc[(h, qb)] = t_a
            prev_tp[0] = t_p
            pend.append((j, h, qb, pj, t_p, t_pp))
            if len(pend) > 2:
                (j_, h_, qb_, pj_, tp_, tpp_) = pend.pop(0)
                t = do_pv(j_, h_, qb_, pj_, tp_)
                pT_free[j_ % NP] = (t, tpp_)
                if pj_ == 15:
                    pend_fin.append((h_, qb_))
            for (pjx, fn) in FIN:
                if pj == pjx and pend_fin:
                    fn(*pend_fin[0])
                    if fn is fin_mul:
                        pend_fin.pop(0)
        while pend:
            (j_, h_, qb_, pj_, tp_, tpp_) = pend.pop(0)
            t = do_pv(j_, h_, qb_, pj_, tp_)
            pT_free[j_ % NP] = (t, tpp_)
            if pj_ == 15:
                pend_fin.append((h_, qb_))
        while pend_fin:
            for (_, fn) in FIN:
                fn(*pend_fin[0])
            pend_fin.pop(0)
        _final_barrier(kb, final_toks)


def merge_weights(kb, dr, es):
    nc = kb.nc
    was = es.enter_context(nc.sbuf_tensor("p5was", [128, 4, D], BF16))
    wbs = es.enter_context(nc.sbuf_tensor("p5wbs", [128, 8, D], BF16))
    wos = es.enter_context(nc.sbuf_tensor("p5wos", [128, 8, D], BF16))
    dw = kb.dsem("p5w")
    kb.dma(kb.pool, was[:], dr["w_ba"].rearrange("(kc p) n -> p kc n", p=128), dw)
    kb.dma(kb.pool, wbs[:], dr["w_bb"].rearrange("(kc p) n -> p kc n", p=128), dw)
    kb.dma(kb.pool, wos[:], dr["w_out"].rearrange("(kc p) n -> p kc n", p=128), dw)
    return was, wbs, wos, dw.tok()


def merge_phase(kb, dr, pre):
    nc = kb.nc
    pe, act, dve, pool, sp = kb.pe, kb.act, kb.dve, kb.pool, kb.sp
    with ExitStack() as ph:
        def sb(name, shape, dt):
            return ph.enter_context(nc.sbuf_tensor("p5" + name, shape, dt))

        def pst(name, shape, dt):
            return ph.enter_context(nc.psum_tensor("p5" + name, shape, dt))

        was, wbs, wos, tok_w = pre
        oa = [sb(f"oa{i}", [128, 4, 512], BF16) for i in range(2)]
        ob = [sb(f"ob{i}", [128, 8, 512], BF16) for i in range(2)]
        gt = [sb(f"gt{i}", [128, 16, 512], BF16) for i in range(2)]
        mT = sb("mT", [128, 8, 512], BF16)
        ta = [sb(f"ta{i}", [128, 512], F32) for i in range(2)]
        tbb = [sb(f"tbb{i}", [128, 512], F32) for i in range(2)]
        xr = [sb(f"xr{i}", [128, D], F32) for i in range(3)]
        pA = [pst(f"pA{i}", [128, 512], F32) for i in range(2)]
        pB = [pst(f"pB{i}", [128, 512], F32) for i in range(2)]
        py = [pst(f"py{i}", [128, 512], F32) for i in range(2)]

        ld = [kb.dsem(f"p5ld{i}") for i in range(2)]
        in_free = [None, None]
        xr_ld = [kb.dsem(f"p5xl{i}") for i in range(3)]
        xr_st = [kb.dsem(f"p5xs{i}") for i in range(3)]
        xr_free = [None] * 3
        pA_free = [None] * 2
        pB_free = [None] * 2
        ta_free = [None] * 2
        tb_free = [None] * 2
        py_free = [None] * 2
        final_toks = []
        t_in = {}

        def load_in(i):
            b = i % 2
            tsl = slice(i * 512, (i + 1) * 512)
            kb.dma(sp, oa[b][:], dr["oaT"][:, :, tsl].rearrange("c p t -> p c t"), ld[b], waits=(in_free[b],))
            kb.dma(sp, ob[b][:], dr["obT"][:, :, tsl].rearrange("c p t -> p c t"), ld[b])
            t_in[i] = kb.dma(sp, gt[b][:], dr["gT"][:, :, tsl].rearrange("c p t -> p c t"), ld[b])

        load_in(0)
        cy = 0
        cx = 0
        mT_free = None
        for i in range(NT):
            if i + 1 < NT:
                load_in(i + 1)
            b = i % 2
            t_m = None
            for c in range(8):
                pb_ = c % 2
                pe.wait(tok_w, t_in[i], pA_free[pb_], pB_free[pb_])
                for kc in range(4):
                    ins = nc.tensor.matmul(pA[pb_][:], lhsT=was[:, kc, c * 128:(c + 1) * 128], rhs=oa[b][:, kc, :],
                                           start=(kc == 0), stop=(kc == 3))
                t_a = pe.mark(ins)
                for kc in range(8):
                    ins = nc.tensor.matmul(pB[pb_][:], lhsT=wbs[:, kc, c * 128:(c + 1) * 128], rhs=ob[b][:, kc, :],
                                           start=(kc == 0), stop=(kc == 7))
                t_b = pe.mark(ins)
                dve.wait(t_a, ta_free[pb_], t_in[i])
                t1_ = dve.mark(nc.vector.tensor_tensor(out=ta[pb_][:], in0=pA[pb_][:], in1=gt[b][:, c, :], op=ALU.mult))
                pA_free[pb_] = t1_
                dve.wait(t_b, tb_free[pb_])
                t2_ = dve.mark(nc.vector.tensor_tensor(out=tbb[pb_][:], in0=pB[pb_][:], in1=gt[b][:, 8 + c, :], op=ALU.mult))
                pB_free[pb_] = t2_
                pool.wait(t1_, t2_, mT_free if c == 0 else None)
                t_m = pool.mark(nc.gpsimd.tensor_tensor(out=mT[:, c, :], in0=ta[pb_][:], in1=tbb[pb_][:], op=ALU.add))
                ta_free[pb_] = t_m
                tb_free[pb_] = t_m
            t_lastmm = None
            for s in range(4):
                rb = cx % 3
                cx += 1
                r0 = i * 512 + s * 128
                tl = kb.dma(sp, xr[rb][:], dr["x1"][r0:r0 + 128, :], xr_ld[rb], waits=(xr_free[rb],))
                t_res = None
                for hf in range(2):
                    yb = cy % 2
                    cy += 1
                    pe.wait(t_m, py_free[yb])
                    for c in range(8):
                        ins = nc.tensor.matmul(py[yb][:], lhsT=mT[:, c, s * 128:(s + 1) * 128],
                                               rhs=wos[:, c, hf * 512:(hf + 1) * 512], start=(c == 0), stop=(c == 7))
                    t_y = pe.mark(ins)
                    t_lastmm = t_y
                    dve.wait(t_y, tl)
                    t_res = dve.mark(nc.vector.tensor_tensor(out=xr[rb][:, hf * 512:(hf + 1) * 512], in0=py[yb][:],
                                                             in1=xr[rb][:, hf * 512:(hf + 1) * 512], op=ALU.add))
                    py_free[yb] = t_res
                tst = kb.dma(sp, dr["x2"][r0:r0 + 128, :], xr[rb][:], xr_st[rb], waits=(t_res,))
                xr_free[rb] = tst
                final_toks.append(tst)
            mT_free = t_lastmm
            in_free[b] = t_lastmm
        _final_barrier(kb, final_toks)


SCR_PHASE = {"x1": 1, "hT": 1, "qTA": 2, "kTA": 2, "vA": 2, "qTB": 2, "kTB": 2, "vB": 2, "gT": 2,
             "oaT": 3, "obT": 4, "x2": 5}


def build_program(stop_after=99, dbg=False, start_at=1):
    kb = KB()
    nc = kb.nc

    def din(name, shape, dt=F32):
        return nc.dram_tensor(name, shape, dt, kind="ExternalInput").ap()

    def scr(name, shape, dt):
        kind = "ExternalOutput" if dbg else "Internal"
        if SCR_PHASE[name] < start_at:
            kind = "ExternalInput"
        return nc.dram_tensor(name, shape, dt, kind=kind).ap()

    dr = {}
    dr["x"] = din("x", [S, D])
    for p in ("ffn1", "ffn2"):
        dr[p + "_w1"] = din(p + "_w1", [D, DFF])
        dr[p + "_w3"] = din(p + "_w3", [D, DFF])
        dr[p + "_w2"] = din(p + "_w2", [DFF, D])
        dr[p + "_gbc"] = din(p + "_gbc", [128, KC, 128])
    dr["mix_gbc"] = din("mix_gbc", [128, KC, 128])
    dr["fin_bc"] = din("fin_bc", [128, D])
    dr["w_in"] = din("w_in", [D, 8192])
    dr["bg_col"] = din("bg_col", [128, 16])
    dr["qkg_col"] = din("qkg_col", [128, 2])
    dr["ropeC"] = din("ropeC", [128, S])
    dr["ropeS"] = din("ropeS", [128, S])
    dr["rotT"] = din("rotT", [128, 128])
    dr["ident"] = din("ident", [128, 128])
    dr["tabA"] = din("tabA", [24, 2, 128, 256])
    dr["w_ba"] = din("w_ba", [512, D])
    dr["w_bb"] = din("w_bb", [D, D])
    dr["w_out"] = din("w_out", [D, D])
    out = nc.dram_tensor("out", [S, D], F32, kind="ExternalOutput").ap()
    dr["x1"] = scr("x1", [S, D], F32)
    dr["hT"] = scr("hT", [KC, 128, S], BF16)
    dr["qTA"] = scr("qTA", [12, 128, S], BF16)
    dr["kTA"] = scr("kTA", [12, 128, S + 128], BF16)
    dr["vA"] = scr("vA", [99, 128, 768], BF16)
    dr["qTB"] = scr("qTB", [8, 128, S], BF16)
    dr["kTB"] = scr("kTB", [2, 128, S], BF16)
    dr["vB"] = scr("vB", [S, 256], BF16)
    dr["gT"] = scr("gT", [16, 128, S], BF16)
    dr["oaT"] = scr("oaT", [4, 128, S], BF16)
    dr["obT"] = scr("obT", [8, 128, S], BF16)
    dr["x2"] = scr("x2", [S, D], F32)

    with kb.es:
        if start_at <= 1:
            ffn_phase(kb, "f1", dr["x"], dr["ffn1_gbc"], dr["ffn1_w1"], dr["ffn1_w3"], dr["ffn1_w2"], dr["ident"],
                      "ffn1", x_dst=dr["x1"], g2bc_d=dr["mix_gbc"], hT_dst=dr["hT"])
        if start_at <= 2 <= stop_after:
            proj_phase(kb, dr)
        with ExitStack() as w4:
            pre4 = attn_b_consts(kb, dr, w4) if (start_at <= 4 <= stop_after) else None
            if start_at <= 3 <= stop_after:
                attn_a_phase(kb, dr)
            with ExitStack() as w5:
                pre5 = merge_weights(kb, dr, w5) if (start_at <= 5 <= stop_after) else None
                if start_at <= 4 <= stop_after:
                    attn_b_phase(kb, dr, pre4)
                if start_at <= 5 <= stop_after:
                    merge_phase(kb, dr, pre5)
        if start_at <= 6 <= stop_after:
            ffn_phase(kb, "f2", dr["x2"], dr["ffn2_gbc"], dr["ffn2_w1"], dr["ffn2_w3"], dr["ffn2_w2"], dr["ident"],
                      "final", x_dst=out, fin_d=dr["fin_bc"])
    return nc


def _gbc(g):
    return np.ascontiguousarray(np.broadcast_to(g.reshape(KC, 128).T[:, :, None], (128, KC, 128))).astype(np.float32)


def make_in_maps(inp):
    f = np.float32
    C, Sn = rope_tables()
    shared = {
        "ffn1_w1": inp["ffn1_w1"][0], "ffn1_w3": inp["ffn1_w3"][0], "ffn1_w2": inp["ffn1_w2"][0],
        "ffn2_w1": inp["ffn2_w1"][0], "ffn2_w3": inp["ffn2_w3"][0], "ffn2_w2": inp["ffn2_w2"][0],
        "ffn1_gbc": _gbc(inp["ffn1_norm"][0]), "ffn2_gbc": _gbc(inp["ffn2_norm"][0]),
        "mix_gbc": _gbc(inp["mix_norm"][0]),
        "fin_bc": np.ascontiguousarray(np.broadcast_to(inp["final_norm"][None, :], (128, D))).astype(f),
        "w_in": inp["w_in"][0],
        "bg_col": np.ascontiguousarray(inp["b_gate"][0].reshape(16, 128).T).astype(f),
        "qkg_col": np.ascontiguousarray(np.stack([inp["q_norm"][0], inp["k_norm"][0]], axis=1)).astype(f),
        "ropeC": C, "ropeS": Sn, "rotT": rot_lhsT(), "ident": np.eye(128, dtype=f),
        "tabA": host_tables(np.asarray(inp["rel_bias"])),
        "w_ba": inp["w_branch_a"][0], "w_bb": inp["w_branch_b"][0], "w_out": inp["w_out"][0],
    }
    shared = {k: np.ascontiguousarray(v, dtype=f) for k, v in shared.items()}
    maps = []
    for b in range(8):
        m = dict(shared)
        m["x"] = np.ascontiguousarray(inp["x"][b], dtype=f)
        maps.append(m)
    return maps


def kernel(**inputs):
    inp = {k: np.asarray(v) for k, v in inputs.items()}
    nc = build_program()
    res = run_bass_kernel_spmd(nc, make_in_maps(inp), core_ids=list(range(8)))
    return np.stack([np.asarray(r["out"]) for r in res.results], axis=0).astype(np.float32)
```

```python
import numpy as np
from contextlib import ExitStack
import concourse.bass as bass
import concourse.mybir as mybir
from concourse.bass_utils import run_bass_kernel_spmd

F32 = mybir.dt.float32
BF16 = mybir.dt.bfloat16
AF = mybir.ActivationFunctionType
ALU = mybir.AluOpType

S = 4096
D = 1024
DFF = 2816
NCH = DFF // 128
KC = D // 128
NT = S // 512
EPS = 1e-6
GROUPS = ((128, 1), (512, 4), (2048, 16))
WBLK = ((0, 6), (6, 12), (12, 17), (17, 22))


class EngQ:
    def __init__(self, kb, eng, name):
        self.eng = eng
        self.sem = kb.sem("q_" + name)
        self.n = 0
        self.waited = {}

    def wait(self, *toks):
        for t in toks:
            if t is None:
                continue
            if isinstance(t, (list, tuple)) and not (len(t) == 2 and isinstance(t[1], int)):
                self.wait(*t)
                continue
            sem, val = t
            key = id(sem)
            if self.waited.get(key, 0) >= val:
                continue
            self.eng.wait_ge(sem, val)
            self.waited[key] = val

    def mark(self, inst):
        self.n += 1
        inst.then_inc(self.sem, 1)
        return (self.sem, self.n)


class DSem:
    def __init__(self, sem):
        self.sem = sem
        self.val = 0

    def tok(self):
        return (self.sem, self.val)


class KB:
    def __init__(self):
        self.nc = bass.Bass("TRN2", target_bir_lowering=False)
        self.es = ExitStack()
        nc = self.nc
        self.pe = EngQ(self, nc.tensor, "pe")
        self.act = EngQ(self, nc.scalar, "act")
        self.dve = EngQ(self, nc.vector, "dve")
        self.pool = EngQ(self, nc.gpsimd, "pool")
        self.sp = EngQ(self, nc.sync, "sp")
        self.engs = [self.pe, self.act, self.dve, self.pool, self.sp]
        self._ds = {}

    def sem(self, name):
        return self.es.enter_context(self.nc.semaphore(name))

    def dsem(self, name):
        if name not in self._ds:
            self._ds[name] = DSem(self.sem("d_" + name))
        return self._ds[name]

    def dma(self, q, out, in_, ds, waits=()):
        q.wait(*waits)
        inst = q.eng.dma_start(out=out, in_=in_)
        inst.then_inc(ds.sem, 16)
        ds.val += 16
        return (ds.sem, ds.val)

    def barrier(self, toks):
        for e in self.engs:
            e.wait(*toks)


def perm_view(ap3, d):
    return ap3.rearrange("p (m r) -> p r m", r=d)


def ffn_phase(kb, tag, x_src, gbc_d, w1d, w3d, w2d, ident_d, mode, x_dst=None, g2bc_d=None,
              hT_dst=None, fin_d=None):
    nc = kb.nc
    pe, act, dve, pool, sp = kb.pe, kb.act, kb.dve, kb.pool, kb.sp
    with ExitStack() as ph:
        def sb(name, shape, dt):
            return ph.enter_context(nc.sbuf_tensor(tag + name, shape, dt))

        def pst(name, shape, dt):
            return ph.enter_context(nc.psum_tensor(tag + name, shape, dt))

        w1s = sb("w1s", [128, KC, DFF], BF16)
        w3s = sb("w3s", [128, KC, DFF], BF16)
        w2s = sb("w2s", [128, NCH, D], BF16)
        gbc = sb("gbc", [128, KC, 128], F32)
        ident = sb("ident", [128, 128], BF16)
        xin = [sb(f"xin{i}", [128, D], F32) for i in range(2)]
        xr = [sb(f"xr{i}", [128, D], F32) for i in range(2)]
        xn = [sb(f"xn{i}", [128, D], BF16) for i in range(4)]
        xnT = sb("xnT", [128, KC, 512], BF16)
        gT = sb("gT", [128, NCH, 512], BF16)
        sl = [sb(f"sl{i}", [128, 512], F32) for i in range(2)]
        ssx = sb("ssx", [128, 8], F32)
        epsc = sb("epsc", [128, 1], F32)
        t_eps = dve.mark(nc.vector.memset(epsc[:], EPS))
        if mode == "ffn1":
            g2bc = sb("g2bc", [128, KC, 128], F32)
            xn2 = [sb(f"xn2{i}", [128, D], BF16) for i in range(2)]
            hst = sb("hst", [128, KC, 256], BF16)
            ss2 = sb("ss2", [128, 8], F32)
        else:
            finbc = sb("finbc", [128, D], F32)
            ss2 = sb("ss2", [128, 8], F32)
            junkb = [sb(f"junkb{i}", [128, D], BF16) for i in range(2)]
        pa = [pst(f"pa{i}", [128, 512], F32) for i in range(2)]
        pb = [pst(f"pb{i}", [128, 512], F32) for i in range(2)]
        py = [pst(f"py{i}", [128, 512], F32) for i in range(2)]
        ptp = [pst(f"ptp{i}", [128, KC, 128], BF16) for i in range(2)]

        dc = kb.dsem(tag + "const")
        kb.dma(sp, gbc[:], gbc_d, dc)
        dcp = kb.dsem(tag + "constp")
        kb.dma(pool, ident[:], ident_d, dcp)
        if mode == "ffn1":
            kb.dma(sp, g2bc[:], g2bc_d, dc)
        else:
            kb.dma(sp, finbc[:], fin_d, dc)
        tok_const = (dc.tok(), dcp.tok())
        w1v = w1d.rearrange("(kc p) n -> p kc n", p=128)
        w3v = w3d.rearrange("(kc p) n -> p kc n", p=128)
        w2v = w2d.rearrange("(c p) n -> p c n", p=128)
        tok_w13 = []
        tok_w2 = []
        for bi, (c0, c1) in enumerate(WBLK):
            ds = kb.dsem(tag + f"w13_{bi}")
            kb.dma(pool, w1s[:, :, c0 * 128:c1 * 128], w1v[:, :, c0 * 128:c1 * 128], ds)
            kb.dma(pool, w3s[:, :, c0 * 128:c1 * 128], w3v[:, :, c0 * 128:c1 * 128], ds)
            tok_w13.append(ds.tok())
        for bi, (c0, c1) in enumerate(WBLK):
            ds = kb.dsem(tag + f"w2_{bi}")
            kb.dma(pool, w2s[:, c0:c1, :], w2v[:, c0:c1, :], ds)
            tok_w2.append(ds.tok())

        def wblk_of(c):
            for bi, (c0, c1) in enumerate(WBLK):
                if c0 <= c < c1:
                    return bi

        def rstd_chain(src, junk, ssap, waits):
            act.wait(waits, t_eps)
            t = act.mark(nc.scalar.activation(out=junk, in_=src, func=AF.Square, accum_out=ssap))
            act.wait(t)
            t = act.mark(nc.scalar.activation(out=ssap, in_=ssap, func=AF.Ln, scale=1.0 / D, bias=epsc[:, 0:1]))
            act.wait(t)
            t = act.mark(nc.scalar.activation(out=ssap, in_=ssap, func=AF.Exp, scale=-0.5))
            return t

        xin_ld = [kb.dsem(tag + f"xin_ld{i}") for i in range(2)]
        xin_free = [None, None]
        xn_free = [None] * 4
        xr_ld = [kb.dsem(tag + f"xr_ld{i}") for i in range(2)]
        xr_st = [kb.dsem(tag + f"xr_st{i}") for i in range(2)]
        xr_free = [None, None]
        ptp_free = [None, None]
        pa_free = [None, None]
        pb_free = [None, None]
        sl_free = [None, None]
        py_free = [None, None]
        st = {"xnT_ready": None, "xnT_parts": [], "hst_free": None, "ctr_xin": 0, "ctr_tp": 0,
              "ctr_xr": 0, "ctr_y": 0, "ctr_xn2": 0}
        hst_ds = kb.dsem(tag + "hst_st") if mode == "ffn1" else None
        xn2_free = [None, None]
        pending_h = []
        final_toks = []

        def norm_chain(i, s):
            k = st["ctr_xin"]; st["ctr_xin"] += 1
            b = k % 2
            r0 = i * 512 + s * 128
            tl = kb.dma(sp, xin[b][:], x_src[r0:r0 + 128, :], xin_ld[b], waits=(xin_free[b],))
            t_r = rstd_chain(xin[b][:], xn[s][:], ssx[:, s:s + 1], (tl, xn_free[s]))
            dve.wait(t_r)
            t_xn = dve.mark(nc.vector.tensor_scalar(out=xn[s][:], in0=xin[b][:], scalar1=ssx[:, s:s + 1],
                                                    scalar2=None, op0=ALU.mult))
            xin_free[b] = t_xn
            st["t_xn", s] = t_xn

        def norm_tp(i, s):
            t_xn = st["t_xn", s]
            j = st["ctr_tp"]; st["ctr_tp"] += 1
            pbk = j % 2
            pe.wait(t_xn, ptp_free[pbk], tok_const)
            for kc in range(KC):
                ins = nc.tensor.transpose(ptp[pbk][:, kc, :], xn[s][:, kc * 128:(kc + 1) * 128], ident[:])
            t_tp = pe.mark(ins)
            xn_free[s] = t_tp
            dve.wait(t_tp, tok_const)
            t_ev = dve.mark(nc.vector.tensor_tensor(out=xnT[:, :, s * 128:(s + 1) * 128], in0=ptp[pbk][:],
                                                    in1=gbc[:], op=ALU.mult))
            ptp_free[pbk] = t_ev
            return t_ev

        def h_transposes():
            while pending_h:
                (bi2, t_rdy, ti, s) = pending_h.pop(0)
                j = st["ctr_tp"]; st["ctr_tp"] += 1
                pbk = j % 2
                pe.wait(t_rdy, ptp_free[pbk])
                for kc in range(KC):
                    ins = nc.tensor.transpose(ptp[pbk][:, kc, :], xn2[bi2][:, kc * 128:(kc + 1) * 128], ident[:])
                t_tp = pe.mark(ins)
                xn2_free[bi2] = t_tp
                waits = [t_tp]
                if s % 2 == 0:
                    waits.append(st["hst_free"])
                dve.wait(*waits)
                s2 = s % 2
                t_ev = dve.mark(nc.vector.tensor_tensor(out=hst[:, :, s2 * 128:(s2 + 1) * 128], in0=ptp[pbk][:],
                                                        in1=g2bc[:], op=ALU.mult))
                ptp_free[pbk] = t_ev
                if s2 == 1:
                    c0 = ti * 512 + (s // 2) * 256
                    tk = kb.dma(sp, hT_dst[:, :, c0:c0 + 256].rearrange("kc p t -> p kc t"), hst[:],
                                hst_ds, waits=(t_ev,))
                    st["hst_free"] = tk
                    final_toks.append(tk)

        def h_stage(i, t_xnT):
            toks = []
            for c in range(NCH):
                if i + 1 < NT and c in (3, 8, 13, 18):
                    norm_chain(i + 1, (3, 8, 13, 18).index(c))
                b = c % 2
                pe.wait(t_xnT, tok_w13[wblk_of(c)], pa_free[b], pb_free[b])
                for kc in range(KC):
                    ins = nc.tensor.matmul(pa[b][:], lhsT=w1s[:, kc, c * 128:(c + 1) * 128], rhs=xnT[:, kc, :],
                                           start=(kc == 0), stop=(kc == KC - 1))
                t_a = pe.mark(ins)
                for kc in range(KC):
                    ins = nc.tensor.matmul(pb[b][:], lhsT=w3s[:, kc, c * 128:(c + 1) * 128], rhs=xnT[:, kc, :],
                                           start=(kc == 0), stop=(kc == KC - 1))
                t_b = pe.mark(ins)
                act.wait(t_a, sl_free[b])
                t_s = act.mark(nc.scalar.activation(out=sl[b][:], in_=pa[b][:], func=AF.Silu))
                pa_free[b] = t_s
                dve.wait(t_s, t_b)
                t_g = dve.mark(nc.vector.tensor_tensor(out=gT[:, c, :], in0=sl[b][:], in1=pb[b][:], op=ALU.mult))
                pb_free[b] = t_g
                sl_free[b] = t_g
                toks.append(t_g)
            return toks[-1]

        def y_stage(i, t_g):
            for s in range(4):
                k = st["ctr_xr"]; st["ctr_xr"] += 1
                rb = k % 2
                r0 = i * 512 + s * 128
                tl = kb.dma(sp, xr[rb][:], x_src[r0:r0 + 128, :], xr_ld[rb], waits=(xr_free[rb],))
                t_res = None
                for hf in range(2):
                    j = st["ctr_y"]; st["ctr_y"] += 1
                    yb = j % 2
                    pe.wait(t_g, py_free[yb], *tok_w2)
                    for c in range(NCH):
                        ins = nc.tensor.matmul(py[yb][:], lhsT=gT[:, c, s * 128:(s + 1) * 128],
                                               rhs=w2s[:, c, hf * 512:(hf + 1) * 512],
                                               start=(c == 0), stop=(c == NCH - 1))
                    t_y = pe.mark(ins)
                    dve.wait(t_y, tl)
                    t_res = dve.mark(nc.vector.scalar_tensor_tensor(
                        out=xr[rb][:, hf * 512:(hf + 1) * 512], in0=py[yb][:], scalar=0.5,
                        in1=xr[rb][:, hf * 512:(hf + 1) * 512], op0=ALU.mult, op1=ALU.add))
                    py_free[yb] = t_res
                if mode == "ffn1":
                    tst = kb.dma(sp, x_dst[r0:r0 + 128, :], xr[rb][:], xr_st[rb], waits=(t_res,))
                    final_toks.append(tst)
                    k2 = st["ctr_xn2"]; st["ctr_xn2"] += 1
                    b2 = k2 % 2
                    t_r = rstd_chain(xr[rb][:], xn2[b2][:], ss2[:, b2:b2 + 1], (t_res, xn2_free[b2]))
                    dve.wait(t_r)
                    t_x2 = dve.mark(nc.vector.tensor_scalar(out=xn2[b2][:], in0=xr[rb][:], scalar1=ss2[:, b2:b2 + 1],
                                                            scalar2=None, op0=ALU.mult))
                    xr_free[rb] = (t_x2, tst)
                    pending_h.append((b2, t_x2, i, s))
                    if len(pending_h) > 1:
                        keep = pending_h.pop()
                        h_transposes()
                        pending_h.append(keep)
                else:
                    jb = k % 2
                    t_r = rstd_chain(xr[rb][:], junkb[jb][:], ss2[:, rb:rb + 1], (t_res,))
                    dve.wait(t_r, tok_const)
                    t_o = dve.mark(nc.vector.scalar_tensor_tensor(
                        out=xr[rb][:], in0=xr[rb][:], scalar=ss2[:, rb:rb + 1], in1=finbc[:],
                        op0=ALU.mult, op1=ALU.mult))
                    tst = kb.dma(sp, x_dst[r0:r0 + 128, :], xr[rb][:], xr_st[rb], waits=(t_o,))
                    final_toks.append(tst)
                    xr_free[rb] = (tst,)
            return

        for s_ in range(4):
            norm_chain(0, s_)
        for s_ in range(4):
            t_xnT = norm_tp(0, s_)
        for i in range(NT):
            t_g = h_stage(i, t_xnT)
            if i + 1 < NT:
                for s_ in range(4):
                    t_xnT = norm_tp(i + 1, s_)
            if mode == "ffn1":
                h_transposes()
            y_stage(i, t_g)
        if mode == "ffn1":
            h_transposes()
        last = {}
        for t in final_toks:
            last[id(t[0])] = t if (id(t[0]) not in last or last[id(t[0])][1] < t[1]) else last[id(t[0])]
        kb.barrier(list(last.values()))


def _t5_bucket_np(rel):
    n = 16
    max_exact = 8
    ret = np.where(rel > 0, n, 0)
    a = np.abs(rel)
    af = np.maximum(a, 1).astype(np.float32)
    large = max_exact + (np.log(af / np.float32(max_exact)) / np.float32(np.log(1024 / max_exact))
                         * np.float32(n - max_exact)).astype(np.int32)
    large = np.minimum(large, n - 1)
    return ret + np.where(a < max_exact, a, large)


def host_tables(rel_bias):
    NEG = np.float32(-30000.0)
    row = np.arange(128)[:, None]
    col = np.arange(256)[None, :]
    rel = np.where(col < 128, 64 + row - col, row - 64 - (col - 128))
    valid_int = np.abs(rel) <= 64
    bnd_ok = np.where(col < 128, row < 64, row >= 64)
    tab = np.empty((24, 2, 128, 256), np.float32)
    for g, (_, d) in enumerate(GROUPS):
        bucket = _t5_bucket_np((rel * d).astype(np.int32))
        for h in range(8):
            bias = rel_bias[bucket, g * 8 + h].astype(np.float32)
            tab[g * 8 + h, 0] = np.where(valid_int, bias, NEG)
            tab[g * 8 + h, 1] = np.where(valid_int & bnd_ok, bias, NEG)
    return tab


def rope_tables():
    t = np.arange(S)
    rowi = (t // 64).astype(np.float32)
    coli = (t % 64).astype(np.float32)
    nf = 32
    freq = (np.float32(10000.0) ** (-np.arange(nf, dtype=np.float32) / np.float32(nf))).astype(np.float32)
    ang = np.concatenate([rowi[:, None] * freq, coli[:, None] * freq], axis=-1).astype(np.float32)
    c = np.cos(ang).astype(np.float32)
    s = np.sin(ang).astype(np.float32)
    C = np.repeat(c.T, 2, axis=0)
    Sn = np.repeat(s.T, 2, axis=0)
    return np.ascontiguousarray(C), np.ascontiguousarray(Sn)


def rot_lhsT():
    m = np.zeros((128, 128), np.float32)
    for i in range(64):
        m[2 * i + 1, 2 * i] = -1.0
        m[2 * i, 2 * i + 1] = 1.0
    return m


P2PARTS = ("bqk", "bv", "gate", "aqk", "av")
AVG = (0, 1, 2)
AVKT = range(33)
AVSIMPLE = 0
AVNOSTORE = 0
AVALIGN = 0
A_Q0, A_K0, A_V0 = 0, 1536, 3072
B_Q0, B_K0, B_V0, G0 = 4608, 5632, 5888, 6144


def window_pieces(g, kt):
    _, d = GROUPS[g]
    L = S // d
    halves = []
    for j in (2 * kt - 1, 2 * kt):
        if j < 0 or j >= S // 64:
            halves.append(None)
        else:
            r, m0 = divmod(64 * j, L)
            halves.append((m0 * d + r, r, m0))
    h0, h1 = halves
    if h0 is not None and h1 is not None and h0[1] == h1[1]:
        return [(0, 128, h0[0], d)]
    out = []
    for i, h in enumerate(halves):
        out.append((64 * i, 64, None if h is None else h[0], d))
    return out


def proj_phase(kb, dr):
    nc = kb.nc
    pe, act, dve, pool, sp = kb.pe, kb.act, kb.dve, kb.pool, kb.sp
    with ExitStack() as ph:
        def sb(name, shape, dt):
            return ph.enter_context(nc.sbuf_tensor("p2" + name, shape, dt))

        def pst(name, shape, dt):
            return ph.enter_context(nc.psum_tensor("p2" + name, shape, dt))

        hTs = sb("hTs", [128, KC, S + 64], BF16)
        ropeC = sb("ropeC", [128, S], F32)
        ropeS = sb("ropeS", [128, S], F32)
        ones_bf = sb("ones", [128, 128], BF16)
        rotT = sb("rotT", [128, 128], BF16)
        bg = sb("bg", [128, 16], F32)
        qkg = sb("qkg", [128, 2], F32)
        epsc = sb("epsc", [128, 1], F32)
        wc = [sb(f"wc{i}", [128, KC, 128], BF16) for i in range(3)]
        wv = [sb(f"wv{i}", [128, KC, 512], BF16) for i in range(2)]
        stg = [sb(f"stg{i}", [128, S + 128], BF16) for i in range(2)]
        vstgB = sb("vstgB", [128, 32, 256], BF16)
        vstgA = [sb(f"vstgA{i}", [128, 768], BF16) for i in range(3)]
        sq = [sb(f"sq{i}", [128, 512], BF16) for i in range(4)]
        t1 = [sb(f"t1{i}", [128, 512], F32) for i in range(4)]
        t2 = [sb(f"t2{i}", [128, 512], F32) for i in range(4)]
        t3 = [sb(f"t3{i}", [128, 512], F32) for i in range(4)]
        qnb = [sb(f"qnb{i}", [128, 512], BF16) for i in range(4)]
        bank = [pst(f"bk{i}", [128, 512], F32) for i in range(8)]
        pm = bank[0:2]
        pv = bank[2:4]

        dc = kb.dsem("p2const")
        hT_tok = []
        for blk in range(8):
            dsb_ = kb.dsem(f"p2hT{blk}")
            hT_tok.append(kb.dma(sp, hTs[:, :, blk * 512:(blk + 1) * 512],
                                 dr["hT"][:, :, blk * 512:(blk + 1) * 512].rearrange("kc p t -> p kc t"), dsb_))
        kb.dma(sp, bg[:], dr["bg_col"], dc)
        kb.dma(sp, qkg[:], dr["qkg_col"], dc)
        drope = kb.dsem("p2rope")
        kb.dma(sp, ropeC[:], dr["ropeC"], drope)
        kb.dma(sp, ropeS[:], dr["ropeS"], drope)
        tok_rope = drope.tok()
        dcp = kb.dsem("p2constp")
        kb.dma(pool, rotT[:], dr["rotT"], dcp)
        tok_const = (dc.tok(), dcp.tok())
        t_m0 = pool.mark(nc.gpsimd.memset(ones_bf[:], 1.0))
        nc.gpsimd.memset(epsc[:], EPS)
        t_m1 = pool.mark(nc.gpsimd.memset(hTs[:, :, S:S + 64], 0.0))
        toks_small = [tok_const, t_m0, t_m1]
        for i in range(3):
            toks_small.append(pool.mark(nc.gpsimd.memset(vstgA[i][:], 1.0)))
        toks_init = toks_small + hT_tok

        w_in_v = dr["w_in"].rearrange("(kc p) n -> p kc n", p=128)
        wc_ld = [kb.dsem(f"p2wc{i}") for i in range(3)]
        wc_free = [None] * 3
        wv_ld = [kb.dsem(f"p2wv{i}") for i in range(2)]
        wv_free = [None] * 2
        stg_st = [kb.dsem(f"p2stg{i}") for i in range(2)]
        stg_free = [None] * 2
        vstgA_st = [kb.dsem(f"p2vsa{i}") for i in range(3)]
        vstgA_free = [None] * 3
        bank_free = [None] * 8

        class _View:
            def __init__(self, off):
                self.off = off

            def __getitem__(self, i):
                return bank_free[self.off + i]

            def __setitem__(self, i, v):
                bank_free[self.off + i] = v
        pm_free = _View(0)
        pv_free = _View(2)
        sq_free = [None] * 4
        t1_free = [None] * 4
        t2_free = [None] * 4
        t3_free = [None] * 4
        qnb_free = [None] * 4
        ctr = {"wc": 0, "stg": 0, "pm": 0, "wv": 0, "pv": 0, "vsa": 0, "b": 0, "ev": 0}
        final_toks = []

        def tok_rhs(g, kc, blk):
            _, d = GROUPS[g]
            base = hTs[:, kc, 0:S]
            if d == 1:
                return base[:, blk * 512:(blk + 1) * 512], None
            v = perm_view(base, d)
            if d == 4:
                return v[:, blk // 2, (blk % 2) * 512:(blk % 2) * 512 + 512], None
            return v[:, 2 * blk:2 * blk + 2, :], 2

        def fm_chunk(col0, kind, g, dst, arg=None):
            k = ctr["wc"]; ctr["wc"] += 1
            ws = k % 3
            t_w = kb.dma(pool, wc[ws][:], w_in_v[:, :, col0:col0 + 128], wc_ld[ws], waits=(wc_free[ws],))
            ks = ctr["stg"]; ctr["stg"] += 1
            ss = ks % 2
            off = 64 if kind == "ak" else 0
            evs = {}
            if kind == "ak":
                dve.wait(stg_free[ss])
                nc.vector.memset(stg[ss][:, 0:64], 0.0)
                evs["pad"] = dve.mark(nc.vector.memset(stg[ss][:, S + 64:S + 128], 0.0))
            for blk in range(8):
                j = ctr["pm"]; ctr["pm"] += 1
                pb_ = j % 2
                pe.wait(t_w, toks_init, pm_free[pb_])
                for kc in range(KC):
                    rhs, two = tok_rhs(g, kc, blk)
                    o = pm[pb_][:]
                    if two:
                        o = o.rearrange("p (a b) -> p a b", a=2)
                    ins = nc.tensor.matmul(o, lhsT=wc[ws][:, kc, :], rhs=rhs, start=(kc == 0), stop=(kc == KC - 1))
                t_mm = pe.mark(ins)
                if blk == 7:
                    wc_free[ws] = t_mm
                dstap = stg[ss][:, off + blk * 512: off + (blk + 1) * 512]
                if kind in ("aq", "ak"):
                    e = ctr["ev"]; ctr["ev"] += 1
                    if e % 2 == 0:
                        dve.wait(t_mm, stg_free[ss])
                        t_ev = dve.mark(nc.vector.tensor_copy(out=dstap, in_=pm[pb_][:]))
                        evs["dve"] = t_ev
                    else:
                        act.wait(t_mm, stg_free[ss])
                        t_ev = act.mark(nc.scalar.activation(out=dstap, in_=pm[pb_][:], func=AF.Copy))
                        evs["act"] = t_ev
                    pm_free[pb_] = t_ev
                elif kind == "gate":
                    act.wait(t_mm, toks_init, stg_free[ss])
                    t_ev = act.mark(nc.scalar.activation(out=dstap, in_=pm[pb_][:], func=AF.Sigmoid,
                                                         bias=bg[:, arg:arg + 1]))
                    pm_free[pb_] = t_ev
                    evs["act"] = t_ev
            width = S + 128 if kind == "ak" else S
            tk = kb.dma(sp, dst, stg[ss][:, 0:width], stg_st[ss], waits=list(evs.values()))
            stg_free[ss] = tk
            final_toks.append(tk)


        def bqk_pipeline():
            chunks = [(B_K0 + j * 128, 1, dr["kTB"][j]) for j in range(2)] + \
                     [(B_Q0 + h * 128, 0, dr["qTB"][h]) for h in range(8)]
            N = len(chunks) * 8
            T = {}
            wtok = {}

            def load_w(c):
                ws = c % 3
                col0 = chunks[c][0]
                wtok[c] = kb.dma(pool, wc[ws][:], w_in_v[:, :, col0:col0 + 128], wc_ld[ws], waits=(wc_free[ws],))

            def S0(i):
                c, blk = divmod(i, 8)
                if blk == 0 and c + 1 < len(chunks):
                    load_w(c + 1)
                pe.wait(wtok[c], toks_small, hT_tok[blk], bank_free[i % 4])
                for kc in range(KC):
                    ins = nc.tensor.matmul(bank[i % 4][:], lhsT=wc[c % 3][:, kc, :],
                                           rhs=hTs[:, kc, blk * 512:(blk + 1) * 512],
                                           start=(kc == 0), stop=(kc == KC - 1))
                T["mm", i] = pe.mark(ins)
                if blk == 7:
                    wc_free[c % 3] = T["mm", i]

            def S1(i):
                act.wait(T["mm", i], sq_free[i % 4])
                T["sq", i] = act.mark(nc.scalar.activation(out=sq[i % 4][:], in_=bank[i % 4][:], func=AF.Square))

            def S2(i):
                pe.wait(T["sq", i], bank_free[4 + i % 2])
                T["ss", i] = pe.mark(nc.tensor.matmul(bank[4 + i % 2][:], lhsT=ones_bf[:], rhs=sq[i % 4][:],
                                                      start=True, stop=True))
                sq_free[i % 4] = T["ss", i]

            def S3(i):
                act.wait(T["ss", i], t1_free[i % 4], toks_init)
                tl = act.mark(nc.scalar.activation(out=t1[i % 4][:], in_=bank[4 + i % 2][:], func=AF.Ln,
                                                   scale=1.0 / 128, bias=epsc[:, 0:1]))
                bank_free[4 + i % 2] = tl
                act.wait(tl)
                T["rs", i] = act.mark(nc.scalar.activation(out=t1[i % 4][:], in_=t1[i % 4][:], func=AF.Exp, scale=-0.5))

            def S5(i):
                c = i // 8
                dve.wait(T["rs", i], t2_free[i % 4], toks_init)
                T["qn", i] = dve.mark(nc.vector.scalar_tensor_tensor(
                    out=t2[i % 4][:], in0=bank[i % 4][:], scalar=qkg[:, chunks[c][1]:chunks[c][1] + 1],
                    in1=t1[i % 4][:], op0=ALU.mult, op1=ALU.mult))
                bank_free[i % 4] = T["qn", i]
                t1_free[i % 4] = T["qn", i]

            def S6(i):
                act.wait(T["qn", i], qnb_free[i % 4])
                T["qb", i] = act.mark(nc.scalar.activation(out=qnb[i % 4][:], in_=t2[i % 4][:], func=AF.Copy))

            def S7(i):
                pe.wait(T["qb", i], bank_free[6 + i % 2])
                T["rot", i] = pe.mark(nc.tensor.matmul(bank[6 + i % 2][:], lhsT=rotT[:], rhs=qnb[i % 4][:],
                                                       start=True, stop=True))
                qnb_free[i % 4] = T["rot", i]

            def S8(i):
                blk = i % 8
                tsl = slice(blk * 512, (blk + 1) * 512)
                dve.wait(T["qb", i], T["qn", i], tok_rope)
                T["c", i] = dve.mark(nc.vector.tensor_tensor(out=t2[i % 4][:], in0=t2[i % 4][:], in1=ropeC[:, tsl],
                                                             op=ALU.mult))
                dve.wait(T["rot", i], t3_free[i % 4])
                T["s", i] = dve.mark(nc.vector.tensor_tensor(out=t3[i % 4][:], in0=bank[6 + i % 2][:],
                                                             in1=ropeS[:, tsl], op=ALU.mult))
                bank_free[6 + i % 2] = T["s", i]

            def S10(i):
                c, blk = divmod(i, 8)
                ss = c % 2
                pool.wait(T["c", i], T["s", i], stg_free[ss])
                T["o", i] = pool.mark(nc.gpsimd.tensor_tensor(out=stg[ss][:, blk * 512:(blk + 1) * 512],
                                                              in0=t2[i % 4][:], in1=t3[i % 4][:], op=ALU.add))
                t2_free[i % 4] = T["o", i]
                t3_free[i % 4] = T["o", i]
                if blk == 7:
                    tk = kb.dma(sp, chunks[c][2], stg[ss][:, 0:S], stg_st[ss], waits=(T["o", i],))
                    stg_free[ss] = tk
                    final_toks.append(tk)

            load_w(0)
            for step in range(N + 5):
                if step < N:
                    S0(step)
                    S1(step)
                if 0 <= step - 1 < N:
                    S2(step - 1)
                    S3(step - 1)
                if 0 <= step - 2 < N:
                    S5(step - 2)
                    S6(step - 2)
                if 0 <= step - 3 < N:
                    S7(step - 3)
                    S8(step - 3)
                if 0 <= step - 4 < N:
                    S10(step - 4)
            ctr["wc"] = len(chunks)
            ctr["stg"] = len(chunks)
            ctr["pm"] = 0

        if "bqk" in P2PARTS:
            bqk_pipeline()
        t_wv = kb.dma(pool, wv[0][:, :, 0:256], w_in_v[:, :, B_V0:B_V0 + 256], wv_ld[0])
        ctr["wv"] = 1
        evb = {}
        for tt in range(32 if "bv" in P2PARTS else 0):
            j = ctr["pv"]; ctr["pv"] += 1
            pb_ = j % 2
            pe.wait(t_wv, toks_init, pv_free[pb_])
            for kc in range(KC):
                ins = nc.tensor.matmul(pv[pb_][:, 0:256], lhsT=hTs[:, kc, tt * 128:(tt + 1) * 128],
                                       rhs=wv[0][:, kc, 0:256], start=(kc == 0), stop=(kc == KC - 1))
            t_mm = pe.mark(ins)
            if tt % 2 == 0:
                dve.wait(t_mm)
                t_ev = dve.mark(nc.vector.tensor_copy(out=vstgB[:, tt, :], in_=pv[pb_][:, 0:256]))
                evb["dve"] = t_ev
            else:
                act.wait(t_mm)
                t_ev = act.mark(nc.scalar.activation(out=vstgB[:, tt, :], in_=pv[pb_][:, 0:256], func=AF.Copy))
                evb["act"] = t_ev
            pv_free[pb_] = t_ev
            if tt == 31:
                wv_free[0] = t_mm
        dsb = kb.dsem("p2vB")
        if "bv" in P2PARTS:
          tk = kb.dma(sp, dr["vB"].rearrange("(t p) c -> p t c", p=128), vstgB[:], dsb, waits=list(evb.values()))
          final_toks.append(tk)
        for c in range(16 if "gate" in P2PARTS else 0):
            fm_chunk(G0 + c * 128, "gate", 0, dr["gT"][c], arg=c)
        for g in range(3):
            for hp in range(4 if "aqk" in P2PARTS else 0):
                fm_chunk(A_Q0 + g * 512 + hp * 128, "aq", g, dr["qTA"][g * 4 + hp])
                fm_chunk(A_K0 + g * 512 + hp * 128, "ak", g, dr["kTA"][g * 4 + hp])
            if "av" not in P2PARTS or g not in AVG:
                continue
            k = ctr["wv"]; ctr["wv"] += 1
            ws = k % 2
            t_wv = kb.dma(pool, wv[ws][:], w_in_v[:, :, A_V0 + g * 512:A_V0 + (g + 1) * 512], wv_ld[ws],
                          waits=(wv_free[ws],))
            for kt in AVKT:
                j = ctr["pv"]; ctr["pv"] += 1
                pb_ = j % 2
                pe.wait(t_wv, toks_init, pv_free[pb_])
                for (p0, cnt, start, d) in window_pieces(g, kt):
                    for kc in range(KC):
                        if start is None:
                            lhsT = hTs[:, kc, S:S + cnt]
                        elif d == 1:
                            lhsT = hTs[:, kc, start:start + cnt]
                        else:
                            r = start % d
                            m0 = start // d
                            lhsT = perm_view(hTs[:, kc, 0:S], d)[:, r, m0:m0 + cnt]
                        ins = nc.tensor.matmul(pv[pb_][p0:p0 + cnt, :], lhsT=lhsT, rhs=wv[ws][:, kc, :],
                                               start=(kc == 0), stop=(kc == KC - 1))
                t_mm = pe.mark(ins)
                if kt == AVKT[-1]:
                    wv_free[ws] = t_mm
                kv_ = ctr["vsa"]; ctr["vsa"] += 1
                vs_ = kv_ % 3
                src = pv[pb_][:].rearrange("p (hp eo d) -> p hp eo d", hp=4, eo=2)
                dstv = vstgA[vs_][:].rearrange("p (hp x) -> p hp x", x=192)
                if kt % 2 == 0:
                    dve.wait(t_mm, vstgA_free[vs_], toks_init)
                    nc.vector.tensor_copy(out=dstv[:, :, 0:64], in_=src[:, :, 0, :])
                    t_e0 = dve.mark(nc.vector.tensor_copy(out=dstv[:, :, 128:192], in_=src[:, :, 1, :]))
                else:
                    act.wait(t_mm, vstgA_free[vs_], toks_init)
                    nc.scalar.activation(out=dstv[:, :, 0:64], in_=src[:, :, 0, :], func=AF.Copy)
                    t_e0 = act.mark(nc.scalar.activation(out=dstv[:, :, 128:192], in_=src[:, :, 1, :], func=AF.Copy))
                t_e1 = t_e0
                pv_free[pb_] = (t_e0, t_e1)
                if AVNOSTORE:
                    vstgA_free[vs_] = (t_e0, t_e1)
                    continue
                tk = kb.dma(sp, dr["vA"][g * 33 + kt], vstgA[vs_][:], vstgA_st[vs_], waits=(t_e0, t_e1))
                vstgA_free[vs_] = tk
                final_toks.append(tk)
        last = {}
        for t in final_toks:
            if id(t[0]) not in last or last[id(t[0])][1] < t[1]:
                last[id(t[0])] = t
        kb.barrier(list(last.values()))


def _final_barrier(kb, final_toks):
    last = {}
    for t in final_toks:
        if id(t[0]) not in last or last[id(t[0])][1] < t[1]:
            last[id(t[0])] = t
    kb.barrier(list(last.values()))


def attn_a_phase(kb, dr):
    nc = kb.nc
    pe, act, dve, pool, sp = kb.pe, kb.act, kb.dve, kb.pool, kb.sp
    with ExitStack() as ph:
        def sb(name, shape, dt):
            return ph.enter_context(nc.sbuf_tensor("p3" + name, shape, dt))

        def pst(name, shape, dt):
            return ph.enter_context(nc.psum_tensor("p3" + name, shape, dt))

        NB = 8
        qs = [sb(f"qs{i}", [128, S], BF16) for i in range(2)]
        ks = [sb(f"ks{i}", [128, S + 128], BF16) for i in range(2)]
        vs = [sb(f"vs{i}", [128, 33, 192], BF16) for i in range(2)]
        tb = [sb(f"tb{i}", [128, 2, 2, 256], F32) for i in range(2)]
        wt = [sb(f"wt{i}", [128, 2, 2, 256], BF16) for i in range(2)]
        acc = [[sb(f"acc{j}{i}", [128, S], F32) for i in range(2)] for j in range(2)]
        den2 = sb("den2", [128, S], F32)
        ost = sb("ost", [128, S], BF16)
        pex = [sb(f"pex{i}", [128, 256], BF16) for i in range(NB)]
        pT = [sb(f"pT{i}", [128, 256], BF16) for i in range(NB)]
        psT = [pst(f"psT{i}", [128, 512], F32) for i in range(4)]
        pU = [[pst(f"pU{e}{i}", [128, 512], F32) for i in range(2)] for e in range(2)]

        ld = [kb.dsem(f"p3ld{i}") for i in range(2)]
        slot_free = [None, None]
        wt_free = [None, None]
        sT_free = [None] * 4
        pex_free = [None] * NB
        pT_free = [None] * NB
        pU_free = [[None, None], [None, None]]
        acc_free = [[None, None], [None, None]]
        den_ds = kb.dsem("p3den")
        ost_ds = kb.dsem("p3ost")
        final_toks = []
        LAG = 3
        it = 0
        pending_norm = []
        nst = {"ost_free": None, "den_free": None}

        def do_norm_dma(hp_, aj_, al_):
            kb.dma(sp, den2[0:64, :], acc[aj_][0][64:128, :], den_ds, waits=(al_[0], al_[1], nst["den_free"]))
            t2_ = kb.dma(sp, den2[64:128, :], acc[aj_][1][0:64, :], den_ds)
            pending_norm2.append((hp_, aj_, t2_))

        def do_norm_act(hp_, aj_, t2_, c):
            cs = slice(c * 1024, (c + 1) * 1024)
            act.wait(t2_)
            t = act.mark(nc.scalar.activation(out=den2[:, cs], in_=den2[:, cs], func=AF.Ln))
            act.wait(t)
            nst["r", c] = act.mark(nc.scalar.activation(out=den2[:, cs], in_=den2[:, cs], func=AF.Exp, scale=-1.0))

        def do_norm_dve(hp_, aj_, t2_, c):
            cs = slice(c * 1024, (c + 1) * 1024)
            dve.wait(nst["r", c], nst["ost_free"])
            nc.vector.tensor_tensor(out=ost[0:64, cs], in0=acc[aj_][0][0:64, cs], in1=den2[0:64, cs], op=ALU.mult)
            t_o = dve.mark(nc.vector.tensor_tensor(out=ost[64:128, cs], in0=acc[aj_][1][64:128, cs],
                                                   in1=den2[64:128, cs], op=ALU.mult))
            if c == 3:
                acc_free[aj_] = [t_o, t_o]
                nst["den_free"] = t_o
                nst["ost_free"] = kb.dma(sp, dr["oaT"][hp_], ost[:], ost_ds, waits=(t_o,))
                final_toks.append(nst["ost_free"])

        pending_norm2 = []
        ldtok = {}

        def issue_loads(n):
            hp_, g_ = divmod(n, 3)
            sl = n % 2
            pidx = g_ * 4 + hp_
            kb.dma(sp, qs[sl][:], dr["qTA"][pidx], ld[sl], waits=(slot_free[sl],))
            kb.dma(sp, ks[sl][:], dr["kTA"][pidx], ld[sl])
            kb.dma(sp, vs[sl][:], dr["vA"][g_ * 33:(g_ + 1) * 33, :, hp_ * 192:(hp_ + 1) * 192].rearrange("kt p c -> p kt c"),
                   ld[sl])
            h0 = g_ * 8 + 2 * hp_
            ldtok[n] = kb.dma(sp, tb[sl][:], dr["tabA"][h0:h0 + 2].rearrange("h v p c -> p h v c"), ld[sl])

        for hp in range(4):
            aj = hp % 2
            acc_last = [None, None]
            for g in range(3):
                _, d = GROUPS[g]
                L = S // d
                sl_ = it % 2
                it += 1
                n_grp = it - 1
                if n_grp == 0:
                    issue_loads(0)
                t_ld = ldtok[n_grp]
                act.wait(t_ld, wt_free[sl_])
                t_wt = act.mark(nc.scalar.activation(out=wt[sl_][:], in_=tb[sl_][:], func=AF.Exp))
                pend = []
                last_pv = [None]

                def evac_bank(b, e):
                    bank = pU[e][b % 2]
                    if d == 1:
                        dst = acc[aj][e][:, b * 512:(b + 1) * 512]
                        src = bank[:]
                    elif d == 4:
                        dst = perm_view(acc[aj][e][:], 4)[:, b // 2, (b % 2) * 512:(b % 2) * 512 + 512]
                        src = bank[:]
                    else:
                        dst = perm_view(acc[aj][e][:], 16)[:, 2 * b:2 * b + 2, :]
                        src = bank[:].rearrange("p (a b) -> p a b", a=2)
                    if g == 0:
                        act.wait(last_pv[0], acc_free[aj][e])
                        t = act.mark(nc.scalar.activation(out=dst, in_=src, func=AF.Copy))
                    else:
                        dve.wait(last_pv[0], acc_last[e])
                        t = dve.mark(nc.vector.tensor_tensor(out=dst, in0=dst, in1=src, op=ALU.add))
                    pU_free[e][b % 2] = t
                    acc_last[e] = t

                def do_pv(kt, e, bufi, t_p):
                    lhsT = vs[sl_][:, kt, 64 * e:64 * e + 128]
                    pe.wait(t_p)
                    ins = None
                    if kt >= 1:
                        a_ = kt - 1
                        ins = nc.tensor.matmul(pU[e][(a_ // 4) % 2][:, (a_ % 4) * 128:(a_ % 4) * 128 + 128], lhsT=lhsT,
                                               rhs=pT[bufi][:, 0:128], start=False, stop=True)
                    if kt <= 31:
                        a_ = kt
                        if a_ % 4 == 0:
                            pe.wait(pU_free[e][(a_ // 4) % 2])
                        ins = nc.tensor.matmul(pU[e][(a_ // 4) % 2][:, (a_ % 4) * 128:(a_ % 4) * 128 + 128], lhsT=lhsT,
                                               rhs=pT[bufi][:, 128:256], start=True, stop=False)
                    t = pe.mark(ins)
                    pT_free[bufi] = t
                    last_pv[0] = t
                    if kt >= 4 and kt % 4 == 0:
                        evac_bank(kt // 4 - 1, e)

                for step in range(33 + LAG):
                    if step < 33:
                        kt = step
                        var = 1 if (128 * kt) % L == 0 else 0
                        lo = 128 if kt == 0 else 0
                        hi = 128 if kt == 32 else 256
                        c0 = 128 * (kt - 1) + lo
                        for e in range(2):
                            rows = slice(64 * e, 64 * e + 64)
                            bufi = (kt % 4) * 2 + e
                            si = (kt % 2) * 2 + e
                            sTt = psT[si][:, 0:256]
                            pe.wait(t_ld, sT_free[si])
                            t_s = pe.mark(nc.tensor.matmul(sTt[:, lo:hi], lhsT=ks[sl_][rows, 128 * kt:128 * kt + 128],
                                                           rhs=qs[sl_][rows, c0:c0 + (hi - lo)], start=True, stop=True))
                            act.wait(t_s, pex_free[bufi])
                            t_x = act.mark(nc.scalar.activation(out=pex[bufi][:, lo:hi], in_=sTt[:, lo:hi], func=AF.Exp,
                                                                scale=0.125))
                            sT_free[si] = t_x
                            dve.wait(t_x, pT_free[bufi], t_wt)
                            t_p = dve.mark(nc.vector.tensor_tensor(out=pT[bufi][:, lo:hi], in0=pex[bufi][:, lo:hi],
                                                                   in1=wt[sl_][:, e, var, lo:hi], op=ALU.mult))
                            pex_free[bufi] = t_p
                            pend.append((kt, e, bufi, t_p))
                    if step >= LAG:
                        for _ in range(2):
                            do_pv(*pend.pop(0))
                    if step == 1 and n_grp + 1 < 12:
                        issue_loads(n_grp + 1)
                    if step == 2 and pending_norm:
                        do_norm_dma(*pending_norm.pop(0))
                    for c_ in range(4):
                        if g == 1 and step == 3 + 5 * c_ and pending_norm2:
                            do_norm_act(*pending_norm2[0], c_)
                        if g == 1 and step == 6 + 5 * c_ and pending_norm2:
                            do_norm_dve(*pending_norm2[0], c_)
                            if c_ == 3:
                                pending_norm2.pop(0)
                slot_free[sl_] = (last_pv[0], acc_last[0], acc_last[1])
                wt_free[sl_] = acc_last[1]
            pending_norm.append((hp, aj, list(acc_last)))
        while pending_norm:
            do_norm_dma(*pending_norm.pop(0))
        while pending_norm2:
            for c_ in range(4):
                do_norm_act(*pending_norm2[0], c_)
                do_norm_dve(*pending_norm2[0], c_)
            pending_norm2.pop(0)
        _final_barrier(kb, final_toks)


def attn_b_consts(kb, dr, es):
    nc = kb.nc
    kTs = es.enter_context(nc.sbuf_tensor("p4kTs", [128, 2, S], BF16))
    vBs = es.enter_context(nc.sbuf_tensor("p4vBs", [128, 32, 256], BF16))
    qs = [es.enter_context(nc.sbuf_tensor("p4qs0", [128, S], BF16))]
    dc = kb.dsem("p4const")
    for j in range(2):
        kb.dma(kb.sp, kTs[:, j, :], dr["kTB"][j], dc)
    kb.dma(kb.sp, vBs[:], dr["vB"].rearrange("(t p) c -> p t c", p=128), dc)
    q_ld = [kb.dsem(f"p4q{i}") for i in range(2)]
    t_q0 = kb.dma(kb.sp, qs[0][:], dr["qTB"][0], q_ld[0])
    return kTs, vBs, qs, dc.tok(), q_ld, t_q0


def attn_b_phase(kb, dr, pre):
    nc = kb.nc
    pe, act, dve, pool, sp = kb.pe, kb.act, kb.dve, kb.pool, kb.sp
    SCALE = 128 ** -0.5
    with ExitStack() as ph:
        def sb(name, shape, dt):
            return ph.enter_context(nc.sbuf_tensor("p4" + name, shape, dt))

        def pst(name, shape, dt):
            return ph.enter_context(nc.psum_tensor("p4" + name, shape, dt))

        NP = 4
        NB = 3
        kTs, vBs, qs, tok_const, q_ld, t_q0 = pre
        qs = [qs[0], sb("qs1", [128, S], BF16)]
        ones_bf = sb("ones", [128, 128], BF16)
        ost = [sb(f"ost{i}", [128, S], BF16) for i in range(2)]
        pT = [sb(f"pT{i}", [128, 1024], BF16) for i in range(NP)]
        xx = [sb(f"xx{i}", [128, 1024], BF16) for i in range(2)]
        xx_free = [None, None]
        prev_tp = [None]
        qd = [sb(f"qd{i}", [128, 512], BF16) for i in range(3)]
        racc = [sb(f"racc{i}", [128, 512], F32) for i in range(2)]
        raccb = [sb(f"raccb{i}", [128, 512], BF16) for i in range(2)]
        rD = [sb(f"rD{i}", [128, 512], F32) for i in range(2)]
        psT = [pst(f"psT{i}", [128, 1024], F32) for i in range(NB)]
        pO = [pst(f"pO{i}", [128, 512], F32) for i in range(2)]

        t_ones = pool.mark(nc.gpsimd.memset(ones_bf[:], 1.0))
        q_free = [None, None]
        o_st = [kb.dsem(f"p4o{i}") for i in range(2)]
        o_free = [None, None]
        sT_free = [None] * NB
        tick = [0]
        pT_free = [None] * NP
        qd_free = [None] * 3
        pO_free = [None, None]
        racc_free = [None, None]
        raccb_free = [None, None]
        rD_free = [None, None]
        final_toks = []
        t_q = {}
        items = [(h, qb, pj) for h in range(8) for qb in range(8) for pj in range(16)]
        pend = []
        pend_fin = []
        last_acc = {}
        last_pv = {}
        npp = [0]

        def load_q(h):
            b = h % 2
            t_q[h] = kb.dma(sp, qs[b][:], dr["qTB"][h], q_ld[b], waits=(q_free[b],))

        def do_pv(j, h, qb, pj, t_p):
            ob = (h * 8 + qb) % 2
            kv = h // 4
            pe.wait(t_p)
            if pj == 0:
                pe.wait(pO_free[ob])
            for u in range(2):
                kt = 2 * pj + u
                ins = nc.tensor.matmul(pO[ob][:], lhsT=vBs[:, kt, kv * 128:(kv + 1) * 128],
                                       rhs=pT[j % NP][:, u * 512:(u + 1) * 512],
                                       start=(kt == 0), stop=(kt == 31))
            t = pe.mark(ins)
            last_pv[(h, qb)] = t
            return t

        fin = {}

        def fin_cast(h, qb):
            ob = (h * 8 + qb) % 2
            dve.wait(last_acc[(h, qb)], raccb_free[ob])
            t_c = dve.mark(nc.vector.tensor_copy(out=raccb[ob][:], in_=racc[ob][:]))
            racc_free[ob] = t_c
            fin["c"] = t_c

        def fin_dmm(h, qb):
            ob = (h * 8 + qb) % 2
            bt = tick[0] % NB
            tick[0] += 1
            pe.wait(fin["c"], t_ones, sT_free[bt])
            t_d = pe.mark(nc.tensor.matmul(psT[bt][:, 0:512], lhsT=ones_bf[:], rhs=raccb[ob][:], start=True, stop=True))
            raccb_free[ob] = t_d
            fin["d"] = (t_d, bt)

        def fin_act(h, qb):
            ob = (h * 8 + qb) % 2
            t_d, bt = fin["d"]
            act.wait(t_d, rD_free[ob])
            t_l = act.mark(nc.scalar.activation(out=rD[ob][:], in_=psT[bt][:, 0:512], func=AF.Ln))
            sT_free[bt] = t_l
            act.wait(t_l)
            fin["r"] = act.mark(nc.scalar.activation(out=rD[ob][:], in_=rD[ob][:], func=AF.Exp, scale=-1.0))

        def fin_mul(h, qb):
            ob = (h * 8 + qb) % 2
            dve.wait(fin["r"], last_pv[(h, qb)], o_free[h % 2] if qb == 0 else None)
            t_o = dve.mark(nc.vector.tensor_tensor(out=ost[h % 2][:, qb * 512:(qb + 1) * 512], in0=pO[ob][:],
                                                   in1=rD[ob][:], op=ALU.mult))
            pO_free[ob] = t_o
            rD_free[ob] = t_o
            if qb == 7:
                tk = kb.dma(sp, dr["obT"][h], ost[h % 2][:], o_st[h % 2], waits=(t_o,))
                o_free[h % 2] = tk
                final_toks.append(tk)

        FIN = ((1, fin_cast), (7, fin_dmm), (9, fin_act), (13, fin_mul))

        t_q[0] = t_q0
        for j, (h, qb, pj) in enumerate(items):
            if qb == 0 and pj == 4 and h + 1 < 8:
                load_q(h + 1)
            kv = h // 4
            ob = (h * 8 + qb) % 2
            bt = tick[0] % NB
            tick[0] += 1
            pe.wait(tok_const, t_q[h], sT_free[bt])
            for u in range(2):
                kt = 2 * pj + u
                ins = nc.tensor.matmul(psT[bt][:, u * 512:(u + 1) * 512], lhsT=kTs[:, kv, kt * 128:(kt + 1) * 128],
                                       rhs=qs[h % 2][:, qb * 512:(qb + 1) * 512], start=True, stop=True)
            t_s = pe.mark(ins)
            if qb == 7 and pj == 15:
                q_free[h % 2] = t_s
            act.wait(t_s, pT_free[j % NP])
            t_p = act.mark(nc.scalar.activation(out=pT[j % NP][:], in_=psT[bt][:], func=AF.Exp, scale=SCALE))
            sT_free[bt] = t_p
            t_pp = None
            if pj % 2 == 1:
                xi = (j // 2) % 2
                qi = npp[0] % 3
                npp[0] += 1
                dve.wait(prev_tp[0], t_p, xx_free[xi])
                t_pp = dve.mark(nc.vector.tensor_tensor(out=xx[xi][:], in0=pT[(j - 1) % NP][:], in1=pT[j % NP][:],
                                                        op=ALU.add))
                for k_, it_ in enumerate(pend):
                    if it_[0] == j - 1:
                        pend[k_] = it_[:5] + (t_pp,)
                dve.wait(t_pp, qd_free[qi])
                t_qd = dve.mark(nc.vector.tensor_tensor(out=qd[qi][:], in0=xx[xi][:, 0:512], in1=xx[xi][:, 512:1024],
                                                        op=ALU.add))
                xx_free[xi] = t_qd
                dve.wait(t_qd, racc_free[ob] if pj == 1 else last_acc.get((h, qb)))
                if pj == 1:
                    t_a = dve.mark(nc.vector.tensor_copy(out=racc[ob][:], in_=qd[qi][:]))
                else:
                    t_a = dve.mark(nc.vector.tensor_tensor(out=racc[ob][:], in0=racc[ob][:], in1=qd[qi][:], op=ALU.add))
                qd_free[qi] = t_a
                last_acc[(h, qb)] = t_a
            prev_tp[0] = t_p
            pend.append((j, h, qb, pj, t_p, t_pp))
            if len(pend) > 2:
                (j_, h_, qb_, pj_, tp_, tpp_) = pend.pop(0)
                t = do_pv(j_, h_, qb_, pj_, tp_)
                pT_free[j_ % NP] = (t, tpp_)
                if pj_ == 15:
                    pend_fin.append((h_, qb_))
            for (pjx, fn) in FIN:
                if pj == pjx and pend_fin:
                    fn(*pend_fin[0])
                    if fn is fin_mul:
                        pend_fin.pop(0)
        while pend:
            (j_, h_, qb_, pj_, tp_, tpp_) = pend.pop(0)
            t = do_pv(j_, h_, qb_, pj_, tp_)
            pT_free[j_ % NP] = (t, tpp_)
            if pj_ == 15:
                pend_fin.append((h_, qb_))
        while pend_fin:
            for (_, fn) in FIN:
                fn(*pend_fin[0])
            pend_fin.pop(0)
        _final_barrier(kb, final_toks)


def merge_weights(kb, dr, es):
    nc = kb.nc
    was = es.enter_context(nc.sbuf_tensor("p5was", [128, 4, D], BF16))
    wbs = es.enter_context(nc.sbuf_tensor("p5wbs", [128, 8, D], BF16))
    wos = es.enter_context(nc.sbuf_tensor("p5wos", [128, 8, D], BF16))
    dw = kb.dsem("p5w")
    kb.dma(kb.pool, was[:], dr["w_ba"].rearrange("(kc p) n -> p kc n", p=128), dw)
    kb.dma(kb.pool, wbs[:], dr["w_bb"].rearrange("(kc p) n -> p kc n", p=128), dw)
    kb.dma(kb.pool, wos[:], dr["w_out"].rearrange("(kc p) n -> p kc n", p=128), dw)
    return was, wbs, wos, dw.tok()


def merge_phase(kb, dr, pre):
    nc = kb.nc
    pe, act, dve, pool, sp = kb.pe, kb.act, kb.dve, kb.pool, kb.sp
    with ExitStack() as ph:
        def sb(name, shape, dt):
            return ph.enter_context(nc.sbuf_tensor("p5" + name, shape, dt))

        def pst(name, shape, dt):
            return ph.enter_context(nc.psum_tensor("p5" + name, shape, dt))

        was, wbs, wos, tok_w = pre
        oa = [sb(f"oa{i}", [128, 4, 512], BF16) for i in range(2)]
        ob = [sb(f"ob{i}", [128, 8, 512], BF16) for i in range(2)]
        gt = [sb(f"gt{i}", [128, 16, 512], BF16) for i in range(2)]
        mT = sb("mT", [128, 8, 512], BF16)
        ta = [sb(f"ta{i}", [128, 512], F32) for i in range(2)]
        tbb = [sb(f"tbb{i}", [128, 512], F32) for i in range(2)]
        xr = [sb(f"xr{i}", [128, D], F32) for i in range(3)]
        pA = [pst(f"pA{i}", [128, 512], F32) for i in range(2)]
        pB = [pst(f"pB{i}", [128, 512], F32) for i in range(2)]
        py = [pst(f"py{i}", [128, 512], F32) for i in range(2)]

        ld = [kb.dsem(f"p5ld{i}") for i in range(2)]
        in_free = [None, None]
        xr_ld = [kb.dsem(f"p5xl{i}") for i in range(3)]
        xr_st = [kb.dsem(f"p5xs{i}") for i in range(3)]
        xr_free = [None] * 3
        pA_free = [None] * 2
        pB_free = [None] * 2
        ta_free = [None] * 2
        tb_free = [None] * 2
        py_free = [None] * 2
        final_toks = []
        t_in = {}

        def load_in(i):
            b = i % 2
            tsl = slice(i * 512, (i + 1) * 512)
            kb.dma(sp, oa[b][:], dr["oaT"][:, :, tsl].rearrange("c p t -> p c t"), ld[b], waits=(in_free[b],))
            kb.dma(sp, ob[b][:], dr["obT"][:, :, tsl].rearrange("c p t -> p c t"), ld[b])
            t_in[i] = kb.dma(sp, gt[b][:], dr["gT"][:, :, tsl].rearrange("c p t -> p c t"), ld[b])

        load_in(0)
        cy = 0
        cx = 0
        mT_free = None
        for i in range(NT):
            if i + 1 < NT:
                load_in(i + 1)
            b = i % 2
            t_m = None
            for c in range(8):
                pb_ = c % 2
                pe.wait(tok_w, t_in[i], pA_free[pb_], pB_free[pb_])
                for kc in range(4):
                    ins = nc.tensor.matmul(pA[pb_][:], lhsT=was[:, kc, c * 128:(c + 1) * 128], rhs=oa[b][:, kc, :],
                                           start=(kc == 0), stop=(kc == 3))
                t_a = pe.mark(ins)
                for kc in range(8):
                    ins = nc.tensor.matmul(pB[pb_][:], lhsT=wbs[:, kc, c * 128:(c + 1) * 128], rhs=ob[b][:, kc, :],
                                           start=(kc == 0), stop=(kc == 7))
                t_b = pe.mark(ins)
                dve.wait(t_a, ta_free[pb_], t_in[i])
                t1_ = dve.mark(nc.vector.tensor_tensor(out=ta[pb_][:], in0=pA[pb_][:], in1=gt[b][:, c, :], op=ALU.mult))
                pA_free[pb_] = t1_
                dve.wait(t_b, tb_free[pb_])
                t2_ = dve.mark(nc.vector.tensor_tensor(out=tbb[pb_][:], in0=pB[pb_][:], in1=gt[b][:, 8 + c, :], op=ALU.mult))
                pB_free[pb_] = t2_
                pool.wait(t1_, t2_, mT_free if c == 0 else None)
                t_m = pool.mark(nc.gpsimd.tensor_tensor(out=mT[:, c, :], in0=ta[pb_][:], in1=tbb[pb_][:], op=ALU.add))
                ta_free[pb_] = t_m
                tb_free[pb_] = t_m
            t_lastmm = None
            for s in range(4):
                rb = cx % 3
                cx += 1
                r0 = i * 512 + s * 128
                tl = kb.dma(sp, xr[rb][:], dr["x1"][r0:r0 + 128, :], xr_ld[rb], waits=(xr_free[rb],))
                t_res = None
                for hf in range(2):
                    yb = cy % 2
                    cy += 1
                    pe.wait(t_m, py_free[yb])
                    for c in range(8):
                        ins = nc.tensor.matmul(py[yb][:], lhsT=mT[:, c, s * 128:(s + 1) * 128],
                                               rhs=wos[:, c, hf * 512:(hf + 1) * 512], start=(c == 0), stop=(c == 7))
                    t_y = pe.mark(ins)
                    t_lastmm = t_y
                    dve.wait(t_y, tl)
                    t_res = dve.mark(nc.vector.tensor_tensor(out=xr[rb][:, hf * 512:(hf + 1) * 512], in0=py[yb][:],
                                                             in1=xr[rb][:, hf * 512:(hf + 1) * 512], op=ALU.add))
                    py_free[yb] = t_res
                tst = kb.dma(sp, dr["x2"][r0:r0 + 128, :], xr[rb][:], xr_st[rb], waits=(t_res,))
                xr_free[rb] = tst
                final_toks.append(tst)
            mT_free = t_lastmm
            in_free[b] = t_lastmm
        _final_barrier(kb, final_toks)


SCR_PHASE = {"x1": 1, "hT": 1, "qTA": 2, "kTA": 2, "vA": 2, "qTB": 2, "kTB": 2, "vB": 2, "gT": 2,
             "oaT": 3, "obT": 4, "x2": 5}


def build_program(stop_after=99, dbg=False, start_at=1):
    kb = KB()
    nc = kb.nc

    def din(name, shape, dt=F32):
        return nc.dram_tensor(name, shape, dt, kind="ExternalInput").ap()

    def scr(name, shape, dt):
        kind = "ExternalOutput" if dbg else "Internal"
        if SCR_PHASE[name] < start_at:
            kind = "ExternalInput"
        return nc.dram_tensor(name, shape, dt, kind=kind).ap()

    dr = {}
    dr["x"] = din("x", [S, D])
    for p in ("ffn1", "ffn2"):
        dr[p + "_w1"] = din(p + "_w1", [D, DFF])
        dr[p + "_w3"] = din(p + "_w3", [D, DFF])
        dr[p + "_w2"] = din(p + "_w2", [DFF, D])
        dr[p + "_gbc"] = din(p + "_gbc", [128, KC, 128])
    dr["mix_gbc"] = din("mix_gbc", [128, KC, 128])
    dr["fin_bc"] = din("fin_bc", [128, D])
    dr["w_in"] = din("w_in", [D, 8192])
    dr["bg_col"] = din("bg_col", [128, 16])
    dr["qkg_col"] = din("qkg_col", [128, 2])
    dr["ropeC"] = din("ropeC", [128, S])
    dr["ropeS"] = din("ropeS", [128, S])
    dr["rotT"] = din("rotT", [128, 128])
    dr["ident"] = din("ident", [128, 128])
    dr["tabA"] = din("tabA", [24, 2, 128, 256])
    dr["w_ba"] = din("w_ba", [512, D])
    dr["w_bb"] = din("w_bb", [D, D])
    dr["w_out"] = din("w_out", [D, D])
    out = nc.dram_tensor("out", [S, D], F32, kind="ExternalOutput").ap()
    dr["x1"] = scr("x1", [S, D], F32)
    dr["hT"] = scr("hT", [KC, 128, S], BF16)
    dr["qTA"] = scr("qTA", [12, 128, S], BF16)
    dr["kTA"] = scr("kTA", [12, 128, S + 128], BF16)
    dr["vA"] = scr("vA", [99, 128, 768], BF16)
    dr["qTB"] = scr("qTB", [8, 128, S], BF16)
    dr["kTB"] = scr("kTB", [2, 128, S], BF16)
    dr["vB"] = scr("vB", [S, 256], BF16)
    dr["gT"] = scr("gT", [16, 128, S], BF16)
    dr["oaT"] = scr("oaT", [4, 128, S], BF16)
    dr["obT"] = scr("obT", [8, 128, S], BF16)
    dr["x2"] = scr("x2", [S, D], F32)

    with kb.es:
        if start_at <= 1:
            ffn_phase(kb, "f1", dr["x"], dr["ffn1_gbc"], dr["ffn1_w1"], dr["ffn1_w3"], dr["ffn1_w2"], dr["ident"],
                      "ffn1", x_dst=dr["x1"], g2bc_d=dr["mix_gbc"], hT_dst=dr["hT"])
        if start_at <= 2 <= stop_after:
            proj_phase(kb, dr)
        with ExitStack() as w4:
            pre4 = attn_b_consts(kb, dr, w4) if (start_at <= 4 <= stop_after) else None
            if start_at <= 3 <= stop_after:
                attn_a_phase(kb, dr)
            with ExitStack() as w5:
                pre5 = merge_weights(kb, dr, w5) if (start_at <= 5 <= stop_after) else None
                if start_at <= 4 <= stop_after:
                    attn_b_phase(kb, dr, pre4)
                if start_at <= 5 <= stop_after:
                    merge_phase(kb, dr, pre5)
        if start_at <= 6 <= stop_after:
            ffn_phase(kb, "f2", dr["x2"], dr["ffn2_gbc"], dr["ffn2_w1"], dr["ffn2_w3"], dr["ffn2_w2"], dr["ident"],
                      "final", x_dst=out, fin_d=dr["fin_bc"])
    return nc


def _gbc(g):
    return np.ascontiguousarray(np.broadcast_to(g.reshape(KC, 128).T[:, :, None], (128, KC, 128))).astype(np.float32)


def make_in_maps(inp):
    f = np.float32
    C, Sn = rope_tables()
    shared = {
        "ffn1_w1": inp["ffn1_w1"][0], "ffn1_w3": inp["ffn1_w3"][0], "ffn1_w2": inp["ffn1_w2"][0],
        "ffn2_w1": inp["ffn2_w1"][0], "ffn2_w3": inp["ffn2_w3"][0], "ffn2_w2": inp["ffn2_w2"][0],
        "ffn1_gbc": _gbc(inp["ffn1_norm"][0]), "ffn2_gbc": _gbc(inp["ffn2_norm"][0]),
        "mix_gbc": _gbc(inp["mix_norm"][0]),
        "fin_bc": np.ascontiguousarray(np.broadcast_to(inp["final_norm"][None, :], (128, D))).astype(f),
        "w_in": inp["w_in"][0],
        "bg_col": np.ascontiguousarray(inp["b_gate"][0].reshape(16, 128).T).astype(f),
        "qkg_col": np.ascontiguousarray(np.stack([inp["q_norm"][0], inp["k_norm"][0]], axis=1)).astype(f),
        "ropeC": C, "ropeS": Sn, "rotT": rot_lhsT(), "ident": np.eye(128, dtype=f),
        "tabA": host_tables(np.asarray(inp["rel_bias"])),
        "w_ba": inp["w_branch_a"][0], "w_bb": inp["w_branch_b"][0], "w_out": inp["w_out"][0],
    }
    shared = {k: np.ascontiguousarray(v, dtype=f) for k, v in shared.items()}
    maps = []
    for b in range(8):
        m = dict(shared)
        m["x"] = np.ascontiguousarray(inp["x"][b], dtype=f)
        maps.append(m)
    return maps


def kernel(**inputs):
    inp = {k: np.asarray(v) for k, v in inputs.items()}
    nc = build_program()
    res = run_bass_kernel_spmd(nc, make_in_maps(inp), core_ids=list(range(8)))
    return np.stack([np.asarray(r["out"]) for r in res.results], axis=0).astype(np.float32)
```
